# Optimizing a Trainium2 kernel written in Bass

```python
import math
import jax
import jax.numpy as jnp
from jax import lax
import numpy as np

D_MODEL = 2048
BATCH = 16
SEQ = 256
DEPTH = 2
DEC_BATCH = 4
DEC_SEQ = 1024
PAST_LEN = 512

GRID_W = 64
ROPE_BASE = 10000.0
NORM_EPS = 1e-6
GN_EPS = 64e-5
Q_BLOCK = 128

A_HEADS = 8
A_QK_DIM = 64
A_V_DIM = 2 * A_QK_DIM
B_HEADS = 8
B_NOPE = 128
B_ROPE = 64
B_V = 128
B_Q_LORA = 512
B_KV_LORA = 256
AB_WIDTH = A_HEADS * A_V_DIM + B_HEADS * B_V
L0_IN = 2 * A_HEADS * 2 * A_QK_DIM + A_HEADS * A_V_DIM + B_Q_LORA + B_KV_LORA + B_ROPE + AB_WIDTH
C_HEAD = 64
C_HEADS = D_MODEL // C_HEAD
C_WIDTH = C_HEADS * C_HEAD
C_DECAY_LORA = 96
C_ICLR_LORA = 96
L1_IN = 4 * C_WIDTH + 2 * C_DECAY_LORA + 2 * C_ICLR_LORA

kernel_name = 'hybrid_diffusion_diffattn_mla_rwkv7_step'


def rmsnorm(x, g, eps=NORM_EPS):
    xf = x.astype(jnp.float32)
    y = xf * lax.rsqrt(jnp.mean(xf * xf, axis=-1, keepdims=True) + eps)
    return (y * g.astype(jnp.float32)).astype(x.dtype)


def adaln(cond, w, b):
    m = (jax.nn.silu(cond) @ w + b)[:, None, :]
    return jnp.split(m, 3, axis=-1)


def rope_1d(x, pos):
    half = x.shape[-1] // 2
    freqs = ROPE_BASE ** (-jnp.arange(half, dtype=jnp.float32) / half)
    ang = pos.astype(jnp.float32)[:, None] * freqs
    cos = jnp.cos(ang)[None, :, None, :]
    sin = jnp.sin(ang)[None, :, None, :]
    xf = x.astype(jnp.float32)
    x1, x2 = xf[..., :half], xf[..., half:]
    return jnp.concatenate([x1 * cos - x2 * sin, x1 * sin + x2 * cos], axis=-1).astype(x.dtype)


def rope_2d(x):
    t_len = x.shape[1]
    n_rows = t_len // GRID_W
    row = jnp.repeat(jnp.arange(n_rows), GRID_W)
    col = jnp.tile(jnp.arange(GRID_W), n_rows)
    half = x.shape[-1] // 2
    return jnp.concatenate([rope_1d(x[..., :half], row), rope_1d(x[..., half:], col)], axis=-1)


def map_query_blocks(fn, *qs):
    bsz, t_len = qs[0].shape[:2]
    blk = min(Q_BLOCK, t_len)
    n_blk = t_len // blk
    blocks = tuple(jnp.moveaxis(q.reshape((bsz, n_blk, blk) + q.shape[2:]), 1, 0) for q in qs)
    out = lax.map(lambda qb: fn(*qb), blocks)
    out = jnp.moveaxis(out, 0, 1)
    return out.reshape((bsz, t_len) + out.shape[3:])


def mixer_ab(h, params, ctx, layer):
    w_in, w_out, diff_lambda, subln_g, q_norm_g, w_uq, kv_norm_g, w_ukv = params
    f32 = jnp.float32
    bsz, t_len, _ = h.shape
    sizes = (A_HEADS * 2 * A_QK_DIM, A_HEADS * 2 * A_QK_DIM, A_HEADS * A_V_DIM,
             B_Q_LORA, B_KV_LORA, B_ROPE, AB_WIDTH)
    a_q, a_k, a_v, cq, ckv, kpe, gate = jnp.split(h @ w_in, np.cumsum(sizes)[:-1].tolist(), axis=-1)
    a_q = a_q.reshape(bsz, t_len, A_HEADS * 2, A_QK_DIM)
    a_k = a_k.reshape(bsz, t_len, A_HEADS * 2, A_QK_DIM)
    a_v = a_v.reshape(bsz, t_len, A_HEADS, A_V_DIM)
    q_b = (rmsnorm(cq, q_norm_g) @ w_uq).reshape(bsz, t_len, B_HEADS, B_NOPE + B_ROPE)
    q_nope, q_pe = q_b[..., :B_NOPE], q_b[..., B_NOPE:]
    ckv = rmsnorm(ckv, kv_norm_g)
    kpe = kpe[:, :, None, :]
    if ctx is None:
        new = (a_k.reshape(bsz, t_len, A_HEADS, 2, A_QK_DIM), a_v, ckv, kpe[:, :, 0])
        a_q = a_q.reshape(bsz, t_len, A_HEADS, 2, A_QK_DIM)
        k_all, v_all, ckv_all, kpe_all = new
    else:
        k_ctx, v_ctx, ckv_ctx, kpe_ctx = ctx
        a_q = rope_2d(a_q).reshape(bsz, t_len, A_HEADS, 2, A_QK_DIM)
        a_k = rope_2d(a_k).reshape(bsz, t_len, A_HEADS, 2, A_QK_DIM)
        q_pe = rope_2d(q_pe)
        kpe = rope_2d(kpe)[:, :, 0]
        k_all = jnp.concatenate([a_k, k_ctx.astype(a_k.dtype)], axis=1)
        v_all = jnp.concatenate([a_v, v_ctx.astype(a_v.dtype)], axis=1)
        ckv_all = jnp.concatenate([ckv, ckv_ctx.astype(ckv.dtype)], axis=1)
        kpe_all = jnp.concatenate([kpe, kpe_ctx.astype(kpe.dtype)], axis=1)
        new = None

    lam_init = 0.8 - 0.6 * math.exp(-0.3 * layer)
    lp = diff_lambda.astype(f32)
    lam = jnp.exp(jnp.sum(lp[0] * lp[1])) - jnp.exp(jnp.sum(lp[2] * lp[3])) + lam_init
    v_all32 = v_all.astype(f32)

    def diff_block(q):
        s = jnp.einsum('bqhcd,bkhcd->bhcqk', q, k_all, preferred_element_type=f32) * (A_QK_DIM ** -0.5)
        p = jax.nn.softmax(s, axis=-1)
        return jnp.einsum('bhqk,bkhe->bqhe', p[:, :, 0] - lam * p[:, :, 1], v_all32)

    o_a = map_query_blocks(diff_block, a_q)
    o_a = rmsnorm(o_a, subln_g) * (1.0 - lam_init)

    kv_b = (ckv_all @ w_ukv).reshape(bsz, -1, B_HEADS, B_NOPE + B_V)
    k_nope, v_b = kv_b[..., :B_NOPE], kv_b[..., B_NOPE:].astype(f32)

    def mla_block(qn, qp):
        s = (jnp.einsum('bqhd,bkhd->bhqk', qn, k_nope, preferred_element_type=f32)
             + jnp.einsum('bqhd,bkd->bhqk', qp, kpe_all, preferred_element_type=f32)) * ((B_NOPE + B_ROPE) ** -0.5)
        p = jax.nn.softmax(s, axis=-1)
        return jnp.einsum('bhqk,bkhd->bqhd', p, v_b)

    o_b = map_query_blocks(mla_block, q_nope, q_pe)
    y = jnp.concatenate([o_a.reshape(bsz, t_len, -1), o_b.reshape(bsz, t_len, -1)], axis=-1).astype(h.dtype)
    return (y * jax.nn.silu(gate)) @ w_out, new


def token_shift_bidir(x, mu):
    prev = jnp.pad(x[:, :-1], ((0, 0), (1, 0), (0, 0)))
    nxt = jnp.pad(x[:, 1:], ((0, 0), (0, 1), (0, 0)))
    return x + mu[0] * (prev - x) + mu[1] * (nxt - x)


def wkv7_scan(s0, r, decay, k, v, kk, a, reverse):
    def step(s, inp):
        r_t, w_t, k_t, v_t, kk_t, a_t = inp
        sa = jnp.einsum('bhij,bhj->bhi', s, -kk_t)
        s = s * w_t[:, :, None, :] + sa[..., None] * (kk_t * a_t)[:, :, None, :] + v_t[..., None] * k_t[:, :, None, :]
        return s, jnp.einsum('bhij,bhj->bhi', s, r_t)
    xs = tuple(jnp.moveaxis(t, 1, 0) for t in (r, decay, k, v, kk, a))
    s_fin, ys = lax.scan(step, s0, xs, reverse=reverse)
    return s_fin, jnp.moveaxis(ys, 0, 1)


def mixer_rwkv(h, params, ctx, layer):
    w_in, w_out, mu, w0, w2, a0, a2, k_k, k_a, r_k, ln_w, ln_b = params
    f32 = jnp.float32
    bsz, t_len, _ = h.shape
    proj = token_shift_bidir(h @ w_in, mu)
    sizes = (C_WIDTH, C_WIDTH, C_WIDTH, C_WIDTH, 2 * C_DECAY_LORA, 2 * C_ICLR_LORA)
    r, k, v, gate, wd, ad = jnp.split(proj, np.cumsum(sizes)[:-1].tolist(), axis=-1)

    def heads(t):
        return t.astype(f32).reshape(bsz, t_len, C_HEADS, C_HEAD)

    r, k, v = heads(r), heads(k), heads(v)
    kk = k * k_k.astype(f32)
    kk = kk * lax.rsqrt(jnp.maximum(jnp.sum(kk * kk, axis=-1, keepdims=True), 1e-12))
    wd = jnp.tanh(wd.astype(f32).reshape(bsz, t_len, 2, C_DECAY_LORA))
    ad = ad.astype(f32).reshape(bsz, t_len, 2, C_ICLR_LORA)
    w_log = -jax.nn.softplus(-(w0.astype(f32) + jnp.einsum('btzr,zrd->btzd', wd, w2.astype(f32)))) - 0.5
    decay = jnp.exp(-jnp.exp(w_log)).reshape(bsz, t_len, 2, C_HEADS, C_HEAD)
    iclr = jax.nn.sigmoid(a0.astype(f32) + jnp.einsum('btzr,zrd->btzd', ad, a2.astype(f32)))
    iclr = iclr.reshape(bsz, t_len, 2, C_HEADS, C_HEAD)
    if ctx is None:
        zero = jnp.zeros((bsz, C_HEADS, C_HEAD, C_HEAD), f32)
        s_init = (zero, zero)
    else:
        s_init = (ctx[0].astype(f32), ctx[1].astype(f32))
    outs = []
    finals = []
    for z in range(2):
        a_z = iclr[:, :, z]
        k_z = k * (1.0 + (a_z - 1.0) * k_a.astype(f32))
        s_fin, y = wkv7_scan(s_init[z], r, decay[:, :, z], k_z, v, kk, a_z, z == 1)
        mean = jnp.mean(y, axis=-1, keepdims=True)
        var = jnp.mean(jnp.square(y - mean), axis=-1, keepdims=True)
        y = (y - mean) * lax.rsqrt(var + GN_EPS) * ln_w.astype(f32) + ln_b.astype(f32)
        bonus = jnp.sum(r * k_z * r_k.astype(f32), axis=-1, keepdims=True) * v
        outs.append(y + bonus)
        finals.append(s_fin)
    y = (outs[0] + outs[1]).reshape(bsz, t_len, C_WIDTH).astype(h.dtype) * jax.nn.silu(gate)
    new = (finals[0], finals[1]) if ctx is None else None
    return y @ w_out, new


def setup_inputs(seed: int = 0) -> dict:
    key = jax.random.key(seed)
    ks = iter(jax.random.split(key, 34))

    def nrm(shape, scale=1.0):
        return jax.random.normal(next(ks), shape, jnp.float32) * scale

    def unif(shape, lo, hi):
        return jax.random.uniform(next(ks), shape, jnp.float32, lo, hi)

    d = D_MODEL
    return {
        'x_prompt': nrm((BATCH, SEQ, d)),
        'x_sample': nrm((DEC_BATCH, DEC_SEQ, d)),
        'cache_l0_a_k': nrm((DEC_BATCH, PAST_LEN, A_HEADS, 2, A_QK_DIM)),
        'cache_l0_a_v': nrm((DEC_BATCH, PAST_LEN, A_HEADS, A_V_DIM)),
        'cache_l0_mla_ckv': nrm((DEC_BATCH, PAST_LEN, B_KV_LORA)),
        'cache_l0_mla_kpe': nrm((DEC_BATCH, PAST_LEN, B_ROPE)),
        'state_l1_fwd': nrm((DEC_BATCH, C_HEADS, C_HEAD, C_HEAD), 0.5),
        'state_l1_bwd': nrm((DEC_BATCH, C_HEADS, C_HEAD, C_HEAD), 0.5),
        'c': nrm((DEC_BATCH, d)),
        'c_ctx': nrm((d,)),
        'mod_w': nrm((DEPTH, d, 3 * d), d ** -0.5),
        'mod_b': nrm((DEPTH, 3 * d), 0.01),
        'norm_g': 1.0 + nrm((DEPTH, d), 0.02),
        'final_norm_g': 1.0 + nrm((d,), 0.02),
        'l0_w_in': nrm((d, L0_IN), d ** -0.5),
        'l0_w_out': nrm((AB_WIDTH, d), AB_WIDTH ** -0.5),
        'l0_diff_lambda': nrm((4, A_QK_DIM), 0.1),
        'l0_subln_g': 1.0 + nrm((A_V_DIM,), 0.02),
        'l0_q_norm_g': 1.0 + nrm((B_Q_LORA,), 0.02),
        'l0_w_uq': nrm((B_Q_LORA, B_HEADS * (B_NOPE + B_ROPE)), B_Q_LORA ** -0.5),
        'l0_kv_norm_g': 1.0 + nrm((B_KV_LORA,), 0.02),
        'l0_w_ukv': nrm((B_KV_LORA, B_HEADS * (B_NOPE + B_V)), B_KV_LORA ** -0.5),
        'l1_w_in': nrm((d, L1_IN), d ** -0.5),
        'l1_w_out': nrm((C_WIDTH, d), C_WIDTH ** -0.5),
        'l1_mu': unif((2, L1_IN), 0.0, 0.5),
        'l1_w0': unif((2, C_WIDTH), -6.0, -1.0),
        'l1_w2': nrm((2, C_DECAY_LORA, C_WIDTH), 0.1 * C_DECAY_LORA ** -0.5),
        'l1_a0': nrm((2, C_WIDTH), 0.1),
        'l1_a2': nrm((2, C_ICLR_LORA, C_WIDTH), 0.5 * C_ICLR_LORA ** -0.5),
        'l1_k_k': 0.85 + nrm((C_HEADS, C_HEAD), 0.02),
        'l1_k_a': 1.0 + nrm((C_HEADS, C_HEAD), 0.02),
        'l1_r_k': nrm((C_HEADS, C_HEAD), 0.1),
        'l1_ln_w': 1.0 + nrm((C_HEADS, C_HEAD), 0.02),
        'l1_ln_b': nrm((C_HEADS, C_HEAD), 0.01),
    }


def reference(x_prompt, x_sample, cache_l0_a_k, cache_l0_a_v, cache_l0_mla_ckv, cache_l0_mla_kpe,
              state_l1_fwd, state_l1_bwd, c, c_ctx, mod_w, mod_b, norm_g, final_norm_g,
              l0_w_in, l0_w_out, l0_diff_lambda, l0_subln_g, l0_q_norm_g, l0_w_uq, l0_kv_norm_g, l0_w_ukv,
              l1_w_in, l1_w_out, l1_mu, l1_w0, l1_w2, l1_a0, l1_a2, l1_k_k, l1_k_a, l1_r_k, l1_ln_w, l1_ln_b):
    layer_params = (
        (l0_w_in, l0_w_out, l0_diff_lambda, l0_subln_g, l0_q_norm_g, l0_w_uq, l0_kv_norm_g, l0_w_ukv),
        (l1_w_in, l1_w_out, l1_mu, l1_w0, l1_w2, l1_a0, l1_a2, l1_k_k, l1_k_a, l1_r_k, l1_ln_w, l1_ln_b),
    )
    caches = (
        (cache_l0_a_k, cache_l0_a_v, cache_l0_mla_ckv, cache_l0_mla_kpe),
        (state_l1_fwd, state_l1_bwd),
    )
    xp, xs = x_prompt, x_sample
    new_states = []
    for layer in range(DEPTH):
        mixer = mixer_ab if layer % 2 == 0 else mixer_rwkv
        shift_p, scale_p, gate_p = adaln(c_ctx[None, :], mod_w[layer], mod_b[layer])
        shift_s, scale_s, gate_s = adaln(c, mod_w[layer], mod_b[layer])
        hp = rmsnorm(xp, norm_g[layer]) * (1.0 + scale_p) + shift_p
        hs = rmsnorm(xs, norm_g[layer]) * (1.0 + scale_s) + shift_s
        op, new = mixer(hp, layer_params[layer], None, layer)
        os_, _ = mixer(hs, layer_params[layer], caches[layer], layer)
        xp = xp + gate_p * op
        xs = xs + gate_s * os_
        new_states.append(new)
    y_prompt = rmsnorm(xp, final_norm_g)
    y_sample = rmsnorm(xs, final_norm_g)
    new_l0_a_k, new_l0_a_v, new_l0_mla_ckv, new_l0_mla_kpe = new_states[0]
    new_l1_state_fwd, new_l1_state_bwd = new_states[1]
    return (y_prompt, y_sample, new_l0_a_k, new_l0_a_v, new_l0_mla_ckv, new_l0_mla_kpe, new_l1_state_fwd, new_l1_state_bwd)
```

```python
import math
import os
from contextlib import ExitStack

import numpy as np
import concourse.bass as bass
import concourse.mybir as mybir
from concourse.bass_utils import run_bass_kernel_spmd

F32 = mybir.dt.float32
BF16 = mybir.dt.bfloat16
AF = mybir.ActivationFunctionType
ALU = mybir.AluOpType
AX = mybir.AxisListType

ENGS = ("pe", "act", "dve", "pool", "sp")
SAME_ENG_SYNC = {"act", "dve", "pool"}

T = 1024
D = 2048
NT = 8
KC = 16
NEG = -30000.0


class Buf:
    __slots__ = ("name", "last_w", "readers", "excl")

    def __init__(self, name):
        self.name = name
        self.last_w = None
        self.readers = []
        self.excl = False


class DSem:
    __slots__ = ("h", "count", "max_wait", "name", "twin")

    def __init__(self, name):
        self.name = name
        self.h = None
        self.count = 0
        self.max_wait = 0
        self.twin = None


class Op:
    __slots__ = ("eng", "fn", "deps", "signal", "dma", "dsem", "dval", "val", "waits", "pre")

    def __init__(self, eng, fn):
        self.eng = eng
        self.fn = fn
        self.deps = []
        self.signal = False
        self.dma = False
        self.dsem = None
        self.dval = 0
        self.val = 0
        self.waits = []
        self.pre = []


class Sched:
    def __init__(self, nc):
        self.nc = nc
        self.ops = []
        self.dsems = []
        self.last = {e: None for e in ENGS}
        self._force = False

    def buf(self, name):
        return Buf(name)

    def dsem(self, name):
        d = DSem(name)
        self.dsems.append(d)
        return d

    def _dep_on(self, op, d):
        if d is None or d is op:
            return
        if d.dma:
            v = d.dsem.count
            d.dsem.max_wait = max(d.dsem.max_wait, v)
            op.pre.append((d.dsem, v))
        else:
            if d.eng == op.eng and not op.dma and d.eng not in SAME_ENG_SYNC and not self._force:
                return
            op.deps.append(d)

    @staticmethod
    def _flat(xs):
        out = []
        for x in xs:
            if isinstance(x, (list, tuple)):
                out.extend(Sched._flat(x))
            else:
                out.append(x)
        return out

    def add(self, eng, fn, reads=(), writes=(), dma=False, dsem=None, force=False):
        reads = self._flat(reads)
        writes = self._flat(writes)
        xr = [b for b in reads if b.excl]
        if xr:
            reads = [b for b in reads if not b.excl]
            writes = writes + [b for b in xr if b not in writes]
        op = Op(eng, fn)
        op.dma = dma
        self._force = force
        seen = set()
        for b in reads:
            if b.last_w is not None and id(b.last_w) not in seen:
                seen.add(id(b.last_w))
                self._dep_on(op, b.last_w)
        for b in writes:
            if b.last_w is not None and id(b.last_w) not in seen:
                seen.add(id(b.last_w))
                self._dep_on(op, b.last_w)
            for r in b.readers:
                if id(r) not in seen:
                    seen.add(id(r))
                    self._dep_on(op, r)
        if dma:
            if eng == "pool":
                if dsem.twin is None:
                    dsem.twin = self.dsem(dsem.name + "_sw")
                dsem = dsem.twin
            op.dsem = dsem
            if dsem.max_wait > 0:
                op.pre.append((dsem, dsem.max_wait))
            dsem.count += 16
            op.dval = dsem.count
        for b in writes:
            b.last_w = op
            b.readers = []
        for b in reads:
            if b.last_w is not op:
                b.readers.append(op)
        self.ops.append(op)
        if not dma:
            self.last[eng] = op
        return op

    def fence(self):
        lasts = dict(self.last)
        for e in ENGS:
            op = Op(e, lambda eng: eng.nop())
            for e2 in ENGS:
                if e2 != e and lasts[e2] is not None:
                    op.deps.append(lasts[e2])
            for ds in self.dsems:
                if ds.count > 0:
                    ds.max_wait = max(ds.max_wait, ds.count)
                    op.pre.append((ds, ds.count))
            self.ops.append(op)
            self.last[e] = op

    def emit(self):
        nc = self.nc
        seqc = {e: 0 for e in ENGS}
        for op in self.ops:
            if not op.dma:
                seqc[op.eng] += 1
                op.val = seqc[op.eng]
        known = {e: {} for e in ENGS}
        snap = {}
        per_eng = {e: [] for e in ENGS}
        for op in self.ops:
            k = known[op.eng]
            need_e = {}
            for d in op.deps:
                if need_e.get(d.eng, (0, None))[0] < d.val:
                    need_e[d.eng] = (d.val, d)
            need_d = {}
            for (ds, v) in op.pre:
                if id(ds) not in need_d or need_d[id(ds)][1] < v:
                    need_d[id(ds)] = (ds, v)
            fin = []
            for e2, (sq, d) in sorted(need_e.items(), key=lambda kv: -kv[1][0]):
                if k.get(("E", e2), 0) >= sq:
                    continue
                fin.append(d)
                d.signal = True
                k[("E", e2)] = sq
                for key, v in snap.get(id(d), {}).items():
                    if k.get(key, 0) < v:
                        k[key] = v
            for key, (ds, v) in need_d.items():
                if k.get(("D", key), 0) >= v:
                    continue
                fin.append((ds, v))
                k[("D", key)] = v
            op.waits = fin
            if not op.dma:
                sn = dict(k)
                sn[("E", op.eng)] = op.val
                snap[id(op)] = sn
            per_eng[op.eng].append(op)
        snap.clear()
        cnt = {e: 0 for e in ENGS}
        for op in self.ops:
            if not op.dma:
                if op.signal:
                    cnt[op.eng] += 1
                    op.val = cnt[op.eng]
                else:
                    op.val = -1
        self.nsig = dict(cnt)
        with ExitStack() as es:
            esem = {e: es.enter_context(nc.semaphore("es_" + e)) for e in ENGS}
            for i, ds in enumerate(self.dsems):
                if ds.count > 0:
                    ds.h = es.enter_context(nc.semaphore("ds%d" % i))
            block = es.enter_context(nc.Block())

            def run(eng_name):
                def body(eng):
                    for op in per_eng[eng_name]:
                        for w in op.waits:
                            if isinstance(w, Op):
                                eng.wait_ge(esem[w.eng], w.val)
                            else:
                                eng.wait_ge(w[0].h, w[1])
                        ins = op.fn(eng)
                        if op.dma:
                            ins.then_inc(op.dsem.h, 16)
                        elif op.signal:
                            ins.then_inc(esem[eng_name], 1)
                    if eng_name == "sp":
                        for ds in self.dsems:
                            if ds.count > 0:
                                eng.wait_ge(ds.h, ds.count)
                        for e in ENGS:
                            if e != "sp" and cnt[e] > 0:
                                eng.wait_ge(esem[e], cnt[e])
                return body

            block.tensor(run("pe"))
            block.scalar(run("act"))
            block.vector(run("dve"))
            block.gpsimd(run("pool"))
            block.sync(run("sp"))
        self.stats = {e: len(per_eng[e]) for e in ENGS}


A_HEADS = 8
B_HEADS = 8
B_Q_LORA = 512
B_KV_LORA = 256
L0_IN = 5952
L1_IN = 8576
C_HEADS = 32
LORA = 96
NORM_EPS = 1e-6
GN_EPS = 64e-5
LAM_INIT0 = 0.8 - 0.6 * math.exp(-0.3 * 0)

DEBUG = bool(int(os.environ.get("KDEBUG", "0")))
LAYOUT = {}
SKIP_L1 = bool(int(os.environ.get("KSKIP_L1", "0")))
FAKE = bool(int(os.environ.get("KFAKE", "0")))


def l0_blocks():
    blks = []
    for i in range(4):
        blks.append(("cq%d" % i, 3072 + 128 * i, 128))
    for i in range(2):
        blks.append(("ckv%d" % i, 3584 + 128 * i, 128))
    blks.append(("kpe", 3840, 64))
    for h in range(8):
        blks.append(("q%d" % h, 128 * h, 128))
        blks.append(("k%d" % h, 1024 + 128 * h, 128))
        blks.append(("v%d" % h, 2048 + 128 * h, 128))
        blks.append(("g%d" % h, 3904 + 128 * h, 128))
    for h in range(8):
        blks.append(("g%d" % (8 + h), 3904 + 128 * (8 + h), 128))
    return blks


def l1_blocks():
    blks = []
    base = 4 * 2048
    for z in range(2):
        blks.append(("wd%d" % z, base + LORA * z, LORA))
    for z in range(2):
        blks.append(("ad%d" % z, base + 2 * LORA + LORA * z, LORA))
    for hp in range(16):
        blks.append(("r%d" % hp, 128 * hp, 128))
        blks.append(("k%d" % hp, 2048 + 128 * hp, 128))
        blks.append(("v%d" % hp, 4096 + 128 * hp, 128))
        blks.append(("g%d" % hp, 6144 + 128 * hp, 128))
    return blks


def pack_blocks(w, blks):
    kk = w.shape[0] // 128
    out = np.zeros((len(blks), 128, kk, 128), np.float32)
    w3 = w.reshape(kk, 128, w.shape[1])
    for i, (_, c0, n) in enumerate(blks):
        out[i, :, :, :n] = w3[:, :, c0:c0 + n].transpose(1, 0, 2)
    return out


def build_program():
    nc = bass.Bass("TRN2", target_bir_lowering=False)
    S = Sched(nc)
    L0B = l0_blocks()
    L1B = l1_blocks()

    def din(name, shape, dt=F32):
        return nc.dram_tensor(name, list(shape), dt, kind="ExternalInput").ap()

    def dout(name, shape, dt=F32):
        return nc.dram_tensor(name, list(shape), dt, kind="ExternalOutput").ap()

    x_in = din("x", [T, D])
    cond_in = din("cond", [128, KC])
    NMOD = 1 if FAKE else 24
    NB0 = 4 if FAKE else len(L0B)
    NB1 = 4 if FAKE else len(L1B)
    NOUT = 1 if FAKE else 8
    modw_in = din("modw", [2, NMOD, 128, KC, 256])
    modb_in = din("modb", [2, 1, 6144])
    normg_in = din("normg", [128, 2, KC])
    fng_in = din("fng", [1, D])
    w0in = din("w0in", [NB0, 128, KC, 128])
    w0out = din("w0out", [NOUT, 128, KC, 256])
    wuq_in = din("wuq", [16, 128, 4, 128])
    wukv_in = din("wukv", [16, 128, 2, 128])
    lam_in = din("lam", [1, 256])
    subg_in = din("subg", [128, 1])
    qng_in = din("qng", [128, 4])
    kvng_in = din("kvng", [128, 2])
    cos_in = din("cos", [128, T])
    sin_in = din("sin", [128, T])
    rm_in = din("rm", [128, 128])
    ident_in = din("ident", [128, 128])
    maskk_in = din("maskk", [8, 1536])
    maskq_in = din("maskq", [8, T])
    ck_in = din("ck", [512, 1024])
    cv_in = din("cv", [512, 1024])
    cckv_in = din("cckv", [512, 256])
    ckpe_in = din("ckpe", [512, 64])
    w1in = din("w1in", [NB1, 128, KC, 128])
    w1out = din("w1out", [NOUT, 128, KC, 256])
    mu_in = din("mu", [128, len(L1B), 2])
    w0row_in = din("w0row", [2, 1, 2048])
    w2_in = din("w2", [2, LORA, 2048])
    a2_in = din("a2", [2, LORA, 2048])
    a0_in = din("a0", [128, 2, 16])
    hpar_in = din("hpar", [128, 5, 16])
    st_in = din("st", [2, 32, 64, 64])
    keep_in = din("keep", [128, 2])
    tri_in = din("tri", [2, 128, 384])
    msk_in = din("msk", [2, 128, 256])
    blk64_in = din("blk64", [128, 128])
    lvl_in = din("lvl", [4, 128, 128])
    y_out = dout("y", [T, D])
    nk_out = dout("nk", [T, 1024])
    nv_out = dout("nv", [T, 1024])
    nckv_out = dout("nckv", [T, 256])
    nkpe_out = dout("nkpe", [T, 64])
    sst_out = dout("sst", [2, 4, 32, 64, 64])
    x1_dr = nc.dram_tensor("x1s", [T, D], F32, kind="Internal").ap()
    gsc_dr = nc.dram_tensor("gsc", [2, 1, D], F32, kind="Internal").ap()
    yT_dr = nc.dram_tensor("yTs", [KC, 128, T], BF16, kind="Internal").ap()
    dbg = {}
    if DEBUG:
        dbg["hT"] = dout("dbg_hT", [128, KC * T], BF16)
        dbg["mod"] = dout("dbg_mod", [128, 96])

    with ExitStack() as es:
        def sbt(name, shape, dt):
            return es.enter_context(nc.sbuf_tensor("s_" + name, list(shape), dt))

        hT = sbt("hT", [128, KC, T], BF16)
        b_hT = [S.buf("hT%d" % i) for i in range(NT)]
        b_yT = [S.buf("yTdr%d" % u) for u in range(KC)]
        d_yT = [S.dsem("dyT%d" % u) for u in range(KC)]
        NSLOT = 4
        wring = sbt("wring", [128, NSLOT * KC * 128], BF16)
        b_w = [S.buf("w%d" % i) for i in range(NSLOT)]
        d_w = [S.dsem("dw%d" % i) for i in range(NSLOT)]
        ident32 = sbt("ident32", [128, 128], F32)
        ident16 = sbt("ident16", [128, 128], BF16)
        ones16 = sbt("ones16", [128, 128], BF16)
        ones32 = sbt("ones32", [128, 128], F32)
        modT = sbt("modT", [128, 96], F32)
        normg = sbt("normg", [128, 2, KC], F32)
        modA = sbt("modA", [128, 2, KC], F32)
        gate_b = sbt("gate_b", [128, D], F32)
        eps_t = sbt("eps_t", [128, 4], F32)
        keep_t = sbt("keep_t", [128, 2], F32)
        b_const = S.buf("const")
        b_mod = S.buf("mod")
        b_gateb = S.buf("gateb")
        b_gsc = S.buf("gsc")
        d_c = S.dsem("dconst")
        ARENA_W = 38144
        arena = sbt("arena", [128, ARENA_W], F32)
        psum = [es.enter_context(nc.psum_tensor("ps%d" % i, [128, 512], F32)) for i in range(8)]
        b_ps = [S.buf("ps%d" % i) for i in range(8)]
        for b_ in b_ps:
            b_.excl = True

        def dma(eng, out, in_, reads, writes, dsem):
            S.add(eng, lambda e: e.dma_start(out=out, in_=in_), reads=reads, writes=writes, dma=True, dsem=dsem)

        last_rg = {}

        def mm(out, lhsT, rhs, start, stop, reads, writes, rg=None):
            force = False
            for w in Sched._flat(writes):
                if w.excl:
                    if last_rg.get(id(w), rg) != rg:
                        force = True
                    last_rg[id(w)] = rg
            S.add("pe", lambda e: e.matmul(out, lhsT=lhsT, rhs=rhs, start=start, stop=stop), reads=reads, writes=writes, force=force)

        def tr(out, in_, idn, reads, writes):
            S.add("pe", lambda e: e.transpose(out, in_, idn), reads=reads, writes=writes)

        def act(out, in_, func, reads, writes, bias=None, scale=None, accum=None):
            kw = {}
            if bias is not None:
                kw["bias"] = bias
            if scale is not None:
                kw["scale"] = scale
            if accum is not None:
                kw["accum_out"] = accum
            S.add("act", lambda e: e.activation(out=out, in_=in_, func=func, **kw), reads=reads, writes=writes)

        def tt(eng, out, in0, in1, op, reads, writes):
            S.add(eng, lambda e: e.tensor_tensor(out=out, in0=in0, in1=in1, op=op), reads=reads, writes=writes)

        def ts(eng, out, in0, s1, s2, op0, op1, reads, writes):
            if s2 is None:
                S.add(eng, lambda e: e.tensor_scalar(out=out, in0=in0, scalar1=s1, scalar2=None, op0=op0), reads=reads, writes=writes)
            else:
                S.add(eng, lambda e: e.tensor_scalar(out=out, in0=in0, scalar1=s1, scalar2=s2, op0=op0, op1=op1), reads=reads, writes=writes)

        def stt(eng, out, in0, scalar, in1, op0, op1, reads, writes):
            S.add(eng, lambda e: e.scalar_tensor_tensor(out=out, in0=in0, scalar=scalar, in1=in1, op0=op0, op1=op1), reads=reads, writes=writes)

        def cp(eng, out, in_, reads, writes):
            if eng == "act":
                S.add("act", lambda e: e.copy(out=out, in_=in_), reads=reads, writes=writes)
            else:
                S.add(eng, lambda e: e.tensor_copy(out=out, in_=in_), reads=reads, writes=writes)

        def recip(out, in_, reads, writes):
            S.add("dve", lambda e: e.reciprocal(out=out, in_=in_), reads=reads, writes=writes)

        def memset(eng, ap, val, writes):
            S.add(eng, lambda e: e.memset(ap, val), writes=writes)

        bank_rr = [0]

        def bank(lo=4, hi=8):
            n = hi - lo
            i = lo + (bank_rr[0] % n)
            bank_rr[0] += 1
            return i

        wslot = [0]

        def wload(src_ap, nwords):
            i = wslot[0] % NSLOT
            wslot[0] += 1
            dst = wring[:, i * KC * 128: i * KC * 128 + nwords]
            dma("pool", dst, src_ap, [], [b_w[i]], d_w[i])
            return i

        def wview(i, kk, n=128):
            return wring[:, i * KC * 128: i * KC * 128 + kk * n].rearrange("p (k c) -> p k c", c=n)

        REG = 256
        region_bufs = [S.buf("ar%d" % i) for i in range((ARENA_W + REG - 1) // REG)]

        class Arena:
            def __init__(self, off=0):
                self.off = off
                self.peak = off
                self.log = []

            def _take(self, w):
                w = ((w + REG - 1) // REG) * REG
                o = self.off
                self.off += w
                self.peak = max(self.peak, self.off)
                self.log.append(o)
                assert self.off <= ARENA_W or os.environ.get("KNOASSERT"), ("arena overflow", self.off)
                return o, region_bufs[o // REG:(o + w) // REG]

            def f32(self, n):
                o, bufs = self._take(n)
                return arena[:, o:o + n], bufs

            def bf16(self, n):
                w = (n + 1) // 2
                o, bufs = self._take(w)
                return arena[:, o:o + w].bitcast(BF16)[:, 0:n], bufs

        dma("sp", ident32[:], ident_in, [], [b_const], d_c)
        dma("pool", ident16[:], ident_in, [], [b_const], d_c)
        dma("sp", normg[:], normg_in, [], [b_const], d_c)
        dma("sp", keep_t[:], keep_in, [], [b_const], d_c)
        memset("dve", ones16[:], 1.0, [b_const])
        memset("dve", ones32[:], 1.0, [b_const])
        memset("dve", eps_t[:, 0:1], NORM_EPS, [b_const])
        memset("dve", eps_t[:, 1:2], GN_EPS, [b_const])
        memset("dve", eps_t[:, 2:3], 1e-12, [b_const])
        memset("dve", eps_t[:, 3:4], 0.0, [b_const])

        def wload3(src_ap3, kk, n=128):
            i = wslot[0] % NSLOT
            wslot[0] += 1
            dst = wview(i, kk, n)
            dma("pool", dst, src_ap3, [], [b_w[i]], d_w[i])
            return i

        def bcast_row(dst_tile, row_ap, reads, wbuf):
            for q in range(4):
                pb = bank()
                mm(psum[pb][:, :], ones32[0:1, :], row_ap[0:1, q * 512:(q + 1) * 512], True, True,
                   reads + [b_const], [b_ps[pb]])
                cp("act", dst_tile[:, q * 512:(q + 1) * 512], psum[pb][:, :], [b_ps[pb]], [wbuf])

        def modulation3():
            ar = Arena()
            cond_t, b_cd = ar.f32(KC)
            scond, b_sc = ar.bf16(KC)
            mrow, b_mr = ar.f32(6144)
            modb_t, b_mb = ar.f32(6144)
            d_l = S.dsem("dmod")
            d_g = S.dsem("dgsc")
            dma("sp", cond_t, cond_in, [], [b_cd], d_l)
            act(scond, cond_t, AF.Silu, [b_cd], [b_sc])
            for l in range(2):
                dma("sp", modb_t[0:1, :], modb_in[l], [], [b_mb], d_l)
                for blk in range(24):
                    pb = bank()
                    for half in range(2):
                        si = wload3(modw_in[l, blk % NMOD, :, :, half * 128:(half + 1) * 128], KC)
                        wv = wview(si, KC)
                        for k in range(KC):
                            mm(psum[pb][0:1, half * 128:(half + 1) * 128], scond[:, k:k + 1], wv[:, k, :],
                               k == 0, k == KC - 1, [b_w[si], b_sc], [b_ps[pb]])
                    tt("dve", mrow[0:1, blk * 256:(blk + 1) * 256], psum[pb][0:1, 0:256],
                       modb_t[0:1, blk * 256:(blk + 1) * 256], ALU.add, [b_ps[pb], b_mb], [b_mr])
                pb = bank()
                for c in range(48):
                    mm(psum[pb][:, c:c + 1], mrow[0:1, c * 128:(c + 1) * 128], ones32[0:1, 0:1], True, True,
                       [b_mr, b_const], [b_ps[pb]])
                cp("dve", modT[:, 48 * l:48 * l + 48], psum[pb][:, 0:48], [b_ps[pb]], [b_mod])
                ts("dve", modA[:, l, :], modT[:, 48 * l + 16:48 * l + 32], 1.0, None, ALU.add, None, [b_mod], [b_mod])
                tt("dve", modA[:, l, :], modA[:, l, :], normg[:, l, :], ALU.mult, [b_mod, b_const], [b_mod])
                dma("sp", gsc_dr[l], mrow[0:1, 2 * D:3 * D], [b_mr], [b_gsc], d_g)

        def load_row_bcast(ar, src_dram_row, src_bufs, dst_tile, dst_buf, name):
            grow, b_g = ar.f32(D)
            d_gr = S.dsem("d" + name)
            dma("sp", grow[0:1, :], src_dram_row, src_bufs, [b_g], d_gr)
            bcast_row(dst_tile, grow, [b_g], dst_buf)

        def norm_phase(l, x_src, x_src_bufs, ar):
            xt = [ar.f32(D), ar.f32(D)]
            xn = [ar.bf16(D), ar.bf16(D)]
            junk, b_j = ar.bf16(D)
            st, b_st = ar.f32(4 * NT)
            d_x = [S.dsem("dnx%d_%d" % (l, i)) for i in range(2)]
            for i in range(NT):
                s = i % 2
                xs, b_x = xt[s]
                xb, b_xn = xn[s]
                dma("sp", xs, x_src[i * 128:(i + 1) * 128, :], [x_src_bufs[i]] if x_src_bufs else [], [b_x], d_x[s])
                act(junk, xs, AF.Square, [b_x], [b_j, b_st], accum=st[:, 4 * i:4 * i + 1])
                act(st[:, 4 * i + 1:4 * i + 2], st[:, 4 * i:4 * i + 1], AF.Sqrt, [b_st, b_const], [b_st],
                    bias=eps_t[:, 0:1], scale=1.0 / D)
                recip(st[:, 4 * i + 2:4 * i + 3], st[:, 4 * i + 1:4 * i + 2], [b_st], [b_st])
                ts("dve", xb, xs, st[:, 4 * i + 2:4 * i + 3], None, ALU.mult, None, [b_x, b_st], [b_xn])
                for g in range(4):
                    pb = bank()
                    pv = psum[pb][:, :].bitcast(BF16)
                    for j in range(4):
                        k = 4 * g + j
                        tr(pv[:, j * 128:(j + 1) * 128], xb[:, k * 128:(k + 1) * 128], ident16[:, :],
                           [b_xn, b_const], [b_ps[pb]])
                    for j in range(4):
                        k = 4 * g + j
                        act(hT[:, k, i * 128:(i + 1) * 128], pv[:, j * 128:(j + 1) * 128], AF.Identity,
                            [b_ps[pb], b_mod], [b_hT[i]], bias=modT[:, 48 * l + k:48 * l + k + 1],
                            scale=modA[:, l, k:k + 1])

        def outproj_phase(wout_in, x_src, x_src_bufs, x_dst, x_dst_bufs, final, ar, tag):
            yTa, b_ya = ar.bf16(KC * T)
            yT = yTa.rearrange("p (u t) -> p u t", t=T)
            d_ya = S.dsem("dya" + tag)
            for u in range(KC):
                dma("sp", yT[:, u, :], yT_dr[u], [b_yT[u]], [b_ya], d_ya)
            fng_b = None
            if final:
                fng_b, b_fng = ar.f32(D)
                load_row_bcast(ar, fng_in, [], fng_b, b_fng, "fng")
            xt = [ar.f32(D) for _ in range(4)]
            tmp, b_tmp = ar.f32(256)
            junk, b_j = ar.bf16(D)
            st, b_st = ar.f32(4 * NT)
            d_x = [S.dsem("dox%s%d" % (tag, i)) for i in range(4)]
            d_o = [S.dsem("doo%s%d" % (tag, i)) for i in range(4)]
            for grp in range(2):
                tiles = list(range(4 * grp, 4 * grp + 4))
                for fb in range(8):
                    sl = [wload3(wout_in[fb % NOUT, :, :, half * 128:(half + 1) * 128], KC) for half in range(2)]
                    for i in tiles:
                        s = i % 4
                        xs, b_x = xt[s]
                        if fb == 0:
                            dma("sp", xs, x_src[i * 128:(i + 1) * 128, :], [x_src_bufs[i]] if x_src_bufs else [], [b_x], d_x[s])
                        pb = bank()
                        for half in range(2):
                            wv = wview(sl[half], KC)
                            for u in range(KC):
                                mm(psum[pb][:, half * 128:(half + 1) * 128], yT[:, u, i * 128:(i + 1) * 128], wv[:, u, :],
                                   u == 0, u == KC - 1, [b_w[sl[half]], b_ya], [b_ps[pb]])
                        tt("dve", tmp[:, 0:256], psum[pb][:, 0:256], gate_b[:, fb * 256:(fb + 1) * 256], ALU.mult,
                           [b_ps[pb], b_gateb], [b_tmp])
                        tt("pool", xs[:, fb * 256:(fb + 1) * 256], xs[:, fb * 256:(fb + 1) * 256], tmp[:, 0:256],
                           ALU.add, [b_tmp, b_x], [b_x])
                        if fb == 7:
                            if final:
                                act(junk, xs, AF.Square, [b_x], [b_j, b_st], accum=st[:, 4 * i:4 * i + 1])
                                act(st[:, 4 * i + 1:4 * i + 2], st[:, 4 * i:4 * i + 1], AF.Sqrt, [b_st, b_const], [b_st],
                                    bias=eps_t[:, 0:1], scale=1.0 / D)
                                recip(st[:, 4 * i + 2:4 * i + 3], st[:, 4 * i + 1:4 * i + 2], [b_st], [b_st])
                                stt("dve", xs, xs, st[:, 4 * i + 2:4 * i + 3], fng_b[:, :], ALU.mult, ALU.mult,
                                    [b_x, b_st, b_fng], [b_x])
                            dma("sp", x_dst[i * 128:(i + 1) * 128, :], xs, [b_x], [x_dst_bufs[i]] if x_dst_bufs else [], d_o[s])

        def store_y(u, yc, b_yc):
            dma("sp", yT_dr[u], yc, [b_yc], [b_yT[u]], d_yT[u])

        def l0_phase():
            ar = Arena()
            cos_t, b_cos = ar.f32(T)
            sin_t, b_sin = ar.f32(T)
            rm_t, b_rm = ar.f32(128)
            mk16, b_mk = ar.bf16(1536)
            mq16, b_mq = ar.bf16(T)
            b_tab = [b_cos, b_sin, b_rm, b_mk, b_mq]
            sm, b_sm = ar.f32(16)
            lamrow, b_lr = ar.f32(272)
            cqg16a, b_cq = ar.bf16(4 * T)
            cqg16 = cqg16a.rearrange("p (k t) -> p k t", t=T)
            rstdq, b_rq = ar.f32(T)
            rkv, b_rkv = ar.f32(T)
            ckvn16a, b_ckv = ar.bf16(2 * 1536)
            ckvn16 = ckvn16a.rearrange("p (k t) -> p k t", t=1536)
            kpe16, b_kpe = ar.bf16(1536)
            qpe16, b_qpe = ar.bf16(T)
            stAa, b_stA = ar.f32(2 * T)
            stA = stAa.rearrange("p (k t) -> p k t", t=T)
            b_st = [b_stA[0:4], b_stA[4:8]]
            sqs = [ar.bf16(512), ar.bf16(512)]
            sq16 = [sqs[0][0], sqs[1][0]]
            b_sq = [sqs[0][1], sqs[1][1]]
            tmpa, b_ta = ar.f32(512)
            tmpb, b_tb = ar.f32(512)
            ctxs, b_ctx = ar.f32(1024)
            ost, b_ost = ar.f32(512)
            q16, b_q = ar.bf16(T)
            K16, b_K = ar.bf16(1536)
            V16a, b_V = ar.bf16(12 * 128)
            V16 = V16a.rearrange("p (k c) -> p k c", c=128)
            vst, b_vst = ar.f32(512)
            gT16, b_g = ar.bf16(T)
            pts = [ar.bf16(512) for _ in range(4)]
            PT = [p[0] for p in pts]
            b_PT = [p[1] for p in pts]
            os_ = [ar.f32(512) for _ in range(3)]
            o32 = [p[0] for p in os_]
            b_o = [p[1] for p in os_]
            ycs = [ar.bf16(T), ar.bf16(T)]
            d_tab = S.dsem("dl0tab")
            d_ctx = S.dsem("dctx")
            d_ost = S.dsem("dost")
            d_V = S.dsem("dV16")
            d_vst = S.dsem("dvst")
            bi_of = {name: i for i, (name, _, _) in enumerate(L0B)}
            sqr = [0]
            ptr = [0]

            dma("sp", cos_t, cos_in, [], [b_tab], d_tab)
            dma("sp", sin_t, sin_in, [], [b_tab], d_tab)
            dma("sp", rm_t, rm_in, [], [b_tab], d_tab)
            dma("pool", mk16[0:8, :], maskk_in, [], [b_tab], d_tab)
            dma("pool", mq16[0:8, :], maskq_in, [], [b_tab], d_tab)
            dma("pool", kpe16[64:72, :], maskk_in, [], [b_kpe], d_tab)
            dma("pool", qpe16[64:72, :], maskq_in, [], [b_qpe], d_tab)
            dma("sp", sm[:, 1:2], subg_in, [], [b_sm], d_tab)
            dma("sp", sm[:, 2:6], qng_in, [], [b_sm], d_tab)
            dma("sp", sm[:, 6:8], kvng_in, [], [b_sm], d_tab)
            dma("sp", lamrow[0:1, 0:256], lam_in, [], [b_lr], d_tab)
            tt("dve", lamrow[0:1, 0:64], lamrow[0:1, 0:64], lamrow[0:1, 64:128], ALU.mult, [b_lr], [b_lr])
            tt("dve", lamrow[0:1, 128:192], lamrow[0:1, 128:192], lamrow[0:1, 192:256], ALU.mult, [b_lr], [b_lr])
            S.add("dve", lambda e: e.reduce_sum(out=lamrow[0:1, 256:257], in_=lamrow[0:1, 0:64], axis=AX.X), reads=[b_lr], writes=[b_lr])
            S.add("dve", lambda e: e.reduce_sum(out=lamrow[0:1, 257:258], in_=lamrow[0:1, 128:192], axis=AX.X), reads=[b_lr], writes=[b_lr])
            act(lamrow[0:1, 258:260], lamrow[0:1, 256:258], AF.Exp, [b_lr], [b_lr])
            tt("dve", lamrow[0:1, 260:261], lamrow[0:1, 259:260], lamrow[0:1, 258:259], ALU.subtract, [b_lr], [b_lr])
            ts("dve", lamrow[0:1, 261:262], lamrow[0:1, 260:261], -LAM_INIT0, None, ALU.add, None, [b_lr], [b_lr])
            pb = bank()
            mm(psum[pb][:, 0:1], ones32[0:1, :], lamrow[0:1, 261:262], True, True, [b_lr, b_const], [b_ps[pb]])
            cp("dve", sm[:, 0:1], psum[pb][:, 0:1], [b_ps[pb]], [b_sm])
            ts("dve", sm[:, 1:2], sm[:, 1:2], 1.0 - LAM_INIT0, None, ALU.mult, None, [b_sm], [b_sm])

            L0S = float(os.environ.get("KL0", "9"))
            if L0S <= 1:
                return

            def inproj_fm(win, bi, ncols, evac):
                si = wload3(win[bi % NB0], KC)
                wv = wview(si, KC)
                for half in range(2):
                    pb = bank()
                    for k in range(KC):
                        mm(psum[pb][0:ncols, :], wv[:, k, 0:ncols], hT[:, k, half * 512:(half + 1) * 512],
                           k == 0, k == KC - 1, [b_w[si]] + b_hT[4 * half:4 * half + 4], [b_ps[pb]])
                    evac(half, pb)

            def rope(src, bsrc, dst, bdst, nrow):
                for half in range(2):
                    hs = slice(half * 512, (half + 1) * 512)
                    pb = bank()
                    mm(psum[pb][0:nrow, :], rm_t[0:nrow, 0:nrow], src[0:nrow, hs], True, True, [bsrc, b_tab], [b_ps[pb]])
                    tt("pool", tmpa[0:nrow, :], src[0:nrow, hs], cos_t[0:nrow, hs], ALU.mult, [bsrc, b_tab], [b_ta])
                    tt("dve", tmpb[0:nrow, :], psum[pb][0:nrow, :], sin_t[0:nrow, hs], ALU.mult, [b_ps[pb], b_tab], [b_tb])
                    tt("pool", dst[0:nrow, hs], tmpa[0:nrow, :], tmpb[0:nrow, :], ALU.add, [b_ta, b_tb], [bdst])

            def rstd_from(pbank, out_ap, n, wbuf, npart=128):
                act(out_ap, psum[pbank][0:npart, :], AF.Sqrt, [b_ps[pbank], b_const], [wbuf], bias=eps_t[0:npart, 0:1], scale=1.0 / n)
                recip(out_ap, out_ap, [wbuf], [wbuf])

            for b in range(4):
                def ev(half, pb, b=b):
                    r = sqr[0] % 2
                    sqr[0] += 1
                    act(sq16[r], psum[pb][:, :], AF.Square, [b_ps[pb]], [b_sq[r]])
                    ts("dve", cqg16[:, b, half * 512:(half + 1) * 512], psum[pb][:, :], sm[:, 2 + b:3 + b], None, ALU.mult, None,
                       [b_ps[pb], b_sm], [b_cq])
                    mm(psum[half][:, :], ones16[:, :], sq16[r], b == 0, b == 3, [b_sq[r], b_const], [b_ps[half]])
                inproj_fm(w0in, bi_of["cq%d" % b], 128, ev)
            for half in range(2):
                rstd_from(half, rstdq[:, half * 512:(half + 1) * 512], B_Q_LORA, b_rq)
            if L0S <= 1.2:
                return
            for b in range(2):
                def ev(half, pb, b=b):
                    r = sqr[0] % 2
                    sqr[0] += 1
                    act(sq16[r], psum[pb][:, :], AF.Square, [b_ps[pb]], [b_sq[r]])
                    cp("dve", stA[:, b, half * 512:(half + 1) * 512], psum[pb][:, :], [b_ps[pb]], [b_st[b]])
                    mm(psum[half][:, :], ones16[:, :], sq16[r], b == 0, b == 1, [b_sq[r], b_const], [b_ps[half]])
                inproj_fm(w0in, bi_of["ckv%d" % b], 128, ev)
            for half in range(2):
                rstd_from(half, rkv[:, half * 512:(half + 1) * 512], B_KV_LORA, b_rkv)
            for b in range(2):
                stt("dve", stA[:, b, :], stA[:, b, :], sm[:, 6 + b:7 + b], rkv[:, :], ALU.mult, ALU.mult,
                    [b_st[b], b_sm, b_rkv], [b_st[b]])
                cp("pool", ckvn16[:, b, 0:T], stA[:, b, :], [b_st[b]], [b_ckv])
            if L0S <= 1.4:
                return
            for i in range(NT):
                pb = bank()
                for b in range(2):
                    tr(psum[pb][:, b * 128:(b + 1) * 128], stA[:, b, i * 128:(i + 1) * 128], ident32[:, :],
                       [b_st[b], b_const], [b_ps[pb]])
                cp("act", ost[:, 0:256], psum[pb][:, 0:256], [b_ps[pb]], [b_ost])
                dma("sp", nckv_out[i * 128:(i + 1) * 128, :], ost[:, 0:256], [b_ost], [], d_ost)
            if L0S <= 1.6:
                return
            dma("sp", ctxs.rearrange("p (a c) -> p a c", c=256), cckv_in.rearrange("(a p) c -> p a c", p=128), [], [b_ctx], d_ctx)
            for b in range(2):
                pb = bank()
                for a in range(4):
                    tr(psum[pb][:, a * 128:(a + 1) * 128], ctxs[:, a * 256 + b * 128:a * 256 + (b + 1) * 128], ident32[:, :],
                       [b_ctx, b_const], [b_ps[pb]])
                cp("act", ckvn16[:, b, T:1536], psum[pb][:, :], [b_ps[pb]], [b_ckv])
            if L0S <= 1.8:
                return
            def ev_kpe(half, pb):
                cp("act", stA[0:64, 0, half * 512:(half + 1) * 512], psum[pb][0:64, :], [b_ps[pb]], [b_st[0]])
            inproj_fm(w0in, bi_of["kpe"], 64, ev_kpe)
            pb = bank()
            for i in range(NT):
                tr(psum[pb][:, i * 64:(i + 1) * 64], stA[0:64, 0, i * 128:(i + 1) * 128], ident32[0:64, 0:64],
                   [b_st[0], b_const], [b_ps[pb]])
            cp("act", ost[:, :], psum[pb][:, :], [b_ps[pb]], [b_ost])
            dma("sp", nkpe_out.rearrange("(i p) c -> p i c", p=128), ost.rearrange("p (i c) -> p i c", c=64), [b_ost], [], d_ost)
            rope(stA[:, 0, :], b_st[0], kpe16, b_kpe, 64)
            dma("sp", ctxs[:, 0:256].rearrange("p (a c) -> p a c", c=64), ckpe_in.rearrange("(a p) c -> p a c", p=128), [], [b_ctx], d_ctx)
            pb = bank()
            for a in range(4):
                tr(psum[pb][0:64, a * 128:(a + 1) * 128], ctxs[:, a * 64:(a + 1) * 64], ident32[:, :],
                   [b_ctx, b_const], [b_ps[pb]])
            cp("act", kpe16[0:64, T:1536], psum[pb][0:64, :], [b_ps[pb]], [b_kpe])

            if L0S <= 2:
                return

            def gate_block(u):
                def ev(half, pb):
                    act(gT16[:, half * 512:(half + 1) * 512], psum[pb][:, :], AF.Silu, [b_ps[pb]], [b_g])
                inproj_fm(w0in, bi_of["g%d" % u], 128, ev)

            def attention(u, diff):
                for qh in range(2):
                    qs = slice(qh * 512, (qh + 1) * 512)
                    ncomp = 2 if diff else 1
                    for kt in range(12):
                        ks = slice(kt * 128, (kt + 1) * 128)
                        for c in range(ncomp):
                            sb = bank()
                            if diff:
                                mm(psum[sb][:, :], K16[64 * c:64 * c + 64, ks], q16[64 * c:64 * c + 64, qs], True, False,
                                   [b_K, b_q], [b_ps[sb]], rg=c)
                                mm(psum[sb][:, :], mk16[0:8, ks], mq16[0:8, qs], False, True, [b_tab], [b_ps[sb]], rg=0)
                                sc = 64 ** -0.5
                            else:
                                mm(psum[sb][:, :], K16[:, ks], q16[:, qs], True, False, [b_K, b_q], [b_ps[sb]])
                                mm(psum[sb][:, :], kpe16[0:72, ks], qpe16[0:72, qs], False, True, [b_kpe, b_qpe], [b_ps[sb]])
                                sc = 192 ** -0.5
                            r = ptr[0] % 4
                            ptr[0] += 1
                            act(PT[r], psum[sb][:, :], AF.Exp, [b_ps[sb]], [b_PT[r]], scale=sc)
                            mm(psum[c][:, :], V16[:, kt, :], PT[r], kt == 0, kt == 11, [b_V, b_PT[r]], [b_ps[c]])
                            mm(psum[2 + c][:, :], ones16[:, :], PT[r], kt == 0, kt == 11, [b_const, b_PT[r]], [b_ps[2 + c]])
                    recip(o32[0], psum[2][:, :], [b_ps[2]], [b_o[0]])
                    tt("dve", o32[0], psum[0][:, :], o32[0], ALU.mult, [b_ps[0], b_o[0]], [b_o[0]])
                    if diff:
                        recip(o32[1], psum[3][:, :], [b_ps[3]], [b_o[1]])
                        tt("dve", o32[1], psum[1][:, :], o32[1], ALU.mult, [b_ps[1], b_o[1]], [b_o[1]])
                        stt("dve", o32[0], o32[1], sm[:, 0:1], o32[0], ALU.mult, ALU.add, [b_o[0], b_o[1], b_sm], [b_o[0]])
                        r = sqr[0] % 2
                        sqr[0] += 1
                        act(sq16[r], o32[0], AF.Square, [b_o[0]], [b_sq[r]])
                        pb = bank()
                        mm(psum[pb][:, :], ones16[:, :], sq16[r], True, True, [b_sq[r], b_const], [b_ps[pb]])
                        rstd_from(pb, o32[2], 128, b_o[2])
                        stt("dve", o32[0], o32[0], sm[:, 1:2], o32[2], ALU.mult, ALU.mult, [b_o[0], b_o[2], b_sm], [b_o[0]])
                    yc, b_yc = ycs[u % 2]
                    tt("pool", yc[:, qs], o32[0], gT16[:, qs], ALU.mult, [b_o[0], b_g], [b_yc])
                    if qh == 1:
                        store_y(u, yc, b_yc)

            for h in range(8 if L0S > 3 else 1):
                def evq(half, pb):
                    cp("act", stA[:, 0, half * 512:(half + 1) * 512], psum[pb][:, :], [b_ps[pb]], [b_st[0]])

                def evk(half, pb):
                    cp("act", stA[:, 1, half * 512:(half + 1) * 512], psum[pb][:, :], [b_ps[pb]], [b_st[1]])
                inproj_fm(w0in, bi_of["q%d" % h], 128, evq)
                inproj_fm(w0in, bi_of["k%d" % h], 128, evk)
                for g in range(2):
                    pb = bank()
                    for ii in range(4):
                        i = 4 * g + ii
                        tr(psum[pb][:, ii * 128:(ii + 1) * 128], stA[:, 1, i * 128:(i + 1) * 128], ident32[:, :],
                           [b_st[1], b_const], [b_ps[pb]])
                    cp("act", ost[:, :], psum[pb][:, :], [b_ps[pb]], [b_ost])
                    dma("sp", nk_out[g * 512:(g + 1) * 512, h * 128:(h + 1) * 128].rearrange("(i p) c -> p i c", p=128),
                        ost.rearrange("p (i c) -> p i c", c=128), [b_ost], [], d_ost)
                rope(stA[:, 0, :], b_st[0], q16, b_q, 128)
                rope(stA[:, 1, :], b_st[1], K16, b_K, 128)
                dma("sp", ctxs[:, 0:512].rearrange("p (a c) -> p a c", c=128),
                    ck_in[:, h * 128:(h + 1) * 128].rearrange("(a p) c -> p a c", p=128), [], [b_ctx], d_ctx)
                pb = bank()
                for a in range(4):
                    tr(psum[pb][:, a * 128:(a + 1) * 128], ctxs[:, a * 128:(a + 1) * 128], ident32[:, :],
                       [b_ctx, b_const], [b_ps[pb]])
                cp("act", K16[:, T:1536], psum[pb][:, :], [b_ps[pb]], [b_K])
                si = wload3(w0in[bi_of["v%d" % h] % NB0], KC)
                wv = wview(si, KC)
                for g in range(2):
                    pb = bank()
                    for ii in range(4):
                        i = 4 * g + ii
                        for k in range(KC):
                            mm(psum[pb][:, ii * 128:(ii + 1) * 128], hT[:, k, i * 128:(i + 1) * 128], wv[:, k, :],
                               k == 0, k == KC - 1, [b_w[si], b_hT[i]], [b_ps[pb]])
                    cp("act", V16[:, 4 * g:4 * g + 4, :], psum[pb][:, :].rearrange("p (a c) -> p a c", c=128), [b_ps[pb]], [b_V])
                    cp("dve", vst[:, :], psum[pb][:, :], [b_ps[pb]], [b_vst])
                    dma("sp", nv_out[g * 512:(g + 1) * 512, h * 128:(h + 1) * 128].rearrange("(i p) c -> p i c", p=128),
                        vst.rearrange("p (i c) -> p i c", c=128), [b_vst], [], d_vst)
                dma("pool", V16[:, 8:12, :], cv_in[:, h * 128:(h + 1) * 128].rearrange("(a p) c -> p a c", p=128), [], [b_V], d_V)
                gate_block(h)
                attention(h, True)

            if L0S <= 4:
                return
            for h in range(8 if L0S > 5 else 1):
                u = 8 + h
                si = wload3(wuq_in[h], 4)
                wv = wview(si, 4)
                for half in range(2):
                    hs = slice(half * 512, (half + 1) * 512)
                    pb = bank()
                    for k in range(4):
                        mm(psum[pb][:, :], wv[:, k, :], cqg16[:, k, hs], k == 0, k == 3, [b_w[si], b_cq], [b_ps[pb]])
                    tt("dve", q16[:, hs], psum[pb][:, :], rstdq[:, hs], ALU.mult, [b_ps[pb], b_rq], [b_q])
                si = wload3(wuq_in[8 + h], 4)
                wv = wview(si, 4)
                for half in range(2):
                    hs = slice(half * 512, (half + 1) * 512)
                    pb = bank()
                    for k in range(4):
                        mm(psum[pb][0:64, :], wv[:, k, 0:64], cqg16[:, k, hs], k == 0, k == 3, [b_w[si], b_cq], [b_ps[pb]])
                    tt("dve", stA[0:64, 0, hs], psum[pb][0:64, :], rstdq[0:64, hs], ALU.mult, [b_ps[pb], b_rq], [b_st[0]])
                rope(stA[:, 0, :], b_st[0], qpe16, b_qpe, 64)
                si = wload3(wukv_in[h], 2)
                wv = wview(si, 2)
                for kb in range(3):
                    pb = bank()
                    for k in range(2):
                        mm(psum[pb][:, :], wv[:, k, :], ckvn16[:, k, kb * 512:(kb + 1) * 512], k == 0, k == 1,
                           [b_w[si], b_ckv], [b_ps[pb]])
                    cp("act", K16[:, kb * 512:(kb + 1) * 512], psum[pb][:, :], [b_ps[pb]], [b_K])
                si = wload3(wukv_in[8 + h], 2)
                wv = wview(si, 2)
                for g in range(3):
                    pb = bank()
                    for ii in range(4):
                        kt = 4 * g + ii
                        for k in range(2):
                            mm(psum[pb][:, ii * 128:(ii + 1) * 128], ckvn16[:, k, kt * 128:(kt + 1) * 128], wv[:, k, :],
                               k == 0, k == 1, [b_w[si], b_ckv], [b_ps[pb]])
                    cp("act", V16[:, 4 * g:4 * g + 4, :], psum[pb][:, :].rearrange("p (a c) -> p a c", c=128), [b_ps[pb]], [b_V])
                gate_block(u)
                attention(u, False)
        def l1_phase():
            ar = Arena()
            tri_a, b_tri = ar.f32(768)
            tri = [tri_a[:, 0:384], tri_a[:, 384:768]]
            msk_a, b_msk = ar.bf16(1024)
            msk4 = [msk_a[:, 0:512], msk_a[:, 512:1024]]
            lvl_a, b_lvl = ar.bf16(4 * 256)
            lvl = lvl_a.rearrange("p (l c) -> p l c", c=256)
            mskT_a, b_mskT = ar.f32(256)
            mskT = [mskT_a[:, 0:128], mskT_a[:, 128:256]]
            blk16, b_blk = ar.bf16(128)
            small, b_small = ar.f32(1024)
            NBL1 = len(L1B)
            mu_t = small[:, 0:2 * NBL1].rearrange("p (b m) -> p b m", m=2)
            c0_t = small[:, 136:136 + NBL1]
            mufix = small[:, 204:204 + 2 * NBL1].rearrange("p (b m) -> p b m", m=2)
            a0_t = small[:, 340:372].rearrange("p (z h) -> p z h", h=16)
            hpar = small[:, 372:468].rearrange("p (w h) -> p w h", h=16)
            b_c = [b_tri, b_msk, b_mskT, b_blk, b_small, b_lvl]
            w0h, b_w0h = ar.f32(256)
            twad = [ar.bf16(T) for _ in range(4)]
            tw16 = [twad[0][0], twad[1][0]]
            ad16 = [twad[2][0], twad[3][0]]
            b_tw = [twad[0][1], twad[1][1]]
            b_ad = [twad[2][1], twad[3][1]]
            w2h_a, b_w2 = ar.bf16(4 * 128)
            w2h = w2h_a.rearrange("p (w c) -> p w c", c=128)
            r32, b_r = ar.f32(T)
            k32, b_k = ar.f32(T)
            v32, b_v = ar.f32(T)
            kk32, b_kk = ar.f32(T)
            acc32, b_acc = ar.f32(T)
            g16, b_g = ar.bf16(T)
            v16, b_v16 = ar.bf16(T)
            bon32, b_bon = ar.f32(T)
            rawy, b_rawy = ar.f32(T)
            raw, b_raw = rawy, b_rawy
            lw32, b_lw = rawy, b_rawy
            y32, b_y = rawy, b_rawy
            sqs = [ar.bf16(512), ar.bf16(512)]
            sq16 = [sqs[0][0], sqs[1][0]]
            b_sq = [sqs[0][1], sqs[1][1]]
            tmp32, b_tmp = ar.f32(512)
            grpA, b_grpA = ar.f32(3072)
            G3 = grpA.rearrange("p (i c) -> p i c", c=384)
            b_G3 = b_grpA
            YX = [grpA[:, 1024 * r_:1024 * (r_ + 1)].bitcast(BF16).rearrange("p (q c) -> p q c", c=256) for r_ in range(2)]
            b_YX = [[b_grpA[4 * r_ + ql // 2] for ql in range(8)] for r_ in range(2)]
            TMPm = grpA[:, 2048:3072].bitcast(BF16).rearrange("p (q c) -> p q c", c=256)
            b_TM = [b_grpA[8 + ql // 2] for ql in range(8)]
            grpB, b_grpB = ar.f32(2048)
            kz32, b_kz = grpB[:, 0:T], b_grpB[0:4]
            al32, b_al = grpB[:, T:2 * T], b_grpB[4:8]
            P2 = grpB[:, 0:1024].bitcast(BF16).rearrange("p (q c) -> p q c", c=256)
            b_P2 = [b_grpB[ql // 2] for ql in range(8)]
            P4 = grpB[:, 1024:2048].bitcast(BF16).rearrange("p (q c) -> p q c", c=256)
            b_P4 = [b_grpB[4 + ql // 2] for ql in range(8)]
            grpC, b_grpC = ar.f32(T)
            a32, b_a = grpC, b_grpC
            d32, b_d = grpC, b_grpC
            Ginv, b_Gi = grpC, b_grpC
            MN_a, b_MN_all = ar.bf16(16 * 256)
            MN = MN_a.rearrange("p (q c) -> p q c", c=256)
            b_MN = [b_MN_all[q // 2] for q in range(16)]
            XTf_a, b_XTf_all = ar.bf16(16 * 128)
            XTf = XTf_a.rearrange("p (q c) -> p q c", c=128)
            b_XTf = [b_XTf_all[q // 4] for q in range(16)]
            BR_a, b_BR = ar.bf16(NT * 256)
            BR = BR_a.rearrange("p (i c) -> p i c", c=256)
            KA_a, b_KA = ar.bf16(NT * 256)
            KA = KA_a.rearrange("p (i c) -> p i c", c=256)
            UG = KA_a.rearrange("p (q c) -> p q c", c=128)
            b_UG = [b_KA[q // 4] for q in range(16)]
            Kb_a, b_Kb = ar.bf16(NT * 256)
            KbAb = Kb_a.rearrange("p (i c) -> p i c", c=256)
            wl16 = Kb_a[:, 0:1024].rearrange("p (q c) -> p q c", c=64)
            b_wl = [b_Kb[q // 8] for q in range(16)]
            lkt = [Kb_a[:, 1024:1152], Kb_a[:, 1536:1664]]
            b_lkt = [b_Kb[2], b_Kb[3]]
            tok_a, b_tok_all = ar.bf16(NT * 512)
            tokT = tok_a.rearrange("p (i c) -> p i c", c=512)
            b_tok = [b_tok_all[i] for i in range(NT)]
            GM_a, b_GM_all = ar.bf16(16 * 256)
            GM = GM_a.rearrange("p (q c) -> p q c", c=256)
            b_GM = [b_GM_all[q // 2] for q in range(16)]
            Qt, b_Qt = ar.bf16(T)
            TT_a, b_TT = ar.f32(16 * 128)
            TT = TT_a.rearrange("p (c j) -> p c j", j=128)
            HL_a, b_HL = ar.f32(16 * 64)
            HL = HL_a.rearrange("p (c j) -> p c j", j=64)
            Hs_a, b_Hs = ar.f32(17 * 64)
            Hs = Hs_a.rearrange("p (n j) -> p n j", j=64)
            H16_a, b_H16 = ar.bf16(17 * 64)
            Hs16 = H16_a.rearrange("p (n j) -> p n j", j=64)
            slsto, b_slsto = ar.f32(256)
            sl32, b_sl = slsto[:, 0:128], b_slsto
            sto, b_sto = slsto[:, 128:256], b_slsto
            ycs = [ar.bf16(T)] * 2
            LAYOUT["l1"] = list(ar.log)
            LAYOUT["l1_total"] = ar.off
            d_c1 = S.dsem("dl1c")
            d_w2 = S.dsem("dw2h")
            d_sl = S.dsem("dsl")
            d_sto = S.dsem("dsto")
            bi_of = {name: i for i, (name, _, _) in enumerate(L1B)}
            sqr = [0]
            lkr = [0]
            lastcol_of = [63, 0]

            for z in range(2):
                dma("sp", tri[z], tri_in[z], [], [b_c], d_c1)
                for rep_ in range(2):
                    dma("pool", msk4[z][:, rep_ * 256:(rep_ + 1) * 256], msk_in[z], [], [b_c], d_c1)
                dma("sp", mskT[z], msk_in[1 - z, :, 0:128], [], [b_c], d_c1)
            dma("pool", blk16, blk64_in, [], [b_c], d_c1)
            for l_ in range(4):
                for rep_ in range(2):
                    dma("pool", lvl[:, l_, rep_ * 128:(rep_ + 1) * 128], lvl_in[l_], [], [b_c], d_c1)
            dma("sp", mu_t, mu_in, [], [b_c], d_c1)
            dma("sp", a0_t, a0_in, [], [b_c], d_c1)
            dma("sp", hpar[:, 0:5, :], hpar_in, [], [b_c], d_c1)
            ts("dve", hpar[:, 5, :], hpar[:, 1, :], -1.0, 1.0, ALU.mult, ALU.add, [b_c], [b_c])
            tt("dve", c0_t, mu_t[:, :, 0], mu_t[:, :, 1], ALU.add, [b_c], [b_c])
            ts("dve", c0_t, c0_t, -1.0, 1.0, ALU.mult, ALU.add, [b_c], [b_c])
            ts("dve", mufix.rearrange("p b m -> p (b m)"), mu_t.rearrange("p b m -> p (b m)"), keep_t[:, 1:2], -1.0, ALU.mult, ALU.mult,
               [b_c, b_const], [b_c])
            memset("dve", TT.rearrange("p c j -> p (c j)"), 0.0, [b_TT])
            d_ones = S.dsem("dl1ones")
            for z in range(2):
                pass

            def inproj_fm(bi, ncols, evac):
                si = wload3(w1in[bi % NB1], KC)
                wv = wview(si, KC)
                for half in range(2):
                    pb = bank()
                    for k in range(KC):
                        mm(psum[pb][0:ncols, :], wv[:, k, 0:ncols], hT[:, k, half * 512:(half + 1) * 512],
                           k == 0, k == KC - 1, [b_w[si]] + b_hT[4 * half:4 * half + 4], [b_ps[pb]])
                    evac(half, pb)

            def shifted(bi, nrow, dst, bdst):
                def ev(half, pb):
                    cp("act", raw[0:nrow, half * 512:(half + 1) * 512], psum[pb][0:nrow, :], [b_ps[pb]], [b_raw])
                inproj_fm(bi, nrow, ev)
                act(dst[0:nrow, :], raw[0:nrow, :], AF.Identity, [b_raw, b_c], [bdst], scale=c0_t[0:nrow, bi:bi + 1])
                stt("dve", dst[0:nrow, 1:T], raw[0:nrow, 0:T - 1], mu_t[0:nrow, bi, 0:1], dst[0:nrow, 1:T], ALU.mult, ALU.add,
                    [b_raw, b_c, bdst], [bdst])
                stt("dve", dst[0:nrow, 0:T - 1], raw[0:nrow, 1:T], mu_t[0:nrow, bi, 1:2], dst[0:nrow, 0:T - 1], ALU.mult, ALU.add,
                    [b_raw, b_c, bdst], [bdst])
                for bnd in (256, 512, 768):
                    stt("dve", dst[0:nrow, bnd:bnd + 1], raw[0:nrow, bnd - 1:bnd], mufix[0:nrow, bi, 0:1], dst[0:nrow, bnd:bnd + 1],
                        ALU.mult, ALU.add, [b_raw, b_c, bdst], [bdst])
                    stt("dve", dst[0:nrow, bnd - 1:bnd], raw[0:nrow, bnd:bnd + 1], mufix[0:nrow, bi, 1:2], dst[0:nrow, bnd - 1:bnd],
                        ALU.mult, ALU.add, [b_raw, b_c, bdst], [bdst])

            for z in range(2):
                shifted(bi_of["wd%d" % z], LORA, r32, b_r)
                act(tw16[z][0:LORA, :], r32[0:LORA, :], AF.Tanh, [b_r], [b_tw[z]])
                shifted(bi_of["ad%d" % z], LORA, r32, b_r)
                cp("dve", ad16[z][0:LORA, :], r32[0:LORA, :], [b_r], [b_ad[z]])

            L1S = float(os.environ.get("KL1", "9"))
            if L1S <= 1:
                return
            for hp in range(16):
                cs = slice(hp * 128, (hp + 1) * 128)
                for z in range(2):
                    dma("pool", w2h[0:LORA, z, :], w2_in[z][:, cs], [], [b_w2], d_w2)
                    dma("pool", w2h[0:LORA, 2 + z, :], a2_in[z][:, cs], [], [b_w2], d_w2)
                    dma("sp", w0h[0:1, z * 128:(z + 1) * 128], w0row_in[z][:, cs], [], [b_w0h], d_w2)
                shifted(bi_of["r%d" % hp], 128, r32, b_r)
                shifted(bi_of["k%d" % hp], 128, k32, b_k)
                shifted(bi_of["v%d" % hp], 128, v32, b_v)
                shifted(bi_of["g%d" % hp], 128, acc32, b_acc)
                act(g16[:, :], acc32[:, :], AF.Silu, [b_acc], [b_g])
                cp("pool", v16[:, :], v32[:, :], [b_v], [b_v16])
                ts("dve", kk32[:, :], k32[:, :], hpar[:, 0, hp:hp + 1], None, ALU.mult, None, [b_k, b_c], [b_kk])

                for half in range(2):
                    hs = slice(half * 512, (half + 1) * 512)
                    r = sqr[0] % 2
                    sqr[0] += 1
                    act(sq16[r], kk32[:, hs], AF.Square, [b_kk], [b_sq[r]])
                    pb = bank()
                    mm(psum[pb][:, :], blk16[:, :], sq16[r], True, True, [b_sq[r], b_c], [b_ps[pb]])
                    ts("dve", tmp32[:, :], psum[pb][:, :], 1e-12, None, ALU.max, None, [b_ps[pb]], [b_tmp])
                    act(tmp32[:, :], tmp32[:, :], AF.Sqrt, [b_tmp], [b_tmp])
                    recip(tmp32[:, :], tmp32[:, :], [b_tmp], [b_tmp])
                    tt("dve", kk32[:, hs], kk32[:, hs], tmp32[:, :], ALU.mult, [b_kk, b_tmp], [b_kk])

                for z in range(2):
                    for half in range(2):
                        hs = slice(half * 512, (half + 1) * 512)
                        pb = bank()
                        mm(psum[pb][:, :], w2h[0:LORA, 2 + z, :], ad16[z][0:LORA, hs], True, True, [b_w2, b_ad[z]], [b_ps[pb]])
                        act(a32[:, hs], psum[pb][:, :], AF.Sigmoid, [b_ps[pb], b_c], [b_a], bias=a0_t[:, z, hp:hp + 1])
                    ts("dve", kz32[:, :], a32[:, :], hpar[:, 1, hp:hp + 1], hpar[:, 5, hp:hp + 1], ALU.mult, ALU.add, [b_a, b_c], [b_kz])
                    tt("pool", kz32[:, :], kz32[:, :], k32[:, :], ALU.mult, [b_kz, b_k], [b_kz])
                    stt("dve", al32[:, :], kk32[:, :], -1.0, a32[:, :], ALU.mult, ALU.mult, [b_kk, b_a], [b_al])
                    tt("pool", d32[:, :], r32[:, :], kz32[:, :], ALU.mult, [b_r, b_kz], [b_d])
                    for half in range(2):
                        hs = slice(half * 512, (half + 1) * 512)
                        r = sqr[0] % 2
                        sqr[0] += 1
                        ts("dve", sq16[r], d32[:, hs], hpar[:, 2, hp:hp + 1], None, ALU.mult, None, [b_d, b_c], [b_sq[r]])
                        pb = bank()
                        mm(psum[pb][:, :], blk16[:, :], sq16[r], True, True, [b_sq[r], b_c], [b_ps[pb]])
                        tt("dve", bon32[:, hs], psum[pb][:, :], v32[:, hs], ALU.mult, [b_ps[pb], b_v], [b_bon])
                    for g in range(2):
                        pb = bank()
                        for ii in range(4):
                            i = 4 * g + ii
                            mm(psum[pb][:, ii * 128:(ii + 1) * 128], tw16[z][0:LORA, i * 128:(i + 1) * 128], w2h[0:LORA, z, :],
                               True, False, [b_tw[z], b_w2], [b_ps[pb]])
                            mm(psum[pb][:, ii * 128:(ii + 1) * 128], ones32[0:1, :], w0h[0:1, z * 128:(z + 1) * 128],
                               False, True, [b_const, b_w0h], [b_ps[pb]])
                        act(lw32[:, g * 512:(g + 1) * 512], psum[pb][:, :], AF.Sigmoid, [b_ps[pb]], [b_lw])
                    ts("dve", lw32[:, :], lw32[:, :], -math.exp(-0.5), None, ALU.mult, None, [b_lw], [b_lw])
                    for i in range(NT):
                        pb = bank()
                        mm(psum[pb][:, 0:384], lw32[:, i * 128:(i + 1) * 128], tri[z][:, :], True, True, [b_lw, b_c], [b_ps[pb]])
                        act(G3[:, i, :], psum[pb][:, 0:384], AF.Exp, [b_ps[pb]], [b_G3])
                        act(Ginv[:, i * 128:(i + 1) * 128], psum[pb][:, 0:128], AF.Exp, [b_ps[pb]], [b_Gi], scale=-1.0)

                    def v3(t32):
                        return t32.rearrange("p (i c) -> p i c", c=128)
                    tt("dve", BR[:, :, 0:128], v3(kk32), G3[:, :, 128:256], ALU.mult, [b_kk, b_G3], [b_BR])
                    tt("pool", BR[:, :, 128:256], v3(r32), G3[:, :, 0:128], ALU.mult, [b_r, b_G3], [b_BR])
                    tt("dve", KA[:, :, 0:128], v3(kz32), v3(Ginv), ALU.mult, [b_kz, b_Gi], [b_KA])
                    tt("pool", KA[:, :, 128:256], v3(al32), v3(Ginv), ALU.mult, [b_al, b_Gi], [b_KA])
                    tt("dve", KbAb[:, :, 0:128], v3(kz32), G3[:, :, 256:384], ALU.mult, [b_kz, b_G3], [b_Kb])
                    tt("pool", KbAb[:, :, 128:256], v3(al32), G3[:, :, 256:384], ALU.mult, [b_al, b_G3], [b_Kb])
                    if L1S <= 2:
                        return
                    for i in range(NT):
                        pb = bank()
                        pv = psum[pb].bitcast(BF16)
                        tr(pv[:, 0:128], v16[:, i * 128:(i + 1) * 128], ident16[:, :], [b_v16, b_const], [b_ps[pb]])
                        tr(pv[:, 128:256], BR[:, i, 0:128], ident16[:, :], [b_BR, b_const], [b_ps[pb]])
                        tr(pv[:, 256:384], KbAb[:, i, 0:128], ident16[:, :], [b_Kb, b_const], [b_ps[pb]])
                        tr(pv[:, 384:512], KbAb[:, i, 128:256], ident16[:, :], [b_Kb, b_const], [b_ps[pb]])
                        cp("act", tokT[:, i, :], pv[:, 0:512], [b_ps[pb]], [b_tok[i]])
                    gam = small[:, 724:740]
                    cp("dve", gam.rearrange("p (i c) -> p i c", c=2), G3[:, :, lastcol_of[z]:lastcol_of[z] + 65:64], [b_G3], [b_small])
                    for i in range(NT):
                        for e in range(2):
                            q = 2 * i + e
                            es_ = slice(64 * e, 64 * e + 64)
                            pb = bank()
                            mm(psum[pb][:, 0:256], KA[es_, i, 0:128], BR[es_, i, 0:256], True, True, [b_KA, b_BR], [b_ps[pb]], rg=e)
                            mm(psum[pb][:, 256:512], KA[es_, i, 128:256], BR[es_, i, 0:256], True, True, [b_KA, b_BR], [b_ps[pb]], rg=e)
                            r = lkr[0] % 2
                            lkr[0] += 1
                            tt("dve", lkt[r], psum[pb][:, 0:128], msk4[z][:, 0:128], ALU.mult, [b_ps[pb], b_c], [b_lkt[r]])
                            tt("dve", GM[:, q, 0:128], psum[pb][:, 128:256], msk4[z][:, 128:256], ALU.mult, [b_ps[pb], b_c], [b_GM[q]])
                            tt("dve", MN[:, q, 0:128], psum[pb][:, 256:384], msk4[z][:, 0:128], ALU.mult, [b_ps[pb], b_c], [b_MN[q]])
                            tt("dve", GM[:, q, 128:256], psum[pb][:, 384:512], msk4[z][:, 128:256], ALU.mult, [b_ps[pb], b_c], [b_GM[q]])
                            pb = bank()
                            mm(psum[pb][:, 0:128], BR[es_, i, 0:128], KA[es_, i, 128:256], True, True, [b_KA, b_BR], [b_ps[pb]], rg=e)
                            tt("dve", MN[:, q, 128:256], psum[pb][:, 0:128], mskT[z][:, :], ALU.mult, [b_ps[pb], b_c], [b_MN[q]])
                            pb = bank()
                            mm(psum[pb][:, 0:64], lkt[r], tokT[:, i, 64 * e:64 * e + 64], True, True, [b_lkt[r], b_tok[i]], [b_ps[pb]])
                            cp("act", wl16[:, q, :], psum[pb][:, 0:64], [b_ps[pb]], [b_wl[q]])
                    if L1S <= 3:
                        return
                    def pair_mm(lhs_a, rhs_a, lhs_b, rhs_b, reads):
                        pb_ = bank()
                        mm(psum[pb_][:, 0:128], lhs_a, rhs_a, True, True, reads, [b_ps[pb_]])
                        mm(psum[pb_][:, 128:256], lhs_b, rhs_b, True, True, reads, [b_ps[pb_]])
                        return pb_
                    for bt in range(2):
                        qs_ = list(range(8 * bt, 8 * bt + 8))
                        for q in qs_:
                            ql = q % 8
                            tt("pool", TMPm[:, ql, :], MN[:, q, :], lvl[:, 0, :], ALU.mult, [b_MN[q], b_c], [b_TM[ql]])
                            tt("pool", YX[0][:, ql, 0:128], TMPm[:, ql, 0:128], ident16[:, :], ALU.add, [b_TM[ql], b_const], [b_YX[0][ql]])
                            tt("pool", YX[0][:, ql, 128:256], TMPm[:, ql, 128:256], ident16[:, :], ALU.add, [b_TM[ql], b_const], [b_YX[0][ql]])
                        for q in qs_:
                            ql = q % 8
                            pb_ = pair_mm(TMPm[:, ql, 128:256], TMPm[:, ql, 0:128], TMPm[:, ql, 0:128], TMPm[:, ql, 128:256], [b_TM[ql]])
                            cp("act", P2[:, ql, :], psum[pb_][:, 0:256], [b_ps[pb_]], [b_P2[ql]])
                        for q in qs_:
                            ql = q % 8
                            pb_ = pair_mm(P2[:, ql, 128:256], YX[0][:, ql, 0:128], P2[:, ql, 0:128], YX[0][:, ql, 128:256], [b_P2[ql], b_YX[0][ql]])
                            tt("dve", YX[1][:, ql, :], psum[pb_][:, 0:256], YX[0][:, ql, :], ALU.add, [b_ps[pb_], b_YX[0][ql]], [b_YX[1][ql]])
                        for q in qs_:
                            ql = q % 8
                            pb_ = pair_mm(P2[:, ql, 128:256], P2[:, ql, 0:128], P2[:, ql, 0:128], P2[:, ql, 128:256], [b_P2[ql]])
                            cp("act", P4[:, ql, :], psum[pb_][:, 0:256], [b_ps[pb_]], [b_P4[ql]])
                        for q in qs_:
                            ql = q % 8
                            pb_ = pair_mm(P4[:, ql, 128:256], YX[1][:, ql, 0:128], P4[:, ql, 0:128], YX[1][:, ql, 128:256], [b_P4[ql], b_YX[1][ql]])
                            tt("dve", YX[0][:, ql, :], psum[pb_][:, 0:256], YX[1][:, ql, :], ALU.add, [b_ps[pb_], b_YX[1][ql]], [b_YX[0][ql]])
                        cur = 0
                        for l_ in (1, 2, 3):
                            nxt = 1 - cur
                            for q in qs_:
                                ql = q % 8
                                tt("pool", TMPm[:, ql, :], MN[:, q, :], lvl[:, l_, :], ALU.mult, [b_MN[q], b_c], [b_TM[ql]])
                            for q in qs_:
                                ql = q % 8
                                pb_ = pair_mm(TMPm[:, ql, 128:256], YX[cur][:, ql, 0:128], TMPm[:, ql, 0:128], YX[cur][:, ql, 128:256],
                                              [b_TM[ql], b_YX[cur][ql]])
                                cp("act", P2[:, ql, :], psum[pb_][:, 0:256], [b_ps[pb_]], [b_P2[ql]])
                            for q in qs_:
                                ql = q % 8
                                pb_ = pair_mm(YX[cur][:, ql, 128:256], P2[:, ql, 0:128], YX[cur][:, ql, 0:128], P2[:, ql, 128:256],
                                              [b_P2[ql], b_YX[cur][ql]])
                                if l_ < 3:
                                    tt("dve", YX[nxt][:, ql, :], psum[pb_][:, 0:256], YX[cur][:, ql, :], ALU.add, [b_ps[pb_], b_YX[cur][ql]], [b_YX[nxt][ql]])
                                else:
                                    tt("dve", XTf[:, q, :], psum[pb_][:, 0:128], YX[cur][:, ql, 0:128], ALU.add, [b_ps[pb_], b_YX[cur][ql]], [b_XTf[q]])
                            cur = nxt
                    X = XTf
                    bX = b_XTf
                    if L1S <= 4:
                        return
                    for i in range(NT):
                        for e in range(2):
                            q = 2 * i + e
                            pb = bank()
                            mm(psum[pb][:, 0:64], X[:, q, :], wl16[:, q, :], True, True, [bX[q], b_wl[q]], [b_ps[pb]])
                            mm(psum[pb][:, 64:128], X[:, q, :], tokT[:, i, 128 + 64 * e:128 + 64 * e + 64], True, True, [bX[q], b_tok[i]], [b_ps[pb]])
                            cp("act", UG[:, q, :], psum[pb][:, 0:128], [b_ps[pb]], [b_UG[q]])
                    for i in range(NT):
                        pb = bank()
                        for e in range(2):
                            q = 2 * i + e
                            es_ = slice(64 * e, 64 * e + 64)
                            mm(psum[pb][es_, 0:128], UG[:, q, 64:128], GM[:, q, 128:256], True, True, [b_UG[q], b_GM[q]], [b_ps[pb]])
                        tt("dve", Qt[:, i * 128:(i + 1) * 128], psum[pb][:, 0:128], BR[:, i, 128:256], ALU.add, [b_ps[pb], b_BR], [b_Qt])
                        pb = bank()
                        for e in range(2):
                            q = 2 * i + e
                            es_ = slice(64 * e, 64 * e + 64)
                            for cc in range(2):
                                ts_ = slice(64 * cc, 64 * cc + 64)
                                mm(psum[pb][es_, cc * 64:cc * 64 + 64], UG[ts_, q, 64:128], tokT[ts_, i, 384 + 64 * e:384 + 64 * e + 64],
                                   True, True, [b_UG[q], b_tok[i]], [b_ps[pb]], rg=cc)
                                mm(psum[pb][es_, 128 + cc * 64:128 + cc * 64 + 64], tokT[ts_, i, 256 + 64 * e:256 + 64 * e + 64],
                                   tokT[ts_, i, 64 * e:64 * e + 64], True, False, [b_tok[i]], [b_ps[pb]], rg=cc)
                                mm(psum[pb][es_, 128 + cc * 64:128 + cc * 64 + 64], tokT[ts_, i, 384 + 64 * e:384 + 64 * e + 64],
                                   UG[ts_, q, 0:64], False, True, [b_tok[i], b_UG[q]], [b_ps[pb]], rg=cc)
                        for e in range(2):
                            es_ = slice(64 * e, 64 * e + 64)
                            cp("act", TT[es_, 2 * i:2 * i + 2, 64 * e:64 * e + 64], psum[pb][es_, 0:128].rearrange("p (c j) -> p c j", j=64),
                               [b_ps[pb]], [b_TT])
                        cp("dve", HL[:, 2 * i:2 * i + 2, :], psum[pb][:, 128:256].rearrange("p (c j) -> p c j", j=64), [b_ps[pb]], [b_HL])
                    if L1S <= 5:
                        return
                    dma("sp", sl32.rearrange("p (e j) -> p e j", j=64)[0:64], st_in[z, 2 * hp:2 * hp + 2].rearrange("e i j -> i e j"),
                        [], [b_sl], d_sl)
                    pb = bank()
                    tr(psum[pb][:, 0:64], sl32[0:64, :], ident32[0:64, 0:64], [b_sl, b_const], [b_ps[pb]])
                    cp("dve", Hs[:, 0, :], psum[pb][:, 0:64], [b_ps[pb]], [b_Hs])
                    cp("pool", Hs16[:, 0, :], Hs[:, 0, :], [b_Hs], [b_H16])
                    order = list(range(16)) if z == 0 else list(range(15, -1, -1))
                    for n, c in enumerate(order):
                        pb = bank()
                        mm(psum[pb][:, 0:64], TT[:, c, :], Hs[:, n, :], True, True, [b_TT, b_Hs], [b_ps[pb]])
                        i, cc = c // 2, c % 2
                        stt("dve", Hs[:, n + 1, :], Hs[:, n, :], gam[:, c:c + 1], psum[pb][:, 0:64], ALU.mult, ALU.add, [b_Hs, b_small, b_ps[pb]], [b_Hs])
                        tt("dve", Hs[:, n + 1, :], Hs[:, n + 1, :], HL[:, c, :], ALU.add, [b_Hs, b_HL], [b_Hs])
                        if (n + 1) % 4 == 0:
                            seg = (n + 1) // 4 - 1
                            if z == 1:
                                seg = 3 - seg
                            pb2 = bank()
                            tr(psum[pb2][0:64, 0:128], Hs[:, n + 1, :], ident32[:, :], [b_Hs, b_const], [b_ps[pb2]])
                            cp("act", sto[0:64, :], psum[pb2][0:64, 0:128], [b_ps[pb2]], [b_sto])
                            dma("sp", sst_out[z, seg, 2 * hp:2 * hp + 2].rearrange("e i j -> i e j"),
                                sto.rearrange("p (e j) -> p e j", j=64)[0:64], [b_sto], [], d_sto)
                            if n + 1 < 16:
                                ts("dve", Hs[:, n + 1, :], Hs[:, n + 1, :], keep_t[:, 0:1], None, ALU.mult, None, [b_Hs, b_const], [b_Hs])
                        if n + 1 < 16:
                            cp("pool", Hs16[:, n + 1, :], Hs[:, n + 1, :], [b_Hs], [b_H16])
                    if L1S <= 6:
                        return
                    for half in range(2):
                        pb = bank()
                        for ii in range(4):
                            i = 4 * half + ii
                            for cc in range(2):
                                c = 2 * i + cc
                                n = order.index(c)
                                ts_ = slice(64 * cc, 64 * cc + 64)
                                col = slice(ii * 128 + cc * 64, ii * 128 + cc * 64 + 64)
                                for e in range(2):
                                    q = 2 * i + e
                                    es_ = slice(64 * e, 64 * e + 64)
                                    mm(psum[pb][es_, col], Hs16[es_, n, :], Qt[es_, i * 128 + cc * 64:i * 128 + cc * 64 + 64], True, False,
                                       [b_H16, b_Qt], [b_ps[pb]], rg=e)
                                    mm(psum[pb][es_, col], tokT[ts_, i, 64 * e:64 * e + 64], GM[ts_, q, cc * 64:cc * 64 + 64], False, False,
                                       [b_tok[i], b_GM[q]], [b_ps[pb]], rg=cc)
                                    mm(psum[pb][es_, col], UG[ts_, q, 0:64], GM[ts_, q, 128 + cc * 64:128 + cc * 64 + 64], False, True,
                                       [b_UG[q], b_GM[q]], [b_ps[pb]], rg=cc)
                        cp("act", y32[:, half * 512:(half + 1) * 512], psum[pb][:, :], [b_ps[pb]], [b_y])
                    for half in range(2):
                        hs = slice(half * 512, (half + 1) * 512)
                        r = sqr[0] % 2
                        sqr[0] += 1
                        cp("pool", sq16[r], y32[:, hs], [b_y], [b_sq[r]])
                        pb = bank()
                        mm(psum[pb][:, :], blk16[:, :], sq16[r], True, True, [b_sq[r], b_c], [b_ps[pb]])
                        stt("dve", d32[:, hs], psum[pb][:, :], -1.0 / 64, y32[:, hs], ALU.mult, ALU.add, [b_ps[pb], b_y], [b_d])
                        r = sqr[0] % 2
                        sqr[0] += 1
                        act(sq16[r], d32[:, hs], AF.Square, [b_d], [b_sq[r]])
                        pb = bank()
                        mm(psum[pb][:, :], blk16[:, :], sq16[r], True, True, [b_sq[r], b_c], [b_ps[pb]])
                        act(tmp32[:, :], psum[pb][:, :], AF.Sqrt, [b_ps[pb], b_const], [b_tmp], bias=eps_t[:, 1:2], scale=1.0 / 64)
                        recip(tmp32[:, :], tmp32[:, :], [b_tmp], [b_tmp])
                        tt("dve", d32[:, hs], d32[:, hs], tmp32[:, :], ALU.mult, [b_d, b_tmp], [b_d])
                        ts("dve", d32[:, hs], d32[:, hs], hpar[:, 3, hp:hp + 1], hpar[:, 4, hp:hp + 1], ALU.mult, ALU.add, [b_d, b_c], [b_d])
                        if z == 0:
                            tt("pool", acc32[:, hs], d32[:, hs], bon32[:, hs], ALU.add, [b_d, b_bon], [b_acc])
                        else:
                            tt("pool", d32[:, hs], d32[:, hs], bon32[:, hs], ALU.add, [b_d, b_bon], [b_d])
                            tt("pool", acc32[:, hs], acc32[:, hs], d32[:, hs], ALU.add, [b_d, b_acc], [b_acc])
                yc, b_yc = ycs[hp % 2]
                tt("dve", yc[:, :], acc32[:, :], g16[:, :], ALU.mult, [b_acc, b_g], [b_yc])
                store_y(hp, yc, b_yc)
        STAGE = int(os.environ.get("KSTAGE", "9"))
        b_x1 = [S.buf("x1_%d" % i) for i in range(NT)]
        modulation3()
        if STAGE >= 2:
            arg = Arena(12800)
            load_row_bcast(arg, gsc_dr[0], [b_gsc], gate_b, b_gateb, "grow0")
            norm_phase(0, x_in, None, Arena(12800 + 2048))
        if STAGE >= 3:
            l0_phase()
        if STAGE >= 4:
            outproj_phase(w0out, x_in, None, x1_dr, b_x1, False, Arena(0), "a")
        if STAGE >= 5:
            load_row_bcast(Arena(30000), gsc_dr[1], [b_gsc], gate_b, b_gateb, "grow1")
            norm_phase(1, x1_dr, b_x1, Arena(20000))
            if not SKIP_L1:
                l1_phase()
            else:
                ar0 = Arena(0)
                zc, b_zc = ar0.bf16(T)
                memset("dve", zc, 0.0, [b_zc])
                for u in range(KC):
                    store_y(u, zc, b_zc)
            if not os.environ.get("KNOFINAL"):
                outproj_phase(w1out, x1_dr, b_x1, y_out, None, True, Arena(0), "b")
        if DEBUG:
            d_d = S.dsem("ddbg")
            dma("sp", dbg["mod"], modT[:], [b_mod], [], d_d)
            dma("sp", dbg["hT"], hT[:].rearrange("p k t -> p (k t)"), b_hT, [], d_d)
        S.emit()
    return nc, S


_CACHE = {}


def _rope_tables(identity):
    cos = np.ones((64, T), np.float32)
    sin = np.zeros((64, T), np.float32)
    if not identity:
        t = np.arange(T)
        row = (t // 64).astype(np.float32)
        col = (t % 64).astype(np.float32)
        freqs = (10000.0 ** (-np.arange(16, dtype=np.float32) / 16)).astype(np.float32)
        for half, pos in ((0, row), (1, col)):
            ang = pos[None, :] * freqs[:, None]
            c, s_ = np.cos(ang), np.sin(ang)
            base = 32 * half
            cos[base:base + 16] = c
            cos[base + 16:base + 32] = c
            sin[base:base + 16] = -s_
            sin[base + 16:base + 32] = s_
    return np.concatenate([cos, cos], 0), np.concatenate([sin, sin], 0)


def _swap_matrix():
    m = np.zeros((128, 128), np.float32)
    for blk in range(4):
        b0 = 32 * blk
        for d in range(16):
            m[b0 + d + 16, b0 + d] = 1.0
            m[b0 + d, b0 + d + 16] = 1.0
    return m


def _chunk_consts():
    idx = np.arange(128)
    same = (idx[:, None] // 64) == (idx[None, :] // 64)
    tri = np.zeros((2, 128, 384), np.float32)
    msk = np.zeros((2, 128, 256), np.float32)
    for z in range(2):
        if z == 0:
            incl = idx[:, None] <= idx[None, :]
            strict = idx[:, None] < idx[None, :]
            after = idx[:, None] > idx[None, :]
        else:
            incl = idx[:, None] >= idx[None, :]
            strict = idx[:, None] > idx[None, :]
            after = idx[:, None] < idx[None, :]
        tri[z, :, 0:128] = incl & same
        tri[z, :, 128:256] = strict & same
        tri[z, :, 256:384] = after & same
        msk[z, :, 0:128] = strict & same
        msk[z, :, 128:256] = incl & same
    blk = same.astype(np.float32)
    lv = np.zeros((4, 128, 128), np.float32)
    lv[0] = (idx[:, None] // 8) == (idx[None, :] // 8)
    for li, b in enumerate((8, 16, 32)):
        lv[1 + li] = ((idx[:, None] // (2 * b)) == (idx[None, :] // (2 * b))) & ((idx[:, None] // b) != (idx[None, :] // b))
    return tri, msk, blk, lv


def _fm(vec, k):
    return np.ascontiguousarray(np.asarray(vec, np.float32).reshape(k, 128).T)


def kernel(x_prompt, x_sample, cache_l0_a_k, cache_l0_a_v, cache_l0_mla_ckv, cache_l0_mla_kpe,
           state_l1_fwd, state_l1_bwd, c, c_ctx, mod_w, mod_b, norm_g, final_norm_g,
           l0_w_in, l0_w_out, l0_diff_lambda, l0_subln_g, l0_q_norm_g, l0_w_uq, l0_kv_norm_g, l0_w_ukv,
           l1_w_in, l1_w_out, l1_mu, l1_w0, l1_w2, l1_a0, l1_a2, l1_k_k, l1_k_a, l1_r_k, l1_ln_w, l1_ln_b):
    f = lambda a: np.ascontiguousarray(np.asarray(a, dtype=np.float32))
    if "nc" not in _CACHE:
        _CACHE["nc"] = build_program()
    nc, S = _CACHE["nc"]
    L0B, L1B = l0_blocks(), l1_blocks()
    shared = {}
    mw = f(mod_w)
    shared["modw"] = np.ascontiguousarray(mw.reshape(2, KC, 128, 24, 256).transpose(0, 3, 2, 1, 4))
    shared["modb"] = f(mod_b).reshape(2, 1, 6144)
    shared["normg"] = np.ascontiguousarray(f(norm_g).reshape(2, KC, 128).transpose(2, 0, 1))
    shared["fng"] = f(final_norm_g).reshape(1, D)
    shared["w0in"] = pack_blocks(f(l0_w_in), L0B)
    shared["w0out"] = np.ascontiguousarray(f(l0_w_out).reshape(KC, 128, 8, 256).transpose(2, 1, 0, 3))
    wuq = f(l0_w_uq)
    uqb = [("n%d" % h, 192 * h, 128) for h in range(8)] + [("p%d" % h, 192 * h + 128, 64) for h in range(8)]
    shared["wuq"] = pack_blocks(wuq, uqb)
    wukv = f(l0_w_ukv)
    kvb = [("k%d" % h, 256 * h, 128) for h in range(8)] + [("v%d" % h, 256 * h + 128, 128) for h in range(8)]
    shared["wukv"] = pack_blocks(wukv, kvb)
    shared["lam"] = f(l0_diff_lambda).reshape(1, 256)
    shared["subg"] = f(l0_subln_g).reshape(128, 1)
    shared["qng"] = _fm(l0_q_norm_g, 4)
    shared["kvng"] = _fm(l0_kv_norm_g, 2)
    shared["rm"] = _swap_matrix()
    shared["ident"] = np.eye(128, dtype=np.float32)
    shared["w1in"] = pack_blocks(f(l1_w_in), L1B)
    shared["w1out"] = np.ascontiguousarray(f(l1_w_out).reshape(KC, 128, 8, 256).transpose(2, 1, 0, 3))
    mu = f(l1_mu)
    mu_t = np.zeros((128, len(L1B), 2), np.float32)
    for i, (_, c0, n) in enumerate(L1B):
        mu_t[:n, i, :] = mu[:, c0:c0 + n].T
    shared["mu"] = mu_t
    shared["w0row"] = f(l1_w0).reshape(2, 1, 2048)
    shared["w2"] = f(l1_w2)
    shared["a2"] = f(l1_a2)
    shared["a0"] = np.ascontiguousarray(f(l1_a0).reshape(2, 16, 128).transpose(2, 0, 1))
    hp = np.stack([f(l1_k_k), f(l1_k_a), f(l1_r_k), f(l1_ln_w), f(l1_ln_b)], 0)
    shared["hpar"] = np.ascontiguousarray(hp.reshape(5, 16, 128).transpose(2, 0, 1))
    tri, msk, blk, lv = _chunk_consts()
    shared["tri"], shared["msk"], shared["blk64"], shared["lvl"] = tri, msk, blk, lv
    cos_id, sin_id = _rope_tables(True)
    cos_r, sin_r = _rope_tables(False)

    xp, xs = f(x_prompt), f(x_sample)
    ck, cv = f(cache_l0_a_k), f(cache_l0_a_v)
    cckv, ckpe = f(cache_l0_mla_ckv), f(cache_l0_mla_kpe)
    sf, sb_ = f(state_l1_fwd), f(state_l1_bwd)
    cc, cctx = f(c), f(c_ctx)
    in_maps = []
    for core in range(8):
        m = dict(shared)
        prompt = core < 4
        maskk = np.zeros((8, 1536), np.float32)
        maskq = np.zeros((8, T), np.float32)
        if prompt:
            m["x"] = np.ascontiguousarray(xp[4 * core:4 * core + 4].reshape(T, D))
            m["cond"] = _fm(cctx, KC)
            m["cos"], m["sin"] = cos_id, sin_id
            for j in range(4):
                maskk[j, 256 * j:256 * (j + 1)] = 1.0
                maskq[j, :] = NEG
                maskq[j, 256 * j:256 * (j + 1)] = 0.0
            maskk[4, 1024:] = 1.0
            maskq[4, :] = NEG
            m["ck"] = np.zeros((512, 1024), np.float32)
            m["cv"] = np.zeros((512, 1024), np.float32)
            m["cckv"] = np.zeros((512, 256), np.float32)
            m["ckpe"] = np.zeros((512, 64), np.float32)
            m["st"] = np.zeros((2, 32, 64, 64), np.float32)
            m["keep"] = np.tile(np.array([[0.0, 1.0]], np.float32), (128, 1))
        else:
            b = core - 4
            m["x"] = np.ascontiguousarray(xs[b])
            m["cond"] = _fm(cc[b], KC)
            m["cos"], m["sin"] = cos_r, sin_r
            maskk[0, :] = 1.0
            m["ck"] = np.ascontiguousarray(ck[b].reshape(512, 1024))
            m["cv"] = np.ascontiguousarray(cv[b].reshape(512, 1024))
            m["cckv"] = np.ascontiguousarray(cckv[b])
            m["ckpe"] = np.ascontiguousarray(ckpe[b])
            m["st"] = np.ascontiguousarray(np.stack([sf[b], sb_[b]], 0))
            m["keep"] = np.tile(np.array([[1.0, 0.0]], np.float32), (128, 1))
        m["maskk"], m["maskq"] = maskk, maskq
        in_maps.append(m)
    if FAKE:
        for m in in_maps:
            m["modw"] = m["modw"][:, 0:1]
            m["w0in"] = m["w0in"][0:4]
            m["w1in"] = m["w1in"][0:4]
            m["w0out"] = m["w0out"][0:1]
            m["w1out"] = m["w1out"][0:1]
    res = run_bass_kernel_spmd(nc, in_maps, core_ids=list(range(8)))
    R = res.results
    _CACHE["last"] = R
    y_prompt = np.stack([R[cidx]["y"].reshape(4, 256, D) for cidx in range(4)], 0).reshape(16, 256, D)
    y_sample = np.stack([R[4 + b]["y"] for b in range(4)], 0)
    nk = np.concatenate([R[cidx]["nk"].reshape(4, 256, 8, 2, 64) for cidx in range(4)], 0)
    nv = np.concatenate([R[cidx]["nv"].reshape(4, 256, 8, 128) for cidx in range(4)], 0)
    nckv = np.concatenate([R[cidx]["nckv"].reshape(4, 256, 256) for cidx in range(4)], 0)
    nkpe = np.concatenate([R[cidx]["nkpe"].reshape(4, 256, 64) for cidx in range(4)], 0)
    sfo = np.concatenate([R[cidx]["sst"][0] for cidx in range(4)], 0)
    sbo = np.concatenate([R[cidx]["sst"][1] for cidx in range(4)], 0)
    out = (y_prompt, y_sample, nk, nv, nckv, nkpe, sfo, sbo)
    return tuple(np.ascontiguousarray(o.astype(np.float32)) for o in out)
```

```python
import math
import os
from contextlib import ExitStack

import numpy as np
import concourse.bass as bass
import concourse.mybir as mybir
from concourse.bass_utils import run_bass_kernel_spmd

F32 = mybir.dt.float32
BF16 = mybir.dt.bfloat16
AF = mybir.ActivationFunctionType
ALU = mybir.AluOpType
AX = mybir.AxisListType

ENGS = ("pe", "act", "dve", "pool", "sp")
SAME_ENG_SYNC = {"act", "dve", "pool"}

T = 1024
D = 2048
NT = 8
KC = 16
NEG = -30000.0


class Buf:
    __slots__ = ("name", "last_w", "readers", "excl")

    def __init__(self, name):
        self.name = name
        self.last_w = None
        self.readers = []
        self.excl = False


class DSem:
    __slots__ = ("h", "count", "max_wait", "name", "twin")

    def __init__(self, name):
        self.name = name
        self.h = None
        self.count = 0
        self.max_wait = 0
        self.twin = None


class Op:
    __slots__ = ("eng", "fn", "deps", "signal", "dma", "dsem", "dval", "val", "waits", "pre")

    def __init__(self, eng, fn):
        self.eng = eng
        self.fn = fn
        self.deps = []
        self.signal = False
        self.dma = False
        self.dsem = None
        self.dval = 0
        self.val = 0
        self.waits = []
        self.pre = []


class Sched:
    def __init__(self, nc):
        self.nc = nc
        self.ops = []
        self.dsems = []
        self.last = {e: None for e in ENGS}
        self._force = False

    def buf(self, name):
        return Buf(name)

    def dsem(self, name):
        d = DSem(name)
        self.dsems.append(d)
        return d

    def _dep_on(self, op, d):
        if d is None or d is op:
            return
        if d.dma:
            v = d.dsem.count
            d.dsem.max_wait = max(d.dsem.max_wait, v)
            op.pre.append((d.dsem, v))
        else:
            if d.eng == op.eng and not op.dma and d.eng not in SAME_ENG_SYNC and not self._force:
                return
            op.deps.append(d)

    @staticmethod
    def _flat(xs):
        out = []
        for x in xs:
            if isinstance(x, (list, tuple)):
                out.extend(Sched._flat(x))
            else:
                out.append(x)
        return out

    def add(self, eng, fn, reads=(), writes=(), dma=False, dsem=None, force=False):
        reads = self._flat(reads)
        writes = self._flat(writes)
        xr = [b for b in reads if b.excl]
        if xr:
            reads = [b for b in reads if not b.excl]
            writes = writes + [b for b in xr if b not in writes]
        op = Op(eng, fn)
        op.dma = dma
        self._force = force
        seen = set()
        for b in reads:
            if b.last_w is not None and id(b.last_w) not in seen:
                seen.add(id(b.last_w))
                self._dep_on(op, b.last_w)
        for b in writes:
            if b.last_w is not None and id(b.last_w) not in seen:
                seen.add(id(b.last_w))
                self._dep_on(op, b.last_w)
            for r in b.readers:
                if id(r) not in seen:
                    seen.add(id(r))
                    self._dep_on(op, r)
        if dma:
            if eng == "pool":
                if dsem.twin is None:
                    dsem.twin = self.dsem(dsem.name + "_sw")
                dsem = dsem.twin
            op.dsem = dsem
            if dsem.max_wait > 0:
                op.pre.append((dsem, dsem.max_wait))
            dsem.count += 16
            op.dval = dsem.count
        for b in writes:
            b.last_w = op
            b.readers = []
        for b in reads:
            if b.last_w is not op:
                b.readers.append(op)
        self.ops.append(op)
        if not dma:
            self.last[eng] = op
        return op

    def fence(self):
        lasts = dict(self.last)
        for e in ENGS:
            op = Op(e, lambda eng: eng.nop())
            for e2 in ENGS:
                if e2 != e and lasts[e2] is not None:
                    op.deps.append(lasts[e2])
            for ds in self.dsems:
                if ds.count > 0:
                    ds.max_wait = max(ds.max_wait, ds.count)
                    op.pre.append((ds, ds.count))
            self.ops.append(op)
            self.last[e] = op

    def emit(self):
        nc = self.nc
        seqc = {e: 0 for e in ENGS}
        for op in self.ops:
            if not op.dma:
                seqc[op.eng] += 1
                op.val = seqc[op.eng]
        known = {e: {} for e in ENGS}
        snap = {}
        per_eng = {e: [] for e in ENGS}
        for op in self.ops:
            k = known[op.eng]
            need_e = {}
            for d in op.deps:
                if need_e.get(d.eng, (0, None))[0] < d.val:
                    need_e[d.eng] = (d.val, d)
            need_d = {}
            for (ds, v) in op.pre:
                if id(ds) not in need_d or need_d[id(ds)][1] < v:
                    need_d[id(ds)] = (ds, v)
            fin = []
            for e2, (sq, d) in sorted(need_e.items(), key=lambda kv: -kv[1][0]):
                if k.get(("E", e2), 0) >= sq:
                    continue
                fin.append(d)
                d.signal = True
                k[("E", e2)] = sq
                for key, v in snap.get(id(d), {}).items():
                    if k.get(key, 0) < v:
                        k[key] = v
            for key, (ds, v) in need_d.items():
                if k.get(("D", key), 0) >= v:
                    continue
                fin.append((ds, v))
                k[("D", key)] = v
            op.waits = fin
            if not op.dma:
                sn = dict(k)
                sn[("E", op.eng)] = op.val
                snap[id(op)] = sn
            per_eng[op.eng].append(op)
        snap.clear()
        cnt = {e: 0 for e in ENGS}
        for op in self.ops:
            if not op.dma:
                if op.signal:
                    cnt[op.eng] += 1
                    op.val = cnt[op.eng]
                else:
                    op.val = -1
        self.nsig = dict(cnt)
        with ExitStack() as es:
            esem = {e: es.enter_context(nc.semaphore("es_" + e)) for e in ENGS}
            for i, ds in enumerate(self.dsems):
                if ds.count > 0:
                    ds.h = es.enter_context(nc.semaphore("ds%d" % i))
            block = es.enter_context(nc.Block())

            def run(eng_name):
                def body(eng):
                    for op in per_eng[eng_name]:
                        for w in op.waits:
                            if isinstance(w, Op):
                                eng.wait_ge(esem[w.eng], w.val)
                            else:
                                eng.wait_ge(w[0].h, w[1])
                        ins = op.fn(eng)
                        if op.dma:
                            ins.then_inc(op.dsem.h, 16)
                        elif op.signal:
                            ins.then_inc(esem[eng_name], 1)
                    if eng_name == "sp":
                        for ds in self.dsems:
                            if ds.count > 0:
                                eng.wait_ge(ds.h, ds.count)
                        for e in ENGS:
                            if e != "sp" and cnt[e] > 0:
                                eng.wait_ge(esem[e], cnt[e])
                return body

            block.tensor(run("pe"))
            block.scalar(run("act"))
            block.vector(run("dve"))
            block.gpsimd(run("pool"))
            block.sync(run("sp"))
        self.stats = {e: len(per_eng[e]) for e in ENGS}


A_HEADS = 8
B_HEADS = 8
B_Q_LORA = 512
B_KV_LORA = 256
L0_IN = 5952
L1_IN = 8576
C_HEADS = 32
LORA = 96
NORM_EPS = 1e-6
GN_EPS = 64e-5
LAM_INIT0 = 0.8 - 0.6 * math.exp(-0.3 * 0)

DEBUG = bool(int(os.environ.get("KDEBUG", "0")))
LAYOUT = {}
SKIP_L1 = bool(int(os.environ.get("KSKIP_L1", "0")))
FAKE = bool(int(os.environ.get("KFAKE", "0")))


def l0_blocks():
    blks = []
    for i in range(4):
        blks.append(("cq%d" % i, 3072 + 128 * i, 128))
    for i in range(2):
        blks.append(("ckv%d" % i, 3584 + 128 * i, 128))
    blks.append(("kpe", 3840, 64))
    for h in range(8):
        blks.append(("q%d" % h, 128 * h, 128))
        blks.append(("k%d" % h, 1024 + 128 * h, 128))
        blks.append(("v%d" % h, 2048 + 128 * h, 128))
        blks.append(("g%d" % h, 3904 + 128 * h, 128))
    for h in range(8):
        blks.append(("g%d" % (8 + h), 3904 + 128 * (8 + h), 128))
    return blks


def l1_blocks():
    blks = []
    base = 4 * 2048
    for z in range(2):
        blks.append(("wd%d" % z, base + LORA * z, LORA))
    for z in range(2):
        blks.append(("ad%d" % z, base + 2 * LORA + LORA * z, LORA))
    for hp in range(16):
        blks.append(("r%d" % hp, 128 * hp, 128))
        blks.append(("k%d" % hp, 2048 + 128 * hp, 128))
        blks.append(("v%d" % hp, 4096 + 128 * hp, 128))
        blks.append(("g%d" % hp, 6144 + 128 * hp, 128))
    return blks


def pack_blocks(w, blks):
    kk = w.shape[0] // 128
    out = np.zeros((len(blks), 128, kk, 128), np.float32)
    w3 = w.reshape(kk, 128, w.shape[1])
    for i, (_, c0, n) in enumerate(blks):
        out[i, :, :, :n] = w3[:, :, c0:c0 + n].transpose(1, 0, 2)
    return out


def build_program():
    nc = bass.Bass("TRN2", target_bir_lowering=False)
    S = Sched(nc)
    L0B = l0_blocks()
    L1B = l1_blocks()

    def din(name, shape, dt=F32):
        return nc.dram_tensor(name, list(shape), dt, kind="ExternalInput").ap()

    def dout(name, shape, dt=F32):
        return nc.dram_tensor(name, list(shape), dt, kind="ExternalOutput").ap()

    x_in = din("x", [T, D])
    cond_in = din("cond", [128, KC])
    NMOD = 1 if FAKE else 24
    NB0 = 4 if FAKE else len(L0B)
    NB1 = 4 if FAKE else len(L1B)
    NOUT = 1 if FAKE else 8
    modw_in = din("modw", [2, NMOD, 128, KC, 256])
    modb_in = din("modb", [2, 1, 6144])
    normg_in = din("normg", [128, 2, KC])
    fng_in = din("fng", [1, D])
    w0in = din("w0in", [NB0, 128, KC, 128])
    w0out = din("w0out", [NOUT, 128, KC, 256])
    wuq_in = din("wuq", [16, 128, 4, 128])
    wukv_in = din("wukv", [16, 128, 2, 128])
    lam_in = din("lam", [1, 256])
    subg_in = din("subg", [128, 1])
    qng_in = din("qng", [128, 4])
    kvng_in = din("kvng", [128, 2])
    cos_in = din("cos", [128, T])
    sin_in = din("sin", [128, T])
    rm_in = din("rm", [128, 128])
    ident_in = din("ident", [128, 128])
    maskk_in = din("maskk", [8, 1536])
    maskq_in = din("maskq", [8, T])
    ck_in = din("ck", [512, 1024])
    cv_in = din("cv", [512, 1024])
    cckv_in = din("cckv", [512, 256])
    ckpe_in = din("ckpe", [512, 64])
    w1in = din("w1in", [NB1, 128, KC, 128])
    w1out = din("w1out", [NOUT, 128, KC, 256])
    mu_in = din("mu", [128, len(L1B), 2])
    w0row_in = din("w0row", [2, 1, 2048])
    w2_in = din("w2", [2, LORA, 2048])
    a2_in = din("a2", [2, LORA, 2048])
    a0_in = din("a0", [128, 2, 16])
    hpar_in = din("hpar", [128, 5, 16])
    st_in = din("st", [2, 32, 64, 64])
    keep_in = din("keep", [128, 2])
    tri_in = din("tri", [2, 128, 384])
    msk_in = din("msk", [2, 128, 256])
    blk64_in = din("blk64", [128, 128])
    lvl_in = din("lvl", [4, 128, 128])
    y_out = dout("y", [T, D])
    nk_out = dout("nk", [T, 1024])
    nv_out = dout("nv", [T, 1024])
    nckv_out = dout("nckv", [T, 256])
    nkpe_out = dout("nkpe", [T, 64])
    sst_out = dout("sst", [2, 4, 32, 64, 64])
    x1_dr = nc.dram_tensor("x1s", [T, D], F32, kind="Internal").ap()
    gsc_dr = nc.dram_tensor("gsc", [2, 1, D], F32, kind="Internal").ap()
    yT_dr = nc.dram_tensor("yTs", [KC, 128, T], BF16, kind="Internal").ap()
    dbg = {}
    if DEBUG:
        dbg["hT"] = dout("dbg_hT", [128, KC * T], BF16)
        dbg["mod"] = dout("dbg_mod", [128, 96])

    with ExitStack() as es:
        def sbt(name, shape, dt):
            return es.enter_context(nc.sbuf_tensor("s_" + name, list(shape), dt))

        hT = sbt("hT", [128, KC, T], BF16)
        b_hT = [S.buf("hT%d" % i) for i in range(NT)]
        b_yT = [S.buf("yTdr%d" % u) for u in range(KC)]
        d_yT = [S.dsem("dyT%d" % u) for u in range(KC)]
        NSLOT = 4
        wring = sbt("wring", [128, NSLOT * KC * 128], BF16)
        b_w = [S.buf("w%d" % i) for i in range(NSLOT)]
        d_w = [S.dsem("dw%d" % i) for i in range(NSLOT)]
        ident32 = sbt("ident32", [128, 128], F32)
        ident16 = sbt("ident16", [128, 128], BF16)
        ones16 = sbt("ones16", [128, 128], BF16)
        ones32 = sbt("ones32", [128, 128], F32)
        modT = sbt("modT", [128, 96], F32)
        normg = sbt("normg", [128, 2, KC], F32)
        modA = sbt("modA", [128, 2, KC], F32)
        gate_b = sbt("gate_b", [128, D], F32)
        eps_t = sbt("eps_t", [128, 4], F32)
        keep_t = sbt("keep_t", [128, 2], F32)
        b_const = S.buf("const")
        b_mod = S.buf("mod")
        b_gateb = S.buf("gateb")
        b_gsc = S.buf("gsc")
        d_c = S.dsem("dconst")
        ARENA_W = 38144
        arena = sbt("arena", [128, ARENA_W], F32)
        psum = [es.enter_context(nc.psum_tensor("ps%d" % i, [128, 512], F32)) for i in range(8)]
        b_ps = [S.buf("ps%d" % i) for i in range(8)]
        for b_ in b_ps:
            b_.excl = True

        def dma(eng, out, in_, reads, writes, dsem):
            S.add(eng, lambda e: e.dma_start(out=out, in_=in_), reads=reads, writes=writes, dma=True, dsem=dsem)

        last_rg = {}

        def mm(out, lhsT, rhs, start, stop, reads, writes, rg=None):
            force = False
            for w in Sched._flat(writes):
                if w.excl:
                    if last_rg.get(id(w), rg) != rg:
                        force = True
                    last_rg[id(w)] = rg
            S.add("pe", lambda e: e.matmul(out, lhsT=lhsT, rhs=rhs, start=start, stop=stop), reads=reads, writes=writes, force=force)

        def tr(out, in_, idn, reads, writes):
            S.add("pe", lambda e: e.transpose(out, in_, idn), reads=reads, writes=writes)

        def act(out, in_, func, reads, writes, bias=None, scale=None, accum=None):
            kw = {}
            if bias is not None:
                kw["bias"] = bias
            if scale is not None:
                kw["scale"] = scale
            if accum is not None:
                kw["accum_out"] = accum
            S.add("act", lambda e: e.activation(out=out, in_=in_, func=func, **kw), reads=reads, writes=writes)

        def tt(eng, out, in0, in1, op, reads, writes):
            S.add(eng, lambda e: e.tensor_tensor(out=out, in0=in0, in1=in1, op=op), reads=reads, writes=writes)

        def ts(eng, out, in0, s1, s2, op0, op1, reads, writes):
            if s2 is None:
                S.add(eng, lambda e: e.tensor_scalar(out=out, in0=in0, scalar1=s1, scalar2=None, op0=op0), reads=reads, writes=writes)
            else:
                S.add(eng, lambda e: e.tensor_scalar(out=out, in0=in0, scalar1=s1, scalar2=s2, op0=op0, op1=op1), reads=reads, writes=writes)

        def stt(eng, out, in0, scalar, in1, op0, op1, reads, writes):
            S.add(eng, lambda e: e.scalar_tensor_tensor(out=out, in0=in0, scalar=scalar, in1=in1, op0=op0, op1=op1), reads=reads, writes=writes)

        def cp(eng, out, in_, reads, writes):
            if eng == "act":
                S.add("act", lambda e: e.copy(out=out, in_=in_), reads=reads, writes=writes)
            else:
                S.add(eng, lambda e: e.tensor_copy(out=out, in_=in_), reads=reads, writes=writes)

        def recip(out, in_, reads, writes):
            S.add("dve", lambda e: e.reciprocal(out=out, in_=in_), reads=reads, writes=writes)

        def memset(eng, ap, val, writes):
            S.add(eng, lambda e: e.memset(ap, val), writes=writes)

        bank_rr = [0]

        def bank(lo=4, hi=8):
            n = hi - lo
            i = lo + (bank_rr[0] % n)
            bank_rr[0] += 1
            return i

        wslot = [0]

        def wload(src_ap, nwords):
            i = wslot[0] % NSLOT
            wslot[0] += 1
            dst = wring[:, i * KC * 128: i * KC * 128 + nwords]
            dma("pool", dst, src_ap, [], [b_w[i]], d_w[i])
            return i

        def wview(i, kk, n=128):
            return wring[:, i * KC * 128: i * KC * 128 + kk * n].rearrange("p (k c) -> p k c", c=n)

        REG = 256
        region_bufs = [S.buf("ar%d" % i) for i in range((ARENA_W + REG - 1) // REG)]

        class Arena:
            def __init__(self, off=0):
                self.off = off
                self.peak = off
                self.log = []

            def _take(self, w):
                w = ((w + REG - 1) // REG) * REG
                o = self.off
                self.off += w
                self.peak = max(self.peak, self.off)
                self.log.append(o)
                assert self.off <= ARENA_W or os.environ.get("KNOASSERT"), ("arena overflow", self.off)
                return o, region_bufs[o // REG:(o + w) // REG]

            def f32(self, n):
                o, bufs = self._take(n)
                return arena[:, o:o + n], bufs

            def bf16(self, n):
                w = (n + 1) // 2
                o, bufs = self._take(w)
                return arena[:, o:o + w].bitcast(BF16)[:, 0:n], bufs

        dma("sp", ident32[:], ident_in, [], [b_const], d_c)
        dma("pool", ident16[:], ident_in, [], [b_const], d_c)
        dma("sp", normg[:], normg_in, [], [b_const], d_c)
        dma("sp", keep_t[:], keep_in, [], [b_const], d_c)
        memset("dve", ones16[:], 1.0, [b_const])
        memset("dve", ones32[:], 1.0, [b_const])
        memset("dve", eps_t[:, 0:1], NORM_EPS, [b_const])
        memset("dve", eps_t[:, 1:2], GN_EPS, [b_const])
        memset("dve", eps_t[:, 2:3], 1e-12, [b_const])
        memset("dve", eps_t[:, 3:4], 0.0, [b_const])

        def wload3(src_ap3, kk, n=128):
            i = wslot[0] % NSLOT
            wslot[0] += 1
            dst = wview(i, kk, n)
            dma("pool", dst, src_ap3, [], [b_w[i]], d_w[i])
            return i

        def bcast_row(dst_tile, row_ap, reads, wbuf):
            for q in range(4):
                pb = bank()
                mm(psum[pb][:, :], ones32[0:1, :], row_ap[0:1, q * 512:(q + 1) * 512], True, True,
                   reads + [b_const], [b_ps[pb]])
                cp("act", dst_tile[:, q * 512:(q + 1) * 512], psum[pb][:, :], [b_ps[pb]], [wbuf])

        def modulation3():
            ar = Arena()
            cond_t, b_cd = ar.f32(KC)
            scond, b_sc = ar.bf16(KC)
            mrow, b_mr = ar.f32(6144)
            modb_t, b_mb = ar.f32(6144)
            d_l = S.dsem("dmod")
            d_g = S.dsem("dgsc")
            dma("sp", cond_t, cond_in, [], [b_cd], d_l)
            act(scond, cond_t, AF.Silu, [b_cd], [b_sc])
            for l in range(2):
                dma("sp", modb_t[0:1, :], modb_in[l], [], [b_mb], d_l)
                for blk in range(24):
                    pb = bank()
                    for half in range(2):
                        si = wload3(modw_in[l, blk % NMOD, :, :, half * 128:(half + 1) * 128], KC)
                        wv = wview(si, KC)
                        for k in range(KC):
                            mm(psum[pb][0:1, half * 128:(half + 1) * 128], scond[:, k:k + 1], wv[:, k, :],
                               k == 0, k == KC - 1, [b_w[si], b_sc], [b_ps[pb]])
                    tt("dve", mrow[0:1, blk * 256:(blk + 1) * 256], psum[pb][0:1, 0:256],
                       modb_t[0:1, blk * 256:(blk + 1) * 256], ALU.add, [b_ps[pb], b_mb], [b_mr])
                pb = bank()
                for c in range(48):
                    mm(psum[pb][:, c:c + 1], mrow[0:1, c * 128:(c + 1) * 128], ones32[0:1, 0:1], True, True,
                       [b_mr, b_const], [b_ps[pb]])
                cp("dve", modT[:, 48 * l:48 * l + 48], psum[pb][:, 0:48], [b_ps[pb]], [b_mod])
                ts("dve", modA[:, l, :], modT[:, 48 * l + 16:48 * l + 32], 1.0, None, ALU.add, None, [b_mod], [b_mod])
                tt("dve", modA[:, l, :], modA[:, l, :], normg[:, l, :], ALU.mult, [b_mod, b_const], [b_mod])
                dma("sp", gsc_dr[l], mrow[0:1, 2 * D:3 * D], [b_mr], [b_gsc], d_g)

        def load_row_bcast(ar, src_dram_row, src_bufs, dst_tile, dst_buf, name):
            grow, b_g = ar.f32(D)
            d_gr = S.dsem("d" + name)
            dma("sp", grow[0:1, :], src_dram_row, src_bufs, [b_g], d_gr)
            bcast_row(dst_tile, grow, [b_g], dst_buf)

        def norm_phase(l, x_src, x_src_bufs, ar):
            xt = [ar.f32(D), ar.f32(D)]
            xn = [ar.bf16(D), ar.bf16(D)]
            junk, b_j = ar.bf16(D)
            st, b_st = ar.f32(4 * NT)
            d_x = [S.dsem("dnx%d_%d" % (l, i)) for i in range(2)]
            for i in range(NT):
                s = i % 2
                xs, b_x = xt[s]
                xb, b_xn = xn[s]
                dma("sp", xs, x_src[i * 128:(i + 1) * 128, :], [x_src_bufs[i]] if x_src_bufs else [], [b_x], d_x[s])
                act(junk, xs, AF.Square, [b_x], [b_j, b_st], accum=st[:, 4 * i:4 * i + 1])
                act(st[:, 4 * i + 1:4 * i + 2], st[:, 4 * i:4 * i + 1], AF.Sqrt, [b_st, b_const], [b_st],
                    bias=eps_t[:, 0:1], scale=1.0 / D)
                recip(st[:, 4 * i + 2:4 * i + 3], st[:, 4 * i + 1:4 * i + 2], [b_st], [b_st])
                ts("dve", xb, xs, st[:, 4 * i + 2:4 * i + 3], None, ALU.mult, None, [b_x, b_st], [b_xn])
                for g in range(4):
                    pb = bank()
                    pv = psum[pb][:, :].bitcast(BF16)
                    for j in range(4):
                        k = 4 * g + j
                        tr(pv[:, j * 128:(j + 1) * 128], xb[:, k * 128:(k + 1) * 128], ident16[:, :],
                           [b_xn, b_const], [b_ps[pb]])
                    for j in range(4):
                        k = 4 * g + j
                        act(hT[:, k, i * 128:(i + 1) * 128], pv[:, j * 128:(j + 1) * 128], AF.Identity,
                            [b_ps[pb], b_mod], [b_hT[i]], bias=modT[:, 48 * l + k:48 * l + k + 1],
                            scale=modA[:, l, k:k + 1])

        def outproj_phase(wout_in, x_src, x_src_bufs, x_dst, x_dst_bufs, final, ar, tag):
            yTa, b_ya = ar.bf16(KC * T)
            yT = yTa.rearrange("p (u t) -> p u t", t=T)
            d_ya = S.dsem("dya" + tag)
            for u in range(KC):
                dma("sp", yT[:, u, :], yT_dr[u], [b_yT[u]], [b_ya], d_ya)
            fng_b = None
            if final:
                fng_b, b_fng = ar.f32(D)
                load_row_bcast(ar, fng_in, [], fng_b, b_fng, "fng")
            xt = [ar.f32(D) for _ in range(4)]
            tmp, b_tmp = ar.f32(256)
            junk, b_j = ar.bf16(D)
            st, b_st = ar.f32(4 * NT)
            d_x = [S.dsem("dox%s%d" % (tag, i)) for i in range(4)]
            d_o = [S.dsem("doo%s%d" % (tag, i)) for i in range(4)]
            for grp in range(2):
                tiles = list(range(4 * grp, 4 * grp + 4))
                for fb in range(8):
                    sl = [wload3(wout_in[fb % NOUT, :, :, half * 128:(half + 1) * 128], KC) for half in range(2)]
                    for i in tiles:
                        s = i % 4
                        xs, b_x = xt[s]
                        if fb == 0:
                            dma("sp", xs, x_src[i * 128:(i + 1) * 128, :], [x_src_bufs[i]] if x_src_bufs else [], [b_x], d_x[s])
                        pb = bank()
                        for half in range(2):
                            wv = wview(sl[half], KC)
                            for u in range(KC):
                                mm(psum[pb][:, half * 128:(half + 1) * 128], yT[:, u, i * 128:(i + 1) * 128], wv[:, u, :],
                                   u == 0, u == KC - 1, [b_w[sl[half]], b_ya], [b_ps[pb]])
                        tt("dve", tmp[:, 0:256], psum[pb][:, 0:256], gate_b[:, fb * 256:(fb + 1) * 256], ALU.mult,
                           [b_ps[pb], b_gateb], [b_tmp])
                        tt("pool", xs[:, fb * 256:(fb + 1) * 256], xs[:, fb * 256:(fb + 1) * 256], tmp[:, 0:256],
                           ALU.add, [b_tmp, b_x], [b_x])
                        if fb == 7:
                            if final:
                                act(junk, xs, AF.Square, [b_x], [b_j, b_st], accum=st[:, 4 * i:4 * i + 1])
                                act(st[:, 4 * i + 1:4 * i + 2], st[:, 4 * i:4 * i + 1], AF.Sqrt, [b_st, b_const], [b_st],
                                    bias=eps_t[:, 0:1], scale=1.0 / D)
                                recip(st[:, 4 * i + 2:4 * i + 3], st[:, 4 * i + 1:4 * i + 2], [b_st], [b_st])
                                stt("dve", xs, xs, st[:, 4 * i + 2:4 * i + 3], fng_b[:, :], ALU.mult, ALU.mult,
                                    [b_x, b_st, b_fng], [b_x])
                            dma("sp", x_dst[i * 128:(i + 1) * 128, :], xs, [b_x], [x_dst_bufs[i]] if x_dst_bufs else [], d_o[s])

        def store_y(u, yc, b_yc):
            dma("sp", yT_dr[u], yc, [b_yc], [b_yT[u]], d_yT[u])

        def l0_phase():
            ar = Arena()
            cos_t, b_cos = ar.f32(T)
            sin_t, b_sin = ar.f32(T)
            rm_t, b_rm = ar.f32(128)
            mk16, b_mk = ar.bf16(1536)
            mq16, b_mq = ar.bf16(T)
            b_tab = [b_cos, b_sin, b_rm, b_mk, b_mq]
            sm, b_sm = ar.f32(16)
            lamrow, b_lr = ar.f32(272)
            cqg16a, b_cq = ar.bf16(4 * T)
            cqg16 = cqg16a.rearrange("p (k t) -> p k t", t=T)
            rstdq, b_rq = ar.f32(T)
            rkv, b_rkv = ar.f32(T)
            ckvn16a, b_ckv = ar.bf16(2 * 1536)
            ckvn16 = ckvn16a.rearrange("p (k t) -> p k t", t=1536)
            kpe16, b_kpe = ar.bf16(1536)
            qpe16, b_qpe = ar.bf16(T)
            stAa, b_stA = ar.f32(2 * T)
            stA = stAa.rearrange("p (k t) -> p k t", t=T)
            b_st = [b_stA[0:4], b_stA[4:8]]
            sqs = [ar.bf16(512), ar.bf16(512)]
            sq16 = [sqs[0][0], sqs[1][0]]
            b_sq = [sqs[0][1], sqs[1][1]]
            tmpa, b_ta = ar.f32(512)
            tmpb, b_tb = ar.f32(512)
            ctxs, b_ctx = ar.f32(1024)
            ost, b_ost = ar.f32(512)
            q16, b_q = ar.bf16(T)
            K16, b_K = ar.bf16(1536)
            V16a, b_V = ar.bf16(12 * 128)
            V16 = V16a.rearrange("p (k c) -> p k c", c=128)
            vst, b_vst = ar.f32(512)
            gT16, b_g = ar.bf16(T)
            pts = [ar.bf16(512) for _ in range(4)]
            PT = [p[0] for p in pts]
            b_PT = [p[1] for p in pts]
            os_ = [ar.f32(512) for _ in range(3)]
            o32 = [p[0] for p in os_]
            b_o = [p[1] for p in os_]
            ycs = [ar.bf16(T), ar.bf16(T)]
            d_tab = S.dsem("dl0tab")
            d_ctx = S.dsem("dctx")
            d_ost = S.dsem("dost")
            d_V = S.dsem("dV16")
            d_vst = S.dsem("dvst")
            bi_of = {name: i for i, (name, _, _) in enumerate(L0B)}
            sqr = [0]
            ptr = [0]

            dma("sp", cos_t, cos_in, [], [b_tab], d_tab)
            dma("sp", sin_t, sin_in, [], [b_tab], d_tab)
            dma("sp", rm_t, rm_in, [], [b_tab], d_tab)
            dma("pool", mk16[0:8, :], maskk_in, [], [b_tab], d_tab)
            dma("pool", mq16[0:8, :], maskq_in, [], [b_tab], d_tab)
            dma("pool", kpe16[64:72, :], maskk_in, [], [b_kpe], d_tab)
            dma("pool", qpe16[64:72, :], maskq_in, [], [b_qpe], d_tab)
            dma("sp", sm[:, 1:2], subg_in, [], [b_sm], d_tab)
            dma("sp", sm[:, 2:6], qng_in, [], [b_sm], d_tab)
            dma("sp", sm[:, 6:8], kvng_in, [], [b_sm], d_tab)
            dma("sp", lamrow[0:1, 0:256], lam_in, [], [b_lr], d_tab)
            tt("dve", lamrow[0:1, 0:64], lamrow[0:1, 0:64], lamrow[0:1, 64:128], ALU.mult, [b_lr], [b_lr])
            tt("dve", lamrow[0:1, 128:192], lamrow[0:1, 128:192], lamrow[0:1, 192:256], ALU.mult, [b_lr], [b_lr])
            S.add("dve", lambda e: e.reduce_sum(out=lamrow[0:1, 256:257], in_=lamrow[0:1, 0:64], axis=AX.X), reads=[b_lr], writes=[b_lr])
            S.add("dve", lambda e: e.reduce_sum(out=lamrow[0:1, 257:258], in_=lamrow[0:1, 128:192], axis=AX.X), reads=[b_lr], writes=[b_lr])
            act(lamrow[0:1, 258:260], lamrow[0:1, 256:258], AF.Exp, [b_lr], [b_lr])
            tt("dve", lamrow[0:1, 260:261], lamrow[0:1, 259:260], lamrow[0:1, 258:259], ALU.subtract, [b_lr], [b_lr])
            ts("dve", lamrow[0:1, 261:262], lamrow[0:1, 260:261], -LAM_INIT0, None, ALU.add, None, [b_lr], [b_lr])
            pb = bank()
            mm(psum[pb][:, 0:1], ones32[0:1, :], lamrow[0:1, 261:262], True, True, [b_lr, b_const], [b_ps[pb]])
            cp("dve", sm[:, 0:1], psum[pb][:, 0:1], [b_ps[pb]], [b_sm])
            ts("dve", sm[:, 1:2], sm[:, 1:2], 1.0 - LAM_INIT0, None, ALU.mult, None, [b_sm], [b_sm])

            L0S = float(os.environ.get("KL0", "9"))
            if L0S <= 1:
                return

            def inproj_fm(win, bi, ncols, evac):
                si = wload3(win[bi % NB0], KC)
                wv = wview(si, KC)
                for half in range(2):
                    pb = bank()
                    for k in range(KC):
                        mm(psum[pb][0:ncols, :], wv[:, k, 0:ncols], hT[:, k, half * 512:(half + 1) * 512],
                           k == 0, k == KC - 1, [b_w[si]] + b_hT[4 * half:4 * half + 4], [b_ps[pb]])
                    evac(half, pb)

            def rope(src, bsrc, dst, bdst, nrow):
                for half in range(2):
                    hs = slice(half * 512, (half + 1) * 512)
                    pb = bank()
                    mm(psum[pb][0:nrow, :], rm_t[0:nrow, 0:nrow], src[0:nrow, hs], True, True, [bsrc, b_tab], [b_ps[pb]])
                    tt("pool", tmpa[0:nrow, :], src[0:nrow, hs], cos_t[0:nrow, hs], ALU.mult, [bsrc, b_tab], [b_ta])
                    tt("dve", tmpb[0:nrow, :], psum[pb][0:nrow, :], sin_t[0:nrow, hs], ALU.mult, [b_ps[pb], b_tab], [b_tb])
                    tt("pool", dst[0:nrow, hs], tmpa[0:nrow, :], tmpb[0:nrow, :], ALU.add, [b_ta, b_tb], [bdst])

            def rstd_from(pbank, out_ap, n, wbuf, npart=128):
                act(out_ap, psum[pbank][0:npart, :], AF.Sqrt, [b_ps[pbank], b_const], [wbuf], bias=eps_t[0:npart, 0:1], scale=1.0 / n)
                recip(out_ap, out_ap, [wbuf], [wbuf])

            for b in range(4):
                def ev(half, pb, b=b):
                    r = sqr[0] % 2
                    sqr[0] += 1
                    act(sq16[r], psum[pb][:, :], AF.Square, [b_ps[pb]], [b_sq[r]])
                    ts("dve", cqg16[:, b, half * 512:(half + 1) * 512], psum[pb][:, :], sm[:, 2 + b:3 + b], None, ALU.mult, None,
                       [b_ps[pb], b_sm], [b_cq])
                    mm(psum[half][:, :], ones16[:, :], sq16[r], b == 0, b == 3, [b_sq[r], b_const], [b_ps[half]])
                inproj_fm(w0in, bi_of["cq%d" % b], 128, ev)
            for half in range(2):
                rstd_from(half, rstdq[:, half * 512:(half + 1) * 512], B_Q_LORA, b_rq)
            if L0S <= 1.2:
                return
            for b in range(2):
                def ev(half, pb, b=b):
                    r = sqr[0] % 2
                    sqr[0] += 1
                    act(sq16[r], psum[pb][:, :], AF.Square, [b_ps[pb]], [b_sq[r]])
                    cp("dve", stA[:, b, half * 512:(half + 1) * 512], psum[pb][:, :], [b_ps[pb]], [b_st[b]])
                    mm(psum[half][:, :], ones16[:, :], sq16[r], b == 0, b == 1, [b_sq[r], b_const], [b_ps[half]])
                inproj_fm(w0in, bi_of["ckv%d" % b], 128, ev)
            for half in range(2):
                rstd_from(half, rkv[:, half * 512:(half + 1) * 512], B_KV_LORA, b_rkv)
            for b in range(2):
                stt("dve", stA[:, b, :], stA[:, b, :], sm[:, 6 + b:7 + b], rkv[:, :], ALU.mult, ALU.mult,
                    [b_st[b], b_sm, b_rkv], [b_st[b]])
                cp("pool", ckvn16[:, b, 0:T], stA[:, b, :], [b_st[b]], [b_ckv])
            if L0S <= 1.4:
                return
            for i in range(NT):
                pb = bank()
                for b in range(2):
                    tr(psum[pb][:, b * 128:(b + 1) * 128], stA[:, b, i * 128:(i + 1) * 128], ident32[:, :],
                       [b_st[b], b_const], [b_ps[pb]])
                cp("act", ost[:, 0:256], psum[pb][:, 0:256], [b_ps[pb]], [b_ost])
                dma("sp", nckv_out[i * 128:(i + 1) * 128, :], ost[:, 0:256], [b_ost], [], d_ost)
            if L0S <= 1.6:
                return
            dma("sp", ctxs.rearrange("p (a c) -> p a c", c=256), cckv_in.rearrange("(a p) c -> p a c", p=128), [], [b_ctx], d_ctx)
            for b in range(2):
                pb = bank()
                for a in range(4):
                    tr(psum[pb][:, a * 128:(a + 1) * 128], ctxs[:, a * 256 + b * 128:a * 256 + (b + 1) * 128], ident32[:, :],
                       [b_ctx, b_const], [b_ps[pb]])
                cp("act", ckvn16[:, b, T:1536], psum[pb][:, :], [b_ps[pb]], [b_ckv])
            if L0S <= 1.8:
                return
            def ev_kpe(half, pb):
                cp("act", stA[0:64, 0, half * 512:(half + 1) * 512], psum[pb][0:64, :], [b_ps[pb]], [b_st[0]])
            inproj_fm(w0in, bi_of["kpe"], 64, ev_kpe)
            pb = bank()
            for i in range(NT):
                tr(psum[pb][:, i * 64:(i + 1) * 64], stA[0:64, 0, i * 128:(i + 1) * 128], ident32[0:64, 0:64],
                   [b_st[0], b_const], [b_ps[pb]])
            cp("act", ost[:, :], psum[pb][:, :], [b_ps[pb]], [b_ost])
            dma("sp", nkpe_out.rearrange("(i p) c -> p i c", p=128), ost.rearrange("p (i c) -> p i c", c=64), [b_ost], [], d_ost)
            rope(stA[:, 0, :], b_st[0], kpe16, b_kpe, 64)
            dma("sp", ctxs[:, 0:256].rearrange("p (a c) -> p a c", c=64), ckpe_in.rearrange("(a p) c -> p a c", p=128), [], [b_ctx], d_ctx)
            pb = bank()
            for a in range(4):
                tr(psum[pb][0:64, a * 128:(a + 1) * 128], ctxs[:, a * 64:(a + 1) * 64], ident32[:, :],
                   [b_ctx, b_const], [b_ps[pb]])
            cp("act", kpe16[0:64, T:1536], psum[pb][0:64, :], [b_ps[pb]], [b_kpe])

            if L0S <= 2:
                return

            def gate_block(u):
                def ev(half, pb):
                    act(gT16[:, half * 512:(half + 1) * 512], psum[pb][:, :], AF.Silu, [b_ps[pb]], [b_g])
                inproj_fm(w0in, bi_of["g%d" % u], 128, ev)

            def attention(u, diff):
                for qh in range(2):
                    qs = slice(qh * 512, (qh + 1) * 512)
                    ncomp = 2 if diff else 1
                    for kt in range(12):
                        ks = slice(kt * 128, (kt + 1) * 128)
                        for c in range(ncomp):
                            sb = bank()
                            if diff:
                                mm(psum[sb][:, :], K16[64 * c:64 * c + 64, ks], q16[64 * c:64 * c + 64, qs], True, False,
                                   [b_K, b_q], [b_ps[sb]], rg=c)
                                mm(psum[sb][:, :], mk16[0:8, ks], mq16[0:8, qs], False, True, [b_tab], [b_ps[sb]], rg=0)
                                sc = 64 ** -0.5
                            else:
                                mm(psum[sb][:, :], K16[:, ks], q16[:, qs], True, False, [b_K, b_q], [b_ps[sb]])
                                mm(psum[sb][:, :], kpe16[0:72, ks], qpe16[0:72, qs], False, True, [b_kpe, b_qpe], [b_ps[sb]])
                                sc = 192 ** -0.5
                            r = ptr[0] % 4
                            ptr[0] += 1
                            act(PT[r], psum[sb][:, :], AF.Exp, [b_ps[sb]], [b_PT[r]], scale=sc)
                            mm(psum[c][:, :], V16[:, kt, :], PT[r], kt == 0, kt == 11, [b_V, b_PT[r]], [b_ps[c]])
                            mm(psum[2 + c][:, :], ones16[:, :], PT[r], kt == 0, kt == 11, [b_const, b_PT[r]], [b_ps[2 + c]])
                    recip(o32[0], psum[2][:, :], [b_ps[2]], [b_o[0]])
                    tt("dve", o32[0], psum[0][:, :], o32[0], ALU.mult, [b_ps[0], b_o[0]], [b_o[0]])
                    if diff:
                        recip(o32[1], psum[3][:, :], [b_ps[3]], [b_o[1]])
                        tt("dve", o32[1], psum[1][:, :], o32[1], ALU.mult, [b_ps[1], b_o[1]], [b_o[1]])
                        stt("dve", o32[0], o32[1], sm[:, 0:1], o32[0], ALU.mult, ALU.add, [b_o[0], b_o[1], b_sm], [b_o[0]])
                        r = sqr[0] % 2
                        sqr[0] += 1
                        act(sq16[r], o32[0], AF.Square, [b_o[0]], [b_sq[r]])
                        pb = bank()
                        mm(psum[pb][:, :], ones16[:, :], sq16[r], True, True, [b_sq[r], b_const], [b_ps[pb]])
                        rstd_from(pb, o32[2], 128, b_o[2])
                        stt("dve", o32[0], o32[0], sm[:, 1:2], o32[2], ALU.mult, ALU.mult, [b_o[0], b_o[2], b_sm], [b_o[0]])
                    yc, b_yc = ycs[u % 2]
                    tt("pool", yc[:, qs], o32[0], gT16[:, qs], ALU.mult, [b_o[0], b_g], [b_yc])
                    if qh == 1:
                        store_y(u, yc, b_yc)

            for h in range(8 if L0S > 3 else 1):
                def evq(half, pb):
                    cp("act", stA[:, 0, half * 512:(half + 1) * 512], psum[pb][:, :], [b_ps[pb]], [b_st[0]])

                def evk(half, pb):
                    cp("act", stA[:, 1, half * 512:(half + 1) * 512], psum[pb][:, :], [b_ps[pb]], [b_st[1]])
                inproj_fm(w0in, bi_of["q%d" % h], 128, evq)
                inproj_fm(w0in, bi_of["k%d" % h], 128, evk)
                for g in range(2):
                    pb = bank()
                    for ii in range(4):
                        i = 4 * g + ii
                        tr(psum[pb][:, ii * 128:(ii + 1) * 128], stA[:, 1, i * 128:(i + 1) * 128], ident32[:, :],
                           [b_st[1], b_const], [b_ps[pb]])
                    cp("act", ost[:, :], psum[pb][:, :], [b_ps[pb]], [b_ost])
                    dma("sp", nk_out[g * 512:(g + 1) * 512, h * 128:(h + 1) * 128].rearrange("(i p) c -> p i c", p=128),
                        ost.rearrange("p (i c) -> p i c", c=128), [b_ost], [], d_ost)
                rope(stA[:, 0, :], b_st[0], q16, b_q, 128)
                rope(stA[:, 1, :], b_st[1], K16, b_K, 128)
                dma("sp", ctxs[:, 0:512].rearrange("p (a c) -> p a c", c=128),
                    ck_in[:, h * 128:(h + 1) * 128].rearrange("(a p) c -> p a c", p=128), [], [b_ctx], d_ctx)
                pb = bank()
                for a in range(4):
                    tr(psum[pb][:, a * 128:(a + 1) * 128], ctxs[:, a * 128:(a + 1) * 128], ident32[:, :],
                       [b_ctx, b_const], [b_ps[pb]])
                cp("act", K16[:, T:1536], psum[pb][:, :], [b_ps[pb]], [b_K])
                si = wload3(w0in[bi_of["v%d" % h] % NB0], KC)
                wv = wview(si, KC)
                for g in range(2):
                    pb = bank()
                    for ii in range(4):
                        i = 4 * g + ii
                        for k in range(KC):
                            mm(psum[pb][:, ii * 128:(ii + 1) * 128], hT[:, k, i * 128:(i + 1) * 128], wv[:, k, :],
                               k == 0, k == KC - 1, [b_w[si], b_hT[i]], [b_ps[pb]])
                    cp("act", V16[:, 4 * g:4 * g + 4, :], psum[pb][:, :].rearrange("p (a c) -> p a c", c=128), [b_ps[pb]], [b_V])
                    cp("dve", vst[:, :], psum[pb][:, :], [b_ps[pb]], [b_vst])
                    dma("sp", nv_out[g * 512:(g + 1) * 512, h * 128:(h + 1) * 128].rearrange("(i p) c -> p i c", p=128),
                        vst.rearrange("p (i c) -> p i c", c=128), [b_vst], [], d_vst)
                dma("pool", V16[:, 8:12, :], cv_in[:, h * 128:(h + 1) * 128].rearrange("(a p) c -> p a c", p=128), [], [b_V], d_V)
                gate_block(h)
                attention(h, True)

            if L0S <= 4:
                return
            for h in range(8 if L0S > 5 else 1):
                u = 8 + h
                si = wload3(wuq_in[h], 4)
                wv = wview(si, 4)
                for half in range(2):
                    hs = slice(half * 512, (half + 1) * 512)
                    pb = bank()
                    for k in range(4):
                        mm(psum[pb][:, :], wv[:, k, :], cqg16[:, k, hs], k == 0, k == 3, [b_w[si], b_cq], [b_ps[pb]])
                    tt("dve", q16[:, hs], psum[pb][:, :], rstdq[:, hs], ALU.mult, [b_ps[pb], b_rq], [b_q])
                si = wload3(wuq_in[8 + h], 4)
                wv = wview(si, 4)
                for half in range(2):
                    hs = slice(half * 512, (half + 1) * 512)
                    pb = bank()
                    for k in range(4):
                        mm(psum[pb][0:64, :], wv[:, k, 0:64], cqg16[:, k, hs], k == 0, k == 3, [b_w[si], b_cq], [b_ps[pb]])
                    tt("dve", stA[0:64, 0, hs], psum[pb][0:64, :], rstdq[0:64, hs], ALU.mult, [b_ps[pb], b_rq], [b_st[0]])
                rope(stA[:, 0, :], b_st[0], qpe16, b_qpe, 64)
                si = wload3(wukv_in[h], 2)
                wv = wview(si, 2)
                for kb in range(3):
                    pb = bank()
                    for k in range(2):
                        mm(psum[pb][:, :], wv[:, k, :], ckvn16[:, k, kb * 512:(kb + 1) * 512], k == 0, k == 1,
                           [b_w[si], b_ckv], [b_ps[pb]])
                    cp("act", K16[:, kb * 512:(kb + 1) * 512], psum[pb][:, :], [b_ps[pb]], [b_K])
                si = wload3(wukv_in[8 + h], 2)
                wv = wview(si, 2)
                for g in range(3):
                    pb = bank()
                    for ii in range(4):
                        kt = 4 * g + ii
                        for k in range(2):
                            mm(psum[pb][:, ii * 128:(ii + 1) * 128], ckvn16[:, k, kt * 128:(kt + 1) * 128], wv[:, k, :],
                               k == 0, k == 1, [b_w[si], b_ckv], [b_ps[pb]])
                    cp("act", V16[:, 4 * g:4 * g + 4, :], psum[pb][:, :].rearrange("p (a c) -> p a c", c=128), [b_ps[pb]], [b_V])
                gate_block(u)
                attention(u, False)
        def l1_phase():
            ar = Arena()
            tri_a, b_tri = ar.f32(768)
            tri = [tri_a[:, 0:384], tri_a[:, 384:768]]
            msk_a, b_msk = ar.bf16(1024)
            msk4 = [msk_a[:, 0:512], msk_a[:, 512:1024]]
            lvl_a, b_lvl = ar.bf16(4 * 256)
            lvl = lvl_a.rearrange("p (l c) -> p l c", c=256)
            mskT_a, b_mskT = ar.f32(256)
            mskT = [mskT_a[:, 0:128], mskT_a[:, 128:256]]
            blk16, b_blk = ar.bf16(128)
            small, b_small = ar.f32(1024)
            NBL1 = len(L1B)
            mu_t = small[:, 0:2 * NBL1].rearrange("p (b m) -> p b m", m=2)
            c0_t = small[:, 136:136 + NBL1]
            mufix = small[:, 204:204 + 2 * NBL1].rearrange("p (b m) -> p b m", m=2)
            a0_t = small[:, 340:372].rearrange("p (z h) -> p z h", h=16)
            hpar = small[:, 372:468].rearrange("p (w h) -> p w h", h=16)
            b_c = [b_tri, b_msk, b_mskT, b_blk, b_small, b_lvl]
            w0h, b_w0h = ar.f32(256)
            twad = [ar.bf16(T) for _ in range(4)]
            tw16 = [twad[0][0], twad[1][0]]
            ad16 = [twad[2][0], twad[3][0]]
            b_tw = [twad[0][1], twad[1][1]]
            b_ad = [twad[2][1], twad[3][1]]
            w2h_a, b_w2 = ar.bf16(4 * 128)
            w2h = w2h_a.rearrange("p (w c) -> p w c", c=128)
            r32, b_r = ar.f32(T)
            k32, b_k = ar.f32(T)
            v32, b_v = ar.f32(T)
            kk32, b_kk = ar.f32(T)
            acc32, b_acc = ar.f32(T)
            g16, b_g = ar.bf16(T)
            v16, b_v16 = ar.bf16(T)
            bon32, b_bon = ar.f32(T)
            rawy, b_rawy = ar.f32(T)
            raw, b_raw = rawy, b_rawy
            lw32, b_lw = rawy, b_rawy
            y32, b_y = rawy, b_rawy
            sqs = [ar.bf16(512), ar.bf16(512)]
            sq16 = [sqs[0][0], sqs[1][0]]
            b_sq = [sqs[0][1], sqs[1][1]]
            tmp32, b_tmp = ar.f32(512)
            grpA, b_grpA = ar.f32(3072)
            G3 = grpA.rearrange("p (i c) -> p i c", c=384)
            b_G3 = b_grpA
            YX = [grpA[:, 1024 * r_:1024 * (r_ + 1)].bitcast(BF16).rearrange("p (q c) -> p q c", c=256) for r_ in range(2)]
            b_YX = [[b_grpA[4 * r_ + ql // 2] for ql in range(8)] for r_ in range(2)]
            TMPm = grpA[:, 2048:3072].bitcast(BF16).rearrange("p (q c) -> p q c", c=256)
            b_TM = [b_grpA[8 + ql // 2] for ql in range(8)]
            grpB, b_grpB = ar.f32(2048)
            kz32, b_kz = grpB[:, 0:T], b_grpB[0:4]
            al32, b_al = grpB[:, T:2 * T], b_grpB[4:8]
            P2 = grpB[:, 0:1024].bitcast(BF16).rearrange("p (q c) -> p q c", c=256)
            b_P2 = [b_grpB[ql // 2] for ql in range(8)]
            P4 = grpB[:, 1024:2048].bitcast(BF16).rearrange("p (q c) -> p q c", c=256)
            b_P4 = [b_grpB[4 + ql // 2] for ql in range(8)]
            grpC, b_grpC = ar.f32(T)
            a32, b_a = grpC, b_grpC
            d32, b_d = grpC, b_grpC
            Ginv, b_Gi = grpC, b_grpC
            MN_a, b_MN_all = ar.bf16(16 * 256)
            MN = MN_a.rearrange("p (q c) -> p q c", c=256)
            b_MN = [b_MN_all[q // 2] for q in range(16)]
            XTf_a, b_XTf_all = ar.bf16(16 * 128)
            XTf = XTf_a.rearrange("p (q c) -> p q c", c=128)
            b_XTf = [b_XTf_all[q // 4] for q in range(16)]
            BR_a, b_BR = ar.bf16(NT * 256)
            BR = BR_a.rearrange("p (i c) -> p i c", c=256)
            KA_a, b_KA = ar.bf16(NT * 256)
            KA = KA_a.rearrange("p (i c) -> p i c", c=256)
            UG = KA_a.rearrange("p (q c) -> p q c", c=128)
            b_UG = [b_KA[q // 4] for q in range(16)]
            Kb_a, b_Kb = ar.bf16(NT * 256)
            KbAb = Kb_a.rearrange("p (i c) -> p i c", c=256)
            wl16 = Kb_a[:, 0:1024].rearrange("p (q c) -> p q c", c=64)
            b_wl = [b_Kb[q // 8] for q in range(16)]
            lkt = [Kb_a[:, 1024:1152], Kb_a[:, 1536:1664]]
            b_lkt = [b_Kb[2], b_Kb[3]]
            tok_a, b_tok_all = ar.bf16(NT * 512)
            tokT = tok_a.rearrange("p (i c) -> p i c", c=512)
            b_tok = [b_tok_all[i] for i in range(NT)]
            GM_a, b_GM_all = ar.bf16(16 * 256)
            GM = GM_a.rearrange("p (q c) -> p q c", c=256)
            b_GM = [b_GM_all[q // 2] for q in range(16)]
            Qt, b_Qt = ar.bf16(T)
            TT_a, b_TT = ar.f32(16 * 128)
            TT = TT_a.rearrange("p (c j) -> p c j", j=128)
            HL_a, b_HL = ar.f32(16 * 64)
            HL = HL_a.rearrange("p (c j) -> p c j", j=64)
            Hs_a, b_Hs = ar.f32(17 * 64)
            Hs = Hs_a.rearrange("p (n j) -> p n j", j=64)
            H16_a, b_H16 = ar.bf16(17 * 64)
            Hs16 = H16_a.rearrange("p (n j) -> p n j", j=64)
            slsto, b_slsto = ar.f32(256)
            sl32, b_sl = slsto[:, 0:128], b_slsto
            sto, b_sto = slsto[:, 128:256], b_slsto
            ycs = [ar.bf16(T)] * 2
            LAYOUT["l1"] = list(ar.log)
            LAYOUT["l1_total"] = ar.off
            d_c1 = S.dsem("dl1c")
            d_w2 = S.dsem("dw2h")
            d_sl = S.dsem("dsl")
            d_sto = S.dsem("dsto")
            bi_of = {name: i for i, (name, _, _) in enumerate(L1B)}
            sqr = [0]
            lkr = [0]
            lastcol_of = [63, 0]

            for z in range(2):
                dma("sp", tri[z], tri_in[z], [], [b_c], d_c1)
                for rep_ in range(2):
                    dma("pool", msk4[z][:, rep_ * 256:(rep_ + 1) * 256], msk_in[z], [], [b_c], d_c1)
                dma("sp", mskT[z], msk_in[1 - z, :, 0:128], [], [b_c], d_c1)
            dma("pool", blk16, blk64_in, [], [b_c], d_c1)
            for l_ in range(4):
                for rep_ in range(2):
                    dma("pool", lvl[:, l_, rep_ * 128:(rep_ + 1) * 128], lvl_in[l_], [], [b_c], d_c1)
            dma("sp", mu_t, mu_in, [], [b_c], d_c1)
            dma("sp", a0_t, a0_in, [], [b_c], d_c1)
            dma("sp", hpar[:, 0:5, :], hpar_in, [], [b_c], d_c1)
            ts("dve", hpar[:, 5, :], hpar[:, 1, :], -1.0, 1.0, ALU.mult, ALU.add, [b_c], [b_c])
            tt("dve", c0_t, mu_t[:, :, 0], mu_t[:, :, 1], ALU.add, [b_c], [b_c])
            ts("dve", c0_t, c0_t, -1.0, 1.0, ALU.mult, ALU.add, [b_c], [b_c])
            ts("dve", mufix.rearrange("p b m -> p (b m)"), mu_t.rearrange("p b m -> p (b m)"), keep_t[:, 1:2], -1.0, ALU.mult, ALU.mult,
               [b_c, b_const], [b_c])
            memset("dve", TT.rearrange("p c j -> p (c j)"), 0.0, [b_TT])
            d_ones = S.dsem("dl1ones")
            for z in range(2):
                pass

            def inproj_fm(bi, ncols, evac):
                si = wload3(w1in[bi % NB1], KC)
                wv = wview(si, KC)
                for half in range(2):
                    pb = bank(0, 8)
                    for k in range(KC):
                        mm(psum[pb][0:ncols, :], wv[:, k, 0:ncols], hT[:, k, half * 512:(half + 1) * 512],
                           k == 0, k == KC - 1, [b_w[si]] + b_hT[4 * half:4 * half + 4], [b_ps[pb]])
                    evac(half, pb)

            def shifted(bi, nrow, dst, bdst):
                def ev(half, pb):
                    cp("act", raw[0:nrow, half * 512:(half + 1) * 512], psum[pb][0:nrow, :], [b_ps[pb]], [b_raw])
                inproj_fm(bi, nrow, ev)
                act(dst[0:nrow, :], raw[0:nrow, :], AF.Identity, [b_raw, b_c], [bdst], scale=c0_t[0:nrow, bi:bi + 1])
                stt("dve", dst[0:nrow, 1:T], raw[0:nrow, 0:T - 1], mu_t[0:nrow, bi, 0:1], dst[0:nrow, 1:T], ALU.mult, ALU.add,
                    [b_raw, b_c, bdst], [bdst])
                stt("dve", dst[0:nrow, 0:T - 1], raw[0:nrow, 1:T], mu_t[0:nrow, bi, 1:2], dst[0:nrow, 0:T - 1], ALU.mult, ALU.add,
                    [b_raw, b_c, bdst], [bdst])
                for bnd in (256, 512, 768):
                    stt("dve", dst[0:nrow, bnd:bnd + 1], raw[0:nrow, bnd - 1:bnd], mufix[0:nrow, bi, 0:1], dst[0:nrow, bnd:bnd + 1],
                        ALU.mult, ALU.add, [b_raw, b_c, bdst], [bdst])
                    stt("dve", dst[0:nrow, bnd - 1:bnd], raw[0:nrow, bnd:bnd + 1], mufix[0:nrow, bi, 1:2], dst[0:nrow, bnd - 1:bnd],
                        ALU.mult, ALU.add, [b_raw, b_c, bdst], [bdst])

            for z in range(2):
                shifted(bi_of["wd%d" % z], LORA, r32, b_r)
                act(tw16[z][0:LORA, :], r32[0:LORA, :], AF.Tanh, [b_r], [b_tw[z]])
                shifted(bi_of["ad%d" % z], LORA, r32, b_r)
                cp("dve", ad16[z][0:LORA, :], r32[0:LORA, :], [b_r], [b_ad[z]])

            L1S = float(os.environ.get("KL1", "9"))
            if L1S <= 1:
                return
            for hp in range(16):
                cs = slice(hp * 128, (hp + 1) * 128)
                for z in range(2):
                    dma("pool", w2h[0:LORA, z, :], w2_in[z][:, cs], [], [b_w2], d_w2)
                    dma("pool", w2h[0:LORA, 2 + z, :], a2_in[z][:, cs], [], [b_w2], d_w2)
                    dma("sp", w0h[0:1, z * 128:(z + 1) * 128], w0row_in[z][:, cs], [], [b_w0h], d_w2)
                shifted(bi_of["r%d" % hp], 128, r32, b_r)
                shifted(bi_of["k%d" % hp], 128, k32, b_k)
                shifted(bi_of["v%d" % hp], 128, v32, b_v)
                shifted(bi_of["g%d" % hp], 128, acc32, b_acc)
                act(g16[:, :], acc32[:, :], AF.Silu, [b_acc], [b_g])
                cp("pool", v16[:, :], v32[:, :], [b_v], [b_v16])
                ts("dve", kk32[:, :], k32[:, :], hpar[:, 0, hp:hp + 1], None, ALU.mult, None, [b_k, b_c], [b_kk])

                for half in range(2):
                    hs = slice(half * 512, (half + 1) * 512)
                    r = sqr[0] % 2
                    sqr[0] += 1
                    act(sq16[r], kk32[:, hs], AF.Square, [b_kk], [b_sq[r]])
                    pb = bank(0, 8)
                    mm(psum[pb][:, :], blk16[:, :], sq16[r], True, True, [b_sq[r], b_c], [b_ps[pb]])
                    ts("dve", tmp32[:, :], psum[pb][:, :], 1e-12, None, ALU.max, None, [b_ps[pb]], [b_tmp])
                    act(tmp32[:, :], tmp32[:, :], AF.Sqrt, [b_tmp], [b_tmp])
                    recip(tmp32[:, :], tmp32[:, :], [b_tmp], [b_tmp])
                    tt("dve", kk32[:, hs], kk32[:, hs], tmp32[:, :], ALU.mult, [b_kk, b_tmp], [b_kk])

                for z in range(2):
                    for half in range(2):
                        hs = slice(half * 512, (half + 1) * 512)
                        pb = bank(0, 8)
                        mm(psum[pb][:, :], w2h[0:LORA, 2 + z, :], ad16[z][0:LORA, hs], True, True, [b_w2, b_ad[z]], [b_ps[pb]])
                        act(a32[:, hs], psum[pb][:, :], AF.Sigmoid, [b_ps[pb], b_c], [b_a], bias=a0_t[:, z, hp:hp + 1])
                    ts("dve", kz32[:, :], a32[:, :], hpar[:, 1, hp:hp + 1], hpar[:, 5, hp:hp + 1], ALU.mult, ALU.add, [b_a, b_c], [b_kz])
                    tt("pool", kz32[:, :], kz32[:, :], k32[:, :], ALU.mult, [b_kz, b_k], [b_kz])
                    stt("dve", al32[:, :], kk32[:, :], -1.0, a32[:, :], ALU.mult, ALU.mult, [b_kk, b_a], [b_al])
                    tt("pool", d32[:, :], r32[:, :], kz32[:, :], ALU.mult, [b_r, b_kz], [b_d])
                    for half in range(2):
                        hs = slice(half * 512, (half + 1) * 512)
                        r = sqr[0] % 2
                        sqr[0] += 1
                        ts("dve", sq16[r], d32[:, hs], hpar[:, 2, hp:hp + 1], None, ALU.mult, None, [b_d, b_c], [b_sq[r]])
                        pb = bank(0, 8)
                        mm(psum[pb][:, :], blk16[:, :], sq16[r], True, True, [b_sq[r], b_c], [b_ps[pb]])
                        tt("dve", bon32[:, hs], psum[pb][:, :], v32[:, hs], ALU.mult, [b_ps[pb], b_v], [b_bon])
                    for g in range(2):
                        pb = bank(0, 8)
                        for ii in range(4):
                            i = 4 * g + ii
                            mm(psum[pb][:, ii * 128:(ii + 1) * 128], tw16[z][0:LORA, i * 128:(i + 1) * 128], w2h[0:LORA, z, :],
                               True, False, [b_tw[z], b_w2], [b_ps[pb]])
                            mm(psum[pb][:, ii * 128:(ii + 1) * 128], ones32[0:1, :], w0h[0:1, z * 128:(z + 1) * 128],
                               False, True, [b_const, b_w0h], [b_ps[pb]])
                        act(lw32[:, g * 512:(g + 1) * 512], psum[pb][:, :], AF.Sigmoid, [b_ps[pb]], [b_lw])
                    ts("dve", lw32[:, :], lw32[:, :], -math.exp(-0.5), None, ALU.mult, None, [b_lw], [b_lw])
                    for i in range(NT):
                        pb = bank(0, 8)
                        mm(psum[pb][:, 0:384], lw32[:, i * 128:(i + 1) * 128], tri[z][:, :], True, True, [b_lw, b_c], [b_ps[pb]])
                        act(G3[:, i, :], psum[pb][:, 0:384], AF.Exp, [b_ps[pb]], [b_G3])
                        act(Ginv[:, i * 128:(i + 1) * 128], psum[pb][:, 0:128], AF.Exp, [b_ps[pb]], [b_Gi], scale=-1.0)

                    def v3(t32):
                        return t32.rearrange("p (i c) -> p i c", c=128)
                    tt("dve", BR[:, :, 0:128], v3(kk32), G3[:, :, 128:256], ALU.mult, [b_kk, b_G3], [b_BR])
                    tt("pool", BR[:, :, 128:256], v3(r32), G3[:, :, 0:128], ALU.mult, [b_r, b_G3], [b_BR])
                    tt("dve", KA[:, :, 0:128], v3(kz32), v3(Ginv), ALU.mult, [b_kz, b_Gi], [b_KA])
                    tt("pool", KA[:, :, 128:256], v3(al32), v3(Ginv), ALU.mult, [b_al, b_Gi], [b_KA])
                    tt("dve", KbAb[:, :, 0:128], v3(kz32), G3[:, :, 256:384], ALU.mult, [b_kz, b_G3], [b_Kb])
                    tt("pool", KbAb[:, :, 128:256], v3(al32), G3[:, :, 256:384], ALU.mult, [b_al, b_G3], [b_Kb])
                    if L1S <= 2:
                        return
                    for i in range(NT):
                        pb = bank(0, 8)
                        pv = psum[pb].bitcast(BF16)
                        tr(pv[:, 0:128], v16[:, i * 128:(i + 1) * 128], ident16[:, :], [b_v16, b_const], [b_ps[pb]])
                        tr(pv[:, 128:256], BR[:, i, 0:128], ident16[:, :], [b_BR, b_const], [b_ps[pb]])
                        tr(pv[:, 256:384], KbAb[:, i, 0:128], ident16[:, :], [b_Kb, b_const], [b_ps[pb]])
                        tr(pv[:, 384:512], KbAb[:, i, 128:256], ident16[:, :], [b_Kb, b_const], [b_ps[pb]])
                        cp("act", tokT[:, i, :], pv[:, 0:512], [b_ps[pb]], [b_tok[i]])
                    gam = small[:, 724:740]
                    cp("dve", gam.rearrange("p (i c) -> p i c", c=2), G3[:, :, lastcol_of[z]:lastcol_of[z] + 65:64], [b_G3], [b_small])
                    for i in range(NT):
                        for e in range(2):
                            q = 2 * i + e
                            es_ = slice(64 * e, 64 * e + 64)
                            pb = bank(0, 8)
                            mm(psum[pb][:, 0:256], KA[es_, i, 0:128], BR[es_, i, 0:256], True, True, [b_KA, b_BR], [b_ps[pb]], rg=e)
                            mm(psum[pb][:, 256:512], KA[es_, i, 128:256], BR[es_, i, 0:256], True, True, [b_KA, b_BR], [b_ps[pb]], rg=e)
                            r = lkr[0] % 2
                            lkr[0] += 1
                            tt("dve", lkt[r], psum[pb][:, 0:128], msk4[z][:, 0:128], ALU.mult, [b_ps[pb], b_c], [b_lkt[r]])
                            tt("dve", GM[:, q, 0:128], psum[pb][:, 128:256], msk4[z][:, 128:256], ALU.mult, [b_ps[pb], b_c], [b_GM[q]])
                            tt("dve", MN[:, q, 0:128], psum[pb][:, 256:384], msk4[z][:, 0:128], ALU.mult, [b_ps[pb], b_c], [b_MN[q]])
                            tt("dve", GM[:, q, 128:256], psum[pb][:, 384:512], msk4[z][:, 128:256], ALU.mult, [b_ps[pb], b_c], [b_GM[q]])
                            pb = bank(0, 8)
                            mm(psum[pb][:, 0:128], BR[es_, i, 0:128], KA[es_, i, 128:256], True, True, [b_KA, b_BR], [b_ps[pb]], rg=e)
                            tt("dve", MN[:, q, 128:256], psum[pb][:, 0:128], mskT[z][:, :], ALU.mult, [b_ps[pb], b_c], [b_MN[q]])
                            pb = bank(0, 8)
                            mm(psum[pb][:, 0:64], lkt[r], tokT[:, i, 64 * e:64 * e + 64], True, True, [b_lkt[r], b_tok[i]], [b_ps[pb]])
                            cp("act", wl16[:, q, :], psum[pb][:, 0:64], [b_ps[pb]], [b_wl[q]])
                    if L1S <= 3:
                        return
                    def pair_mm(lhs_a, rhs_a, lhs_b, rhs_b, reads):
                        pb_ = bank(0, 8)
                        mm(psum[pb_][:, 0:128], lhs_a, rhs_a, True, True, reads, [b_ps[pb_]])
                        mm(psum[pb_][:, 128:256], lhs_b, rhs_b, True, True, reads, [b_ps[pb_]])
                        return pb_
                    for bt in range(2):
                        qs_ = list(range(8 * bt, 8 * bt + 8))
                        for q in qs_:
                            ql = q % 8
                            tt("pool", TMPm[:, ql, :], MN[:, q, :], lvl[:, 0, :], ALU.mult, [b_MN[q], b_c], [b_TM[ql]])
                            tt("pool", YX[0][:, ql, 0:128], TMPm[:, ql, 0:128], ident16[:, :], ALU.add, [b_TM[ql], b_const], [b_YX[0][ql]])
                            tt("pool", YX[0][:, ql, 128:256], TMPm[:, ql, 128:256], ident16[:, :], ALU.add, [b_TM[ql], b_const], [b_YX[0][ql]])
                        for q in qs_:
                            ql = q % 8
                            pb_ = pair_mm(TMPm[:, ql, 128:256], TMPm[:, ql, 0:128], TMPm[:, ql, 0:128], TMPm[:, ql, 128:256], [b_TM[ql]])
                            cp("act", P2[:, ql, :], psum[pb_][:, 0:256], [b_ps[pb_]], [b_P2[ql]])
                        for q in qs_:
                            ql = q % 8
                            pb_ = pair_mm(P2[:, ql, 128:256], YX[0][:, ql, 0:128], P2[:, ql, 0:128], YX[0][:, ql, 128:256], [b_P2[ql], b_YX[0][ql]])
                            tt("dve", YX[1][:, ql, :], psum[pb_][:, 0:256], YX[0][:, ql, :], ALU.add, [b_ps[pb_], b_YX[0][ql]], [b_YX[1][ql]])
                        for q in qs_:
                            ql = q % 8
                            pb_ = pair_mm(P2[:, ql, 128:256], P2[:, ql, 0:128], P2[:, ql, 0:128], P2[:, ql, 128:256], [b_P2[ql]])
                            cp("act", P4[:, ql, :], psum[pb_][:, 0:256], [b_ps[pb_]], [b_P4[ql]])
                        for q in qs_:
                            ql = q % 8
                            pb_ = pair_mm(P4[:, ql, 128:256], YX[1][:, ql, 0:128], P4[:, ql, 0:128], YX[1][:, ql, 128:256], [b_P4[ql], b_YX[1][ql]])
                            tt("dve", YX[0][:, ql, :], psum[pb_][:, 0:256], YX[1][:, ql, :], ALU.add, [b_ps[pb_], b_YX[1][ql]], [b_YX[0][ql]])
                        cur = 0
                        for l_ in (1, 2, 3):
                            nxt = 1 - cur
                            for q in qs_:
                                ql = q % 8
                                tt("pool", TMPm[:, ql, :], MN[:, q, :], lvl[:, l_, :], ALU.mult, [b_MN[q], b_c], [b_TM[ql]])
                            for q in qs_:
                                ql = q % 8
                                pb_ = pair_mm(TMPm[:, ql, 128:256], YX[cur][:, ql, 0:128], TMPm[:, ql, 0:128], YX[cur][:, ql, 128:256],
                                              [b_TM[ql], b_YX[cur][ql]])
                                cp("act", P2[:, ql, :], psum[pb_][:, 0:256], [b_ps[pb_]], [b_P2[ql]])
                            for q in qs_:
                                ql = q % 8
                                pb_ = pair_mm(YX[cur][:, ql, 128:256], P2[:, ql, 0:128], YX[cur][:, ql, 0:128], P2[:, ql, 128:256],
                                              [b_P2[ql], b_YX[cur][ql]])
                                if l_ < 3:
                                    tt("dve", YX[nxt][:, ql, :], psum[pb_][:, 0:256], YX[cur][:, ql, :], ALU.add, [b_ps[pb_], b_YX[cur][ql]], [b_YX[nxt][ql]])
                                else:
                                    tt("dve", XTf[:, q, :], psum[pb_][:, 0:128], YX[cur][:, ql, 0:128], ALU.add, [b_ps[pb_], b_YX[cur][ql]], [b_XTf[q]])
                            cur = nxt
                    X = XTf
                    bX = b_XTf
                    if L1S <= 4:
                        return
                    for i in range(NT):
                        for e in range(2):
                            q = 2 * i + e
                            pb = bank(0, 8)
                            mm(psum[pb][:, 0:64], X[:, q, :], wl16[:, q, :], True, True, [bX[q], b_wl[q]], [b_ps[pb]])
                            mm(psum[pb][:, 64:128], X[:, q, :], tokT[:, i, 128 + 64 * e:128 + 64 * e + 64], True, True, [bX[q], b_tok[i]], [b_ps[pb]])
                            cp("act", UG[:, q, :], psum[pb][:, 0:128], [b_ps[pb]], [b_UG[q]])
                    for i in range(NT):
                        pb = bank(0, 8)
                        for e in range(2):
                            q = 2 * i + e
                            es_ = slice(64 * e, 64 * e + 64)
                            mm(psum[pb][es_, 0:128], UG[:, q, 64:128], GM[:, q, 128:256], True, True, [b_UG[q], b_GM[q]], [b_ps[pb]])
                        tt("dve", Qt[:, i * 128:(i + 1) * 128], psum[pb][:, 0:128], BR[:, i, 128:256], ALU.add, [b_ps[pb], b_BR], [b_Qt])
                        pb = bank(0, 8)
                        for e in range(2):
                            q = 2 * i + e
                            es_ = slice(64 * e, 64 * e + 64)
                            for cc in range(2):
                                ts_ = slice(64 * cc, 64 * cc + 64)
                                mm(psum[pb][es_, cc * 64:cc * 64 + 64], UG[ts_, q, 64:128], tokT[ts_, i, 384 + 64 * e:384 + 64 * e + 64],
                                   True, True, [b_UG[q], b_tok[i]], [b_ps[pb]], rg=cc)
                                mm(psum[pb][es_, 128 + cc * 64:128 + cc * 64 + 64], tokT[ts_, i, 256 + 64 * e:256 + 64 * e + 64],
                                   tokT[ts_, i, 64 * e:64 * e + 64], True, False, [b_tok[i]], [b_ps[pb]], rg=cc)
                                mm(psum[pb][es_, 128 + cc * 64:128 + cc * 64 + 64], tokT[ts_, i, 384 + 64 * e:384 + 64 * e + 64],
                                   UG[ts_, q, 0:64], False, True, [b_tok[i], b_UG[q]], [b_ps[pb]], rg=cc)
                        for e in range(2):
                            es_ = slice(64 * e, 64 * e + 64)
                            cp("act", TT[es_, 2 * i:2 * i + 2, 64 * e:64 * e + 64], psum[pb][es_, 0:128].rearrange("p (c j) -> p c j", j=64),
                               [b_ps[pb]], [b_TT])
                        cp("dve", HL[:, 2 * i:2 * i + 2, :], psum[pb][:, 128:256].rearrange("p (c j) -> p c j", j=64), [b_ps[pb]], [b_HL])
                    if L1S <= 5:
                        return
                    dma("sp", sl32.rearrange("p (e j) -> p e j", j=64)[0:64], st_in[z, 2 * hp:2 * hp + 2].rearrange("e i j -> i e j"),
                        [], [b_sl], d_sl)
                    pb = bank(0, 8)
                    tr(psum[pb][:, 0:64], sl32[0:64, :], ident32[0:64, 0:64], [b_sl, b_const], [b_ps[pb]])
                    cp("dve", Hs[:, 0, :], psum[pb][:, 0:64], [b_ps[pb]], [b_Hs])
                    cp("pool", Hs16[:, 0, :], Hs[:, 0, :], [b_Hs], [b_H16])
                    order = list(range(16)) if z == 0 else list(range(15, -1, -1))
                    for n, c in enumerate(order):
                        pb = bank(0, 8)
                        mm(psum[pb][:, 0:64], TT[:, c, :], Hs[:, n, :], True, True, [b_TT, b_Hs], [b_ps[pb]])
                        i, cc = c // 2, c % 2
                        stt("dve", Hs[:, n + 1, :], Hs[:, n, :], gam[:, c:c + 1], psum[pb][:, 0:64], ALU.mult, ALU.add, [b_Hs, b_small, b_ps[pb]], [b_Hs])
                        tt("dve", Hs[:, n + 1, :], Hs[:, n + 1, :], HL[:, c, :], ALU.add, [b_Hs, b_HL], [b_Hs])
                        if (n + 1) % 4 == 0:
                            seg = (n + 1) // 4 - 1
                            if z == 1:
                                seg = 3 - seg
                            pb2 = bank(0, 8)
                            tr(psum[pb2][0:64, 0:128], Hs[:, n + 1, :], ident32[:, :], [b_Hs, b_const], [b_ps[pb2]])
                            cp("act", sto[0:64, :], psum[pb2][0:64, 0:128], [b_ps[pb2]], [b_sto])
                            dma("sp", sst_out[z, seg, 2 * hp:2 * hp + 2].rearrange("e i j -> i e j"),
                                sto.rearrange("p (e j) -> p e j", j=64)[0:64], [b_sto], [], d_sto)
                            if n + 1 < 16:
                                ts("dve", Hs[:, n + 1, :], Hs[:, n + 1, :], keep_t[:, 0:1], None, ALU.mult, None, [b_Hs, b_const], [b_Hs])
                        if n + 1 < 16:
                            cp("pool", Hs16[:, n + 1, :], Hs[:, n + 1, :], [b_Hs], [b_H16])
                    if L1S <= 6:
                        return
                    for half in range(2):
                        pb = bank(0, 8)
                        for ii in range(4):
                            i = 4 * half + ii
                            for cc in range(2):
                                c = 2 * i + cc
                                n = order.index(c)
                                ts_ = slice(64 * cc, 64 * cc + 64)
                                col = slice(ii * 128 + cc * 64, ii * 128 + cc * 64 + 64)
                                for e in range(2):
                                    q = 2 * i + e
                                    es_ = slice(64 * e, 64 * e + 64)
                                    mm(psum[pb][es_, col], Hs16[es_, n, :], Qt[es_, i * 128 + cc * 64:i * 128 + cc * 64 + 64], True, False,
                                       [b_H16, b_Qt], [b_ps[pb]], rg=e)
                                    mm(psum[pb][es_, col], tokT[ts_, i, 64 * e:64 * e + 64], GM[ts_, q, cc * 64:cc * 64 + 64], False, False,
                                       [b_tok[i], b_GM[q]], [b_ps[pb]], rg=cc)
                                    mm(psum[pb][es_, col], UG[ts_, q, 0:64], GM[ts_, q, 128 + cc * 64:128 + cc * 64 + 64], False, True,
                                       [b_UG[q], b_GM[q]], [b_ps[pb]], rg=cc)
                        cp("act", y32[:, half * 512:(half + 1) * 512], psum[pb][:, :], [b_ps[pb]], [b_y])
                    for half in range(2):
                        hs = slice(half * 512, (half + 1) * 512)
                        r = sqr[0] % 2
                        sqr[0] += 1
                        cp("pool", sq16[r], y32[:, hs], [b_y], [b_sq[r]])
                        pb = bank(0, 8)
                        mm(psum[pb][:, :], blk16[:, :], sq16[r], True, True, [b_sq[r], b_c], [b_ps[pb]])
                        stt("dve", d32[:, hs], psum[pb][:, :], -1.0 / 64, y32[:, hs], ALU.mult, ALU.add, [b_ps[pb], b_y], [b_d])
                        r = sqr[0] % 2
                        sqr[0] += 1
                        act(sq16[r], d32[:, hs], AF.Square, [b_d], [b_sq[r]])
                        pb = bank(0, 8)
                        mm(psum[pb][:, :], blk16[:, :], sq16[r], True, True, [b_sq[r], b_c], [b_ps[pb]])
                        act(tmp32[:, :], psum[pb][:, :], AF.Sqrt, [b_ps[pb], b_const], [b_tmp], bias=eps_t[:, 1:2], scale=1.0 / 64)
                        recip(tmp32[:, :], tmp32[:, :], [b_tmp], [b_tmp])
                        tt("dve", d32[:, hs], d32[:, hs], tmp32[:, :], ALU.mult, [b_d, b_tmp], [b_d])
                        ts("dve", d32[:, hs], d32[:, hs], hpar[:, 3, hp:hp + 1], hpar[:, 4, hp:hp + 1], ALU.mult, ALU.add, [b_d, b_c], [b_d])
                        if z == 0:
                            tt("pool", acc32[:, hs], d32[:, hs], bon32[:, hs], ALU.add, [b_d, b_bon], [b_acc])
                        else:
                            tt("pool", d32[:, hs], d32[:, hs], bon32[:, hs], ALU.add, [b_d, b_bon], [b_d])
                            tt("pool", acc32[:, hs], acc32[:, hs], d32[:, hs], ALU.add, [b_d, b_acc], [b_acc])
                yc, b_yc = ycs[hp % 2]
                tt("dve", yc[:, :], acc32[:, :], g16[:, :], ALU.mult, [b_acc, b_g], [b_yc])
                store_y(hp, yc, b_yc)
        STAGE = int(os.environ.get("KSTAGE", "9"))
        b_x1 = [S.buf("x1_%d" % i) for i in range(NT)]
        modulation3()
        if STAGE >= 2:
            arg = Arena(12800)
            load_row_bcast(arg, gsc_dr[0], [b_gsc], gate_b, b_gateb, "grow0")
            norm_phase(0, x_in, None, Arena(12800 + 2048))
        if STAGE >= 3:
            l0_phase()
        if STAGE >= 4:
            outproj_phase(w0out, x_in, None, x1_dr, b_x1, False, Arena(0), "a")
        if STAGE >= 5:
            load_row_bcast(Arena(30000), gsc_dr[1], [b_gsc], gate_b, b_gateb, "grow1")
            norm_phase(1, x1_dr, b_x1, Arena(20000))
            if not SKIP_L1:
                l1_phase()
            else:
                ar0 = Arena(0)
                zc, b_zc = ar0.bf16(T)
                memset("dve", zc, 0.0, [b_zc])
                for u in range(KC):
                    store_y(u, zc, b_zc)
            if not os.environ.get("KNOFINAL"):
                outproj_phase(w1out, x1_dr, b_x1, y_out, None, True, Arena(0), "b")
        if DEBUG:
            d_d = S.dsem("ddbg")
            dma("sp", dbg["mod"], modT[:], [b_mod], [], d_d)
            dma("sp", dbg["hT"], hT[:].rearrange("p k t -> p (k t)"), b_hT, [], d_d)
        S.emit()
    return nc, S


_CACHE = {}


def _rope_tables(identity):
    cos = np.ones((64, T), np.float32)
    sin = np.zeros((64, T), np.float32)
    if not identity:
        t = np.arange(T)
        row = (t // 64).astype(np.float32)
        col = (t % 64).astype(np.float32)
        freqs = (10000.0 ** (-np.arange(16, dtype=np.float32) / 16)).astype(np.float32)
        for half, pos in ((0, row), (1, col)):
            ang = pos[None, :] * freqs[:, None]
            c, s_ = np.cos(ang), np.sin(ang)
            base = 32 * half
            cos[base:base + 16] = c
            cos[base + 16:base + 32] = c
            sin[base:base + 16] = -s_
            sin[base + 16:base + 32] = s_
    return np.concatenate([cos, cos], 0), np.concatenate([sin, sin], 0)


def _swap_matrix():
    m = np.zeros((128, 128), np.float32)
    for blk in range(4):
        b0 = 32 * blk
        for d in range(16):
            m[b0 + d + 16, b0 + d] = 1.0
            m[b0 + d, b0 + d + 16] = 1.0
    return m


def _chunk_consts():
    idx = np.arange(128)
    same = (idx[:, None] // 64) == (idx[None, :] // 64)
    tri = np.zeros((2, 128, 384), np.float32)
    msk = np.zeros((2, 128, 256), np.float32)
    for z in range(2):
        if z == 0:
            incl = idx[:, None] <= idx[None, :]
            strict = idx[:, None] < idx[None, :]
            after = idx[:, None] > idx[None, :]
        else:
            incl = idx[:, None] >= idx[None, :]
            strict = idx[:, None] > idx[None, :]
            after = idx[:, None] < idx[None, :]
        tri[z, :, 0:128] = incl & same
        tri[z, :, 128:256] = strict & same
        tri[z, :, 256:384] = after & same
        msk[z, :, 0:128] = strict & same
        msk[z, :, 128:256] = incl & same
    blk = same.astype(np.float32)
    lv = np.zeros((4, 128, 128), np.float32)
    lv[0] = (idx[:, None] // 8) == (idx[None, :] // 8)
    for li, b in enumerate((8, 16, 32)):
        lv[1 + li] = ((idx[:, None] // (2 * b)) == (idx[None, :] // (2 * b))) & ((idx[:, None] // b) != (idx[None, :] // b))
    return tri, msk, blk, lv


def _fm(vec, k):
    return np.ascontiguousarray(np.asarray(vec, np.float32).reshape(k, 128).T)


def kernel(x_prompt, x_sample, cache_l0_a_k, cache_l0_a_v, cache_l0_mla_ckv, cache_l0_mla_kpe,
           state_l1_fwd, state_l1_bwd, c, c_ctx, mod_w, mod_b, norm_g, final_norm_g,
           l0_w_in, l0_w_out, l0_diff_lambda, l0_subln_g, l0_q_norm_g, l0_w_uq, l0_kv_norm_g, l0_w_ukv,
           l1_w_in, l1_w_out, l1_mu, l1_w0, l1_w2, l1_a0, l1_a2, l1_k_k, l1_k_a, l1_r_k, l1_ln_w, l1_ln_b):
    f = lambda a: np.ascontiguousarray(np.asarray(a, dtype=np.float32))
    if "nc" not in _CACHE:
        _CACHE["nc"] = build_program()
    nc, S = _CACHE["nc"]
    L0B, L1B = l0_blocks(), l1_blocks()
    shared = {}
    mw = f(mod_w)
    shared["modw"] = np.ascontiguousarray(mw.reshape(2, KC, 128, 24, 256).transpose(0, 3, 2, 1, 4))
    shared["modb"] = f(mod_b).reshape(2, 1, 6144)
    shared["normg"] = np.ascontiguousarray(f(norm_g).reshape(2, KC, 128).transpose(2, 0, 1))
    shared["fng"] = f(final_norm_g).reshape(1, D)
    shared["w0in"] = pack_blocks(f(l0_w_in), L0B)
    shared["w0out"] = np.ascontiguousarray(f(l0_w_out).reshape(KC, 128, 8, 256).transpose(2, 1, 0, 3))
    wuq = f(l0_w_uq)
    uqb = [("n%d" % h, 192 * h, 128) for h in range(8)] + [("p%d" % h, 192 * h + 128, 64) for h in range(8)]
    shared["wuq"] = pack_blocks(wuq, uqb)
    wukv = f(l0_w_ukv)
    kvb = [("k%d" % h, 256 * h, 128) for h in range(8)] + [("v%d" % h, 256 * h + 128, 128) for h in range(8)]
    shared["wukv"] = pack_blocks(wukv, kvb)
    shared["lam"] = f(l0_diff_lambda).reshape(1, 256)
    shared["subg"] = f(l0_subln_g).reshape(128, 1)
    shared["qng"] = _fm(l0_q_norm_g, 4)
    shared["kvng"] = _fm(l0_kv_norm_g, 2)
    shared["rm"] = _swap_matrix()
    shared["ident"] = np.eye(128, dtype=np.float32)
    shared["w1in"] = pack_blocks(f(l1_w_in), L1B)
    shared["w1out"] = np.ascontiguousarray(f(l1_w_out).reshape(KC, 128, 8, 256).transpose(2, 1, 0, 3))
    mu = f(l1_mu)
    mu_t = np.zeros((128, len(L1B), 2), np.float32)
    for i, (_, c0, n) in enumerate(L1B):
        mu_t[:n, i, :] = mu[:, c0:c0 + n].T
    shared["mu"] = mu_t
    shared["w0row"] = f(l1_w0).reshape(2, 1, 2048)
    shared["w2"] = f(l1_w2)
    shared["a2"] = f(l1_a2)
    shared["a0"] = np.ascontiguousarray(f(l1_a0).reshape(2, 16, 128).transpose(2, 0, 1))
    hp = np.stack([f(l1_k_k), f(l1_k_a), f(l1_r_k), f(l1_ln_w), f(l1_ln_b)], 0)
    shared["hpar"] = np.ascontiguousarray(hp.reshape(5, 16, 128).transpose(2, 0, 1))
    tri, msk, blk, lv = _chunk_consts()
    shared["tri"], shared["msk"], shared["blk64"], shared["lvl"] = tri, msk, blk, lv
    cos_id, sin_id = _rope_tables(True)
    cos_r, sin_r = _rope_tables(False)

    xp, xs = f(x_prompt), f(x_sample)
    ck, cv = f(cache_l0_a_k), f(cache_l0_a_v)
    cckv, ckpe = f(cache_l0_mla_ckv), f(cache_l0_mla_kpe)
    sf, sb_ = f(state_l1_fwd), f(state_l1_bwd)
    cc, cctx = f(c), f(c_ctx)
    in_maps = []
    for core in range(8):
        m = dict(shared)
        prompt = core < 4
        maskk = np.zeros((8, 1536), np.float32)
        maskq = np.zeros((8, T), np.float32)
        if prompt:
            m["x"] = np.ascontiguousarray(xp[4 * core:4 * core + 4].reshape(T, D))
            m["cond"] = _fm(cctx, KC)
            m["cos"], m["sin"] = cos_id, sin_id
            for j in range(4):
                maskk[j, 256 * j:256 * (j + 1)] = 1.0
                maskq[j, :] = NEG
                maskq[j, 256 * j:256 * (j + 1)] = 0.0
            maskk[4, 1024:] = 1.0
            maskq[4, :] = NEG
            m["ck"] = np.zeros((512, 1024), np.float32)
            m["cv"] = np.zeros((512, 1024), np.float32)
            m["cckv"] = np.zeros((512, 256), np.float32)
            m["ckpe"] = np.zeros((512, 64), np.float32)
            m["st"] = np.zeros((2, 32, 64, 64), np.float32)
            m["keep"] = np.tile(np.array([[0.0, 1.0]], np.float32), (128, 1))
        else:
            b = core - 4
            m["x"] = np.ascontiguousarray(xs[b])
            m["cond"] = _fm(cc[b], KC)
            m["cos"], m["sin"] = cos_r, sin_r
            maskk[0, :] = 1.0
            m["ck"] = np.ascontiguousarray(ck[b].reshape(512, 1024))
            m["cv"] = np.ascontiguousarray(cv[b].reshape(512, 1024))
            m["cckv"] = np.ascontiguousarray(cckv[b])
            m["ckpe"] = np.ascontiguousarray(ckpe[b])
            m["st"] = np.ascontiguousarray(np.stack([sf[b], sb_[b]], 0))
            m["keep"] = np.tile(np.array([[1.0, 0.0]], np.float32), (128, 1))
        m["maskk"], m["maskq"] = maskk, maskq
        in_maps.append(m)
    if FAKE:
        for m in in_maps:
            m["modw"] = m["modw"][:, 0:1]
            m["w0in"] = m["w0in"][0:4]
            m["w1in"] = m["w1in"][0:4]
            m["w0out"] = m["w0out"][0:1]
            m["w1out"] = m["w1out"][0:1]
    res = run_bass_kernel_spmd(nc, in_maps, core_ids=list(range(8)))
    R = res.results
    _CACHE["last"] = R
    y_prompt = np.stack([R[cidx]["y"].reshape(4, 256, D) for cidx in range(4)], 0).reshape(16, 256, D)
    y_sample = np.stack([R[4 + b]["y"] for b in range(4)], 0)
    nk = np.concatenate([R[cidx]["nk"].reshape(4, 256, 8, 2, 64) for cidx in range(4)], 0)
    nv = np.concatenate([R[cidx]["nv"].reshape(4, 256, 8, 128) for cidx in range(4)], 0)
    nckv = np.concatenate([R[cidx]["nckv"].reshape(4, 256, 256) for cidx in range(4)], 0)
    nkpe = np.concatenate([R[cidx]["nkpe"].reshape(4, 256, 64) for cidx in range(4)], 0)
    sfo = np.concatenate([R[cidx]["sst"][0] for cidx in range(4)], 0)
    sbo = np.concatenate([R[cidx]["sst"][1] for cidx in range(4)], 0)
    out = (y_prompt, y_sample, nk, nv, nckv, nkpe, sfo, sbo)
    return tuple(np.ascontiguousarray(o.astype(np.float32)) for o in out)
```

```python
import math
import os
from contextlib import ExitStack

import numpy as np
import concourse.bass as bass
import concourse.mybir as mybir
from concourse.bass_utils import run_bass_kernel_spmd

F32 = mybir.dt.float32
BF16 = mybir.dt.bfloat16
AF = mybir.ActivationFunctionType
ALU = mybir.AluOpType
AX = mybir.AxisListType

ENGS = ("pe", "act", "dve", "pool", "sp")
SAME_ENG_SYNC = {"act", "dve", "pool"}
RAW_ONLY = True

T = 1024
D = 2048
NT = 8
KC = 16
NEG = -30000.0


class Buf:
    __slots__ = ("name", "last_w", "readers", "excl")

    def __init__(self, name):
        self.name = name
        self.last_w = None
        self.readers = []
        self.excl = False


class DSem:
    __slots__ = ("h", "count", "max_wait", "name", "twin")

    def __init__(self, name):
        self.name = name
        self.h = None
        self.count = 0
        self.max_wait = 0
        self.twin = None


class Op:
    __slots__ = ("eng", "fn", "deps", "signal", "dma", "dsem", "dval", "val", "waits", "pre")

    def __init__(self, eng, fn):
        self.eng = eng
        self.fn = fn
        self.deps = []
        self.signal = False
        self.dma = False
        self.dsem = None
        self.dval = 0
        self.val = 0
        self.waits = []
        self.pre = []


class Sched:
    def __init__(self, nc):
        self.nc = nc
        self.ops = []
        self.dsems = []
        self.last = {e: None for e in ENGS}
        self._force = False

    def buf(self, name):
        return Buf(name)

    def dsem(self, name):
        d = DSem(name)
        self.dsems.append(d)
        return d

    def _dep_on(self, op, d, raw=True):
        if d is None or d is op:
            return
        if (not raw) and RAW_ONLY and d.eng == op.eng and not op.dma and not d.dma and not self._force:
            return
        if d.dma:
            v = d.dsem.count
            d.dsem.max_wait = max(d.dsem.max_wait, v)
            op.pre.append((d.dsem, v))
        else:
            if d.eng == op.eng and not op.dma and d.eng not in SAME_ENG_SYNC and not self._force:
                return
            op.deps.append(d)

    @staticmethod
    def _flat(xs):
        out = []
        for x in xs:
            if isinstance(x, (list, tuple)):
                out.extend(Sched._flat(x))
            else:
                out.append(x)
        return out

    def add(self, eng, fn, reads=(), writes=(), dma=False, dsem=None, force=False):
        reads = self._flat(reads)
        writes = self._flat(writes)
        xr = [b for b in reads if b.excl]
        if xr:
            reads = [b for b in reads if not b.excl]
            writes = writes + [b for b in xr if b not in writes]
        op = Op(eng, fn)
        op.dma = dma
        self._force = force
        seen = set()
        for b in reads:
            if b.last_w is not None and id(b.last_w) not in seen:
                seen.add(id(b.last_w))
                self._dep_on(op, b.last_w, True)
        for b in writes:
            if b.last_w is not None and id(b.last_w) not in seen:
                seen.add(id(b.last_w))
                self._dep_on(op, b.last_w, b in xr)
            for r in b.readers:
                if id(r) not in seen:
                    seen.add(id(r))
                    self._dep_on(op, r, False)
        if dma:
            if eng == "pool":
                if dsem.twin is None:
                    dsem.twin = self.dsem(dsem.name + "_sw")
                dsem = dsem.twin
            op.dsem = dsem
            if dsem.max_wait > 0:
                op.pre.append((dsem, dsem.max_wait))
            dsem.count += 16
            op.dval = dsem.count
        for b in writes:
            b.last_w = op
            b.readers = []
        for b in reads:
            if b.last_w is not op:
                b.readers.append(op)
        self.ops.append(op)
        if not dma:
            self.last[eng] = op
        return op

    def fence(self):
        lasts = dict(self.last)
        for e in ENGS:
            op = Op(e, lambda eng: eng.nop())
            for e2 in ENGS:
                if e2 != e and lasts[e2] is not None:
                    op.deps.append(lasts[e2])
            for ds in self.dsems:
                if ds.count > 0:
                    ds.max_wait = max(ds.max_wait, ds.count)
                    op.pre.append((ds, ds.count))
            self.ops.append(op)
            self.last[e] = op

    def emit(self):
        nc = self.nc
        seqc = {e: 0 for e in ENGS}
        for op in self.ops:
            if not op.dma:
                seqc[op.eng] += 1
                op.val = seqc[op.eng]
        known = {e: {} for e in ENGS}
        snap = {}
        per_eng = {e: [] for e in ENGS}
        for op in self.ops:
            k = known[op.eng]
            need_e = {}
            for d in op.deps:
                if need_e.get(d.eng, (0, None))[0] < d.val:
                    need_e[d.eng] = (d.val, d)
            need_d = {}
            for (ds, v) in op.pre:
                if id(ds) not in need_d or need_d[id(ds)][1] < v:
                    need_d[id(ds)] = (ds, v)
            fin = []
            for e2, (sq, d) in sorted(need_e.items(), key=lambda kv: -kv[1][0]):
                if k.get(("E", e2), 0) >= sq:
                    continue
                fin.append(d)
                d.signal = True
                k[("E", e2)] = sq
                for key, v in snap.get(id(d), {}).items():
                    if k.get(key, 0) < v:
                        k[key] = v
            for key, (ds, v) in need_d.items():
                if k.get(("D", key), 0) >= v:
                    continue
                fin.append((ds, v))
                k[("D", key)] = v
            op.waits = fin
            if not op.dma:
                sn = dict(k)
                sn[("E", op.eng)] = op.val
                snap[id(op)] = sn
            per_eng[op.eng].append(op)
        snap.clear()
        cnt = {e: 0 for e in ENGS}
        for op in self.ops:
            if not op.dma:
                if op.signal:
                    cnt[op.eng] += 1
                    op.val = cnt[op.eng]
                else:
                    op.val = -1
        self.nsig = dict(cnt)
        with ExitStack() as es:
            esem = {e: es.enter_context(nc.semaphore("es_" + e)) for e in ENGS}
            for i, ds in enumerate(self.dsems):
                if ds.count > 0:
                    ds.h = es.enter_context(nc.semaphore("ds%d" % i))
            block = es.enter_context(nc.Block())

            def run(eng_name):
                def body(eng):
                    for op in per_eng[eng_name]:
                        for w in op.waits:
                            if isinstance(w, Op):
                                eng.wait_ge(esem[w.eng], w.val)
                            else:
                                eng.wait_ge(w[0].h, w[1])
                        ins = op.fn(eng)
                        if op.dma:
                            ins.then_inc(op.dsem.h, 16)
                        elif op.signal:
                            ins.then_inc(esem[eng_name], 1)
                    if eng_name == "sp":
                        for ds in self.dsems:
                            if ds.count > 0:
                                eng.wait_ge(ds.h, ds.count)
                        for e in ENGS:
                            if e != "sp" and cnt[e] > 0:
                                eng.wait_ge(esem[e], cnt[e])
                return body

            block.tensor(run("pe"))
            block.scalar(run("act"))
            block.vector(run("dve"))
            block.gpsimd(run("pool"))
            block.sync(run("sp"))
        self.stats = {e: len(per_eng[e]) for e in ENGS}


A_HEADS = 8
B_HEADS = 8
B_Q_LORA = 512
B_KV_LORA = 256
L0_IN = 5952
L1_IN = 8576
C_HEADS = 32
LORA = 96
NORM_EPS = 1e-6
GN_EPS = 64e-5
LAM_INIT0 = 0.8 - 0.6 * math.exp(-0.3 * 0)

DEBUG = bool(int(os.environ.get("KDEBUG", "0")))
LAYOUT = {}
SKIP_L1 = bool(int(os.environ.get("KSKIP_L1", "0")))
FAKE = bool(int(os.environ.get("KFAKE", "0")))


def l0_blocks():
    blks = []
    for i in range(4):
        blks.append(("cq%d" % i, 3072 + 128 * i, 128))
    for i in range(2):
        blks.append(("ckv%d" % i, 3584 + 128 * i, 128))
    blks.append(("kpe", 3840, 64))
    for h in range(8):
        blks.append(("q%d" % h, 128 * h, 128))
        blks.append(("k%d" % h, 1024 + 128 * h, 128))
        blks.append(("v%d" % h, 2048 + 128 * h, 128))
        blks.append(("g%d" % h, 3904 + 128 * h, 128))
    for h in range(8):
        blks.append(("g%d" % (8 + h), 3904 + 128 * (8 + h), 128))
    return blks


def l1_blocks():
    blks = []
    base = 4 * 2048
    for z in range(2):
        blks.append(("wd%d" % z, base + LORA * z, LORA))
    for z in range(2):
        blks.append(("ad%d" % z, base + 2 * LORA + LORA * z, LORA))
    for hp in range(16):
        blks.append(("r%d" % hp, 128 * hp, 128))
        blks.append(("k%d" % hp, 2048 + 128 * hp, 128))
        blks.append(("v%d" % hp, 4096 + 128 * hp, 128))
        blks.append(("g%d" % hp, 6144 + 128 * hp, 128))
    return blks


def pack_blocks(w, blks):
    kk = w.shape[0] // 128
    out = np.zeros((len(blks), 128, kk, 128), np.float32)
    w3 = w.reshape(kk, 128, w.shape[1])
    for i, (_, c0, n) in enumerate(blks):
        out[i, :, :, :n] = w3[:, :, c0:c0 + n].transpose(1, 0, 2)
    return out


def build_program():
    nc = bass.Bass("TRN2", target_bir_lowering=False)
    S = Sched(nc)
    L0B = l0_blocks()
    L1B = l1_blocks()

    def din(name, shape, dt=F32):
        return nc.dram_tensor(name, list(shape), dt, kind="ExternalInput").ap()

    def dout(name, shape, dt=F32):
        return nc.dram_tensor(name, list(shape), dt, kind="ExternalOutput").ap()

    x_in = din("x", [T, D])
    cond_in = din("cond", [128, KC])
    NMOD = 1 if FAKE else 24
    NB0 = 4 if FAKE else len(L0B)
    NB1 = 4 if FAKE else len(L1B)
    NOUT = 1 if FAKE else 8
    modw_in = din("modw", [2, NMOD, 128, KC, 256])
    modb_in = din("modb", [2, 1, 6144])
    normg_in = din("normg", [128, 2, KC])
    fng_in = din("fng", [1, D])
    w0in = din("w0in", [NB0, 128, KC, 128])
    w0out = din("w0out", [NOUT, 128, KC, 256])
    wuq_in = din("wuq", [16, 128, 4, 128])
    wukv_in = din("wukv", [16, 128, 2, 128])
    lam_in = din("lam", [1, 256])
    subg_in = din("subg", [128, 1])
    qng_in = din("qng", [128, 4])
    kvng_in = din("kvng", [128, 2])
    cos_in = din("cos", [128, T])
    sin_in = din("sin", [128, T])
    rm_in = din("rm", [128, 128])
    ident_in = din("ident", [128, 128])
    maskk_in = din("maskk", [8, 1536])
    maskq_in = din("maskq", [8, T])
    ck_in = din("ck", [512, 1024])
    cv_in = din("cv", [512, 1024])
    cckv_in = din("cckv", [512, 256])
    ckpe_in = din("ckpe", [512, 64])
    w1in = din("w1in", [NB1, 128, KC, 128])
    w1out = din("w1out", [NOUT, 128, KC, 256])
    mu_in = din("mu", [128, len(L1B), 2])
    w0row_in = din("w0row", [2, 1, 2048])
    w2_in = din("w2", [2, LORA, 2048])
    a2_in = din("a2", [2, LORA, 2048])
    a0_in = din("a0", [128, 2, 16])
    hpar_in = din("hpar", [128, 5, 16])
    st_in = din("st", [2, 32, 64, 64])
    keep_in = din("keep", [128, 2])
    tri_in = din("tri", [2, 128, 384])
    msk_in = din("msk", [2, 128, 256])
    blk64_in = din("blk64", [128, 128])
    lvl_in = din("lvl", [4, 128, 128])
    y_out = dout("y", [T, D])
    nk_out = dout("nk", [T, 1024])
    nv_out = dout("nv", [T, 1024])
    nckv_out = dout("nckv", [T, 256])
    nkpe_out = dout("nkpe", [T, 64])
    sst_out = dout("sst", [2, 4, 32, 64, 64])
    x1_dr = nc.dram_tensor("x1s", [T, D], F32, kind="Internal").ap()
    gsc_dr = nc.dram_tensor("gsc", [2, 1, D], F32, kind="Internal").ap()
    yT_dr = nc.dram_tensor("yTs", [KC, 128, T], BF16, kind="Internal").ap()
    dbg = {}
    if DEBUG:
        dbg["hT"] = dout("dbg_hT", [128, KC * T], BF16)
        dbg["mod"] = dout("dbg_mod", [128, 96])

    with ExitStack() as es:
        def sbt(name, shape, dt):
            return es.enter_context(nc.sbuf_tensor("s_" + name, list(shape), dt))

        hT = sbt("hT", [128, KC, T], BF16)
        b_hT = [S.buf("hT%d" % i) for i in range(NT)]
        b_yT = [S.buf("yTdr%d" % u) for u in range(KC)]
        d_yT = [S.dsem("dyT%d" % u) for u in range(KC)]
        NSLOT = 4
        wring = sbt("wring", [128, NSLOT * KC * 128], BF16)
        b_w = [S.buf("w%d" % i) for i in range(NSLOT)]
        d_w = [S.dsem("dw%d" % i) for i in range(NSLOT)]
        ident32 = sbt("ident32", [128, 128], F32)
        ident16 = sbt("ident16", [128, 128], BF16)
        ones16 = sbt("ones16", [128, 128], BF16)
        ones32 = sbt("ones32", [128, 128], F32)
        modT = sbt("modT", [128, 96], F32)
        normg = sbt("normg", [128, 2, KC], F32)
        modA = sbt("modA", [128, 2, KC], F32)
        gate_b = sbt("gate_b", [128, D], F32)
        eps_t = sbt("eps_t", [128, 4], F32)
        keep_t = sbt("keep_t", [128, 2], F32)
        b_const = S.buf("const")
        b_mod = S.buf("mod")
        b_gateb = S.buf("gateb")
        b_gsc = S.buf("gsc")
        d_c = S.dsem("dconst")
        ARENA_W = 38144
        arena = sbt("arena", [128, ARENA_W], F32)
        psum = [es.enter_context(nc.psum_tensor("ps%d" % i, [128, 512], F32)) for i in range(8)]
        b_ps = [S.buf("ps%d" % i) for i in range(8)]
        for b_ in b_ps:
            b_.excl = True

        def dma(eng, out, in_, reads, writes, dsem):
            S.add(eng, lambda e: e.dma_start(out=out, in_=in_), reads=reads, writes=writes, dma=True, dsem=dsem)

        last_rg = {}

        def mm(out, lhsT, rhs, start, stop, reads, writes, rg=None):
            force = False
            for w in Sched._flat(writes):
                if w.excl:
                    if last_rg.get(id(w), rg) != rg:
                        force = True
                    last_rg[id(w)] = rg
            S.add("pe", lambda e: e.matmul(out, lhsT=lhsT, rhs=rhs, start=start, stop=stop), reads=reads, writes=writes, force=force)

        def tr(out, in_, idn, reads, writes):
            S.add("pe", lambda e: e.transpose(out, in_, idn), reads=reads, writes=writes)

        def act(out, in_, func, reads, writes, bias=None, scale=None, accum=None):
            kw = {}
            if bias is not None:
                kw["bias"] = bias
            if scale is not None:
                kw["scale"] = scale
            if accum is not None:
                kw["accum_out"] = accum
            S.add("act", lambda e: e.activation(out=out, in_=in_, func=func, **kw), reads=reads, writes=writes)

        def tt(eng, out, in0, in1, op, reads, writes):
            S.add(eng, lambda e: e.tensor_tensor(out=out, in0=in0, in1=in1, op=op), reads=reads, writes=writes)

        def ts(eng, out, in0, s1, s2, op0, op1, reads, writes):
            if s2 is None:
                S.add(eng, lambda e: e.tensor_scalar(out=out, in0=in0, scalar1=s1, scalar2=None, op0=op0), reads=reads, writes=writes)
            else:
                S.add(eng, lambda e: e.tensor_scalar(out=out, in0=in0, scalar1=s1, scalar2=s2, op0=op0, op1=op1), reads=reads, writes=writes)

        def stt(eng, out, in0, scalar, in1, op0, op1, reads, writes):
            S.add(eng, lambda e: e.scalar_tensor_tensor(out=out, in0=in0, scalar=scalar, in1=in1, op0=op0, op1=op1), reads=reads, writes=writes)

        def cp(eng, out, in_, reads, writes):
            if eng == "act":
                S.add("act", lambda e: e.copy(out=out, in_=in_), reads=reads, writes=writes)
            else:
                S.add(eng, lambda e: e.tensor_copy(out=out, in_=in_), reads=reads, writes=writes)

        def recip(out, in_, reads, writes):
            S.add("dve", lambda e: e.reciprocal(out=out, in_=in_), reads=reads, writes=writes)

        def memset(eng, ap, val, writes):
            S.add(eng, lambda e: e.memset(ap, val), writes=writes)

        bank_rr = [0]

        def bank(lo=4, hi=8):
            n = hi - lo
            i = lo + (bank_rr[0] % n)
            bank_rr[0] += 1
            return i

        wslot = [0]

        def wload(src_ap, nwords):
            i = wslot[0] % NSLOT
            wslot[0] += 1
            dst = wring[:, i * KC * 128: i * KC * 128 + nwords]
            dma("pool", dst, src_ap, [], [b_w[i]], d_w[i])
            return i

        def wview(i, kk, n=128):
            return wring[:, i * KC * 128: i * KC * 128 + kk * n].rearrange("p (k c) -> p k c", c=n)

        REG = 256
        region_bufs = [S.buf("ar%d" % i) for i in range((ARENA_W + REG - 1) // REG)]

        class Arena:
            def __init__(self, off=0):
                self.off = off
                self.peak = off
                self.log = []

            def _take(self, w):
                w = ((w + REG - 1) // REG) * REG
                o = self.off
                self.off += w
                self.peak = max(self.peak, self.off)
                self.log.append(o)
                assert self.off <= ARENA_W or os.environ.get("KNOASSERT"), ("arena overflow", self.off)
                return o, region_bufs[o // REG:(o + w) // REG]

            def f32(self, n):
                o, bufs = self._take(n)
                return arena[:, o:o + n], bufs

            def bf16(self, n):
                w = (n + 1) // 2
                o, bufs = self._take(w)
                return arena[:, o:o + w].bitcast(BF16)[:, 0:n], bufs

        dma("sp", ident32[:], ident_in, [], [b_const], d_c)
        dma("pool", ident16[:], ident_in, [], [b_const], d_c)
        dma("sp", normg[:], normg_in, [], [b_const], d_c)
        dma("sp", keep_t[:], keep_in, [], [b_const], d_c)
        memset("dve", ones16[:], 1.0, [b_const])
        memset("dve", ones32[:], 1.0, [b_const])
        memset("dve", eps_t[:, 0:1], NORM_EPS, [b_const])
        memset("dve", eps_t[:, 1:2], GN_EPS, [b_const])
        memset("dve", eps_t[:, 2:3], 1e-12, [b_const])
        memset("dve", eps_t[:, 3:4], 0.0, [b_const])

        def wload3(src_ap3, kk, n=128):
            i = wslot[0] % NSLOT
            wslot[0] += 1
            dst = wview(i, kk, n)
            dma("pool", dst, src_ap3, [], [b_w[i]], d_w[i])
            return i

        def bcast_row(dst_tile, row_ap, reads, wbuf):
            for q in range(4):
                pb = bank()
                mm(psum[pb][:, :], ones32[0:1, :], row_ap[0:1, q * 512:(q + 1) * 512], True, True,
                   reads + [b_const], [b_ps[pb]])
                cp("act", dst_tile[:, q * 512:(q + 1) * 512], psum[pb][:, :], [b_ps[pb]], [wbuf])

        def modulation3():
            ar = Arena()
            cond_t, b_cd = ar.f32(KC)
            scond, b_sc = ar.bf16(KC)
            mrow, b_mr = ar.f32(6144)
            modb_t, b_mb = ar.f32(6144)
            d_l = S.dsem("dmod")
            d_g = S.dsem("dgsc")
            dma("sp", cond_t, cond_in, [], [b_cd], d_l)
            act(scond, cond_t, AF.Silu, [b_cd], [b_sc])
            for l in range(2):
                dma("sp", modb_t[0:1, :], modb_in[l], [], [b_mb], d_l)
                for blk in range(24):
                    pb = bank()
                    for half in range(2):
                        si = wload3(modw_in[l, blk % NMOD, :, :, half * 128:(half + 1) * 128], KC)
                        wv = wview(si, KC)
                        for k in range(KC):
                            mm(psum[pb][0:1, half * 128:(half + 1) * 128], scond[:, k:k + 1], wv[:, k, :],
                               k == 0, k == KC - 1, [b_w[si], b_sc], [b_ps[pb]])
                    tt("dve", mrow[0:1, blk * 256:(blk + 1) * 256], psum[pb][0:1, 0:256],
                       modb_t[0:1, blk * 256:(blk + 1) * 256], ALU.add, [b_ps[pb], b_mb], [b_mr])
                pb = bank()
                for c in range(48):
                    mm(psum[pb][:, c:c + 1], mrow[0:1, c * 128:(c + 1) * 128], ones32[0:1, 0:1], True, True,
                       [b_mr, b_const], [b_ps[pb]])
                cp("dve", modT[:, 48 * l:48 * l + 48], psum[pb][:, 0:48], [b_ps[pb]], [b_mod])
                ts("dve", modA[:, l, :], modT[:, 48 * l + 16:48 * l + 32], 1.0, None, ALU.add, None, [b_mod], [b_mod])
                tt("dve", modA[:, l, :], modA[:, l, :], normg[:, l, :], ALU.mult, [b_mod, b_const], [b_mod])
                dma("sp", gsc_dr[l], mrow[0:1, 2 * D:3 * D], [b_mr], [b_gsc], d_g)

        def load_row_bcast(ar, src_dram_row, src_bufs, dst_tile, dst_buf, name):
            grow, b_g = ar.f32(D)
            d_gr = S.dsem("d" + name)
            dma("sp", grow[0:1, :], src_dram_row, src_bufs, [b_g], d_gr)
            bcast_row(dst_tile, grow, [b_g], dst_buf)

        def norm_phase(l, x_src, x_src_bufs, ar):
            xt = [ar.f32(D), ar.f32(D)]
            xn = [ar.bf16(D), ar.bf16(D)]
            junk, b_j = ar.bf16(D)
            st, b_st = ar.f32(4 * NT)
            d_x = [S.dsem("dnx%d_%d" % (l, i)) for i in range(2)]
            for i in range(NT):
                s = i % 2
                xs, b_x = xt[s]
                xb, b_xn = xn[s]
                dma("sp", xs, x_src[i * 128:(i + 1) * 128, :], [x_src_bufs[i]] if x_src_bufs else [], [b_x], d_x[s])
                act(junk, xs, AF.Square, [b_x], [b_j, b_st], accum=st[:, 4 * i:4 * i + 1])
                act(st[:, 4 * i + 1:4 * i + 2], st[:, 4 * i:4 * i + 1], AF.Sqrt, [b_st, b_const], [b_st],
                    bias=eps_t[:, 0:1], scale=1.0 / D)
                recip(st[:, 4 * i + 2:4 * i + 3], st[:, 4 * i + 1:4 * i + 2], [b_st], [b_st])
                ts("dve", xb, xs, st[:, 4 * i + 2:4 * i + 3], None, ALU.mult, None, [b_x, b_st], [b_xn])
                for g in range(4):
                    pb = bank()
                    pv = psum[pb][:, :].bitcast(BF16)
                    for j in range(4):
                        k = 4 * g + j
                        tr(pv[:, j * 128:(j + 1) * 128], xb[:, k * 128:(k + 1) * 128], ident16[:, :],
                           [b_xn, b_const], [b_ps[pb]])
                    for j in range(4):
                        k = 4 * g + j
                        act(hT[:, k, i * 128:(i + 1) * 128], pv[:, j * 128:(j + 1) * 128], AF.Identity,
                            [b_ps[pb], b_mod], [b_hT[i]], bias=modT[:, 48 * l + k:48 * l + k + 1],
                            scale=modA[:, l, k:k + 1])

        def outproj_phase(wout_in, x_src, x_src_bufs, x_dst, x_dst_bufs, final, ar, tag):
            yTa, b_ya = ar.bf16(KC * T)
            yT = yTa.rearrange("p (u t) -> p u t", t=T)
            d_ya = S.dsem("dya" + tag)
            for u in range(KC):
                dma("sp", yT[:, u, :], yT_dr[u], [b_yT[u]], [b_ya], d_ya)
            fng_b = None
            if final:
                fng_b, b_fng = ar.f32(D)
                load_row_bcast(ar, fng_in, [], fng_b, b_fng, "fng")
            xt = [ar.f32(D) for _ in range(4)]
            tmp, b_tmp = ar.f32(256)
            junk, b_j = ar.bf16(D)
            st, b_st = ar.f32(4 * NT)
            d_x = [S.dsem("dox%s%d" % (tag, i)) for i in range(4)]
            d_o = [S.dsem("doo%s%d" % (tag, i)) for i in range(4)]
            for grp in range(2):
                tiles = list(range(4 * grp, 4 * grp + 4))
                for fb in range(8):
                    sl = [wload3(wout_in[fb % NOUT, :, :, half * 128:(half + 1) * 128], KC) for half in range(2)]
                    for i in tiles:
                        s = i % 4
                        xs, b_x = xt[s]
                        if fb == 0:
                            dma("sp", xs, x_src[i * 128:(i + 1) * 128, :], [x_src_bufs[i]] if x_src_bufs else [], [b_x], d_x[s])
                        pb = bank()
                        for half in range(2):
                            wv = wview(sl[half], KC)
                            for u in range(KC):
                                mm(psum[pb][:, half * 128:(half + 1) * 128], yT[:, u, i * 128:(i + 1) * 128], wv[:, u, :],
                                   u == 0, u == KC - 1, [b_w[sl[half]], b_ya], [b_ps[pb]])
                        tt("dve", tmp[:, 0:256], psum[pb][:, 0:256], gate_b[:, fb * 256:(fb + 1) * 256], ALU.mult,
                           [b_ps[pb], b_gateb], [b_tmp])
                        tt("pool", xs[:, fb * 256:(fb + 1) * 256], xs[:, fb * 256:(fb + 1) * 256], tmp[:, 0:256],
                           ALU.add, [b_tmp, b_x], [b_x])
                        if fb == 7:
                            if final:
                                act(junk, xs, AF.Square, [b_x], [b_j, b_st], accum=st[:, 4 * i:4 * i + 1])
                                act(st[:, 4 * i + 1:4 * i + 2], st[:, 4 * i:4 * i + 1], AF.Sqrt, [b_st, b_const], [b_st],
                                    bias=eps_t[:, 0:1], scale=1.0 / D)
                                recip(st[:, 4 * i + 2:4 * i + 3], st[:, 4 * i + 1:4 * i + 2], [b_st], [b_st])
                                stt("dve", xs, xs, st[:, 4 * i + 2:4 * i + 3], fng_b[:, :], ALU.mult, ALU.mult,
                                    [b_x, b_st, b_fng], [b_x])
                            dma("sp", x_dst[i * 128:(i + 1) * 128, :], xs, [b_x], [x_dst_bufs[i]] if x_dst_bufs else [], d_o[s])

        def store_y(u, yc, b_yc):
            dma("sp", yT_dr[u], yc, [b_yc], [b_yT[u]], d_yT[u])

        def l0_phase():
            ar = Arena()
            cos_t, b_cos = ar.f32(T)
            sin_t, b_sin = ar.f32(T)
            rm_t, b_rm = ar.f32(128)
            mk16, b_mk = ar.bf16(1536)
            mq16, b_mq = ar.bf16(T)
            b_tab = [b_cos, b_sin, b_rm, b_mk, b_mq]
            sm, b_sm = ar.f32(16)
            lamrow, b_lr = ar.f32(272)
            cqg16a, b_cq = ar.bf16(4 * T)
            cqg16 = cqg16a.rearrange("p (k t) -> p k t", t=T)
            rstdq, b_rq = ar.f32(T)
            rkv, b_rkv = ar.f32(T)
            ckvn16a, b_ckv = ar.bf16(2 * 1536)
            ckvn16 = ckvn16a.rearrange("p (k t) -> p k t", t=1536)
            kpe16, b_kpe = ar.bf16(1536)
            qpe16, b_qpe = ar.bf16(T)
            stAa, b_stA = ar.f32(2 * T)
            stA = stAa.rearrange("p (k t) -> p k t", t=T)
            b_st = [b_stA[0:4], b_stA[4:8]]
            sqs = [ar.bf16(512), ar.bf16(512)]
            sq16 = [sqs[0][0], sqs[1][0]]
            b_sq = [sqs[0][1], sqs[1][1]]
            tmpa, b_ta = ar.f32(512)
            tmpb, b_tb = ar.f32(512)
            ctxs, b_ctx = ar.f32(1024)
            ost, b_ost = ar.f32(512)
            q16, b_q = ar.bf16(T)
            K16, b_K = ar.bf16(1536)
            V16a, b_V = ar.bf16(12 * 128)
            V16 = V16a.rearrange("p (k c) -> p k c", c=128)
            vst, b_vst = ar.f32(512)
            gT16, b_g = ar.bf16(T)
            pts = [ar.bf16(512) for _ in range(4)]
            PT = [p[0] for p in pts]
            b_PT = [p[1] for p in pts]
            os_ = [ar.f32(512) for _ in range(3)]
            o32 = [p[0] for p in os_]
            b_o = [p[1] for p in os_]
            ycs = [ar.bf16(T), ar.bf16(T)]
            d_tab = S.dsem("dl0tab")
            d_ctx = S.dsem("dctx")
            d_ost = S.dsem("dost")
            d_V = S.dsem("dV16")
            d_vst = S.dsem("dvst")
            bi_of = {name: i for i, (name, _, _) in enumerate(L0B)}
            sqr = [0]
            ptr = [0]

            dma("sp", cos_t, cos_in, [], [b_tab], d_tab)
            dma("sp", sin_t, sin_in, [], [b_tab], d_tab)
            dma("sp", rm_t, rm_in, [], [b_tab], d_tab)
            dma("pool", mk16[0:8, :], maskk_in, [], [b_tab], d_tab)
            dma("pool", mq16[0:8, :], maskq_in, [], [b_tab], d_tab)
            dma("pool", kpe16[64:72, :], maskk_in, [], [b_kpe], d_tab)
            dma("pool", qpe16[64:72, :], maskq_in, [], [b_qpe], d_tab)
            dma("sp", sm[:, 1:2], subg_in, [], [b_sm], d_tab)
            dma("sp", sm[:, 2:6], qng_in, [], [b_sm], d_tab)
            dma("sp", sm[:, 6:8], kvng_in, [], [b_sm], d_tab)
            dma("sp", lamrow[0:1, 0:256], lam_in, [], [b_lr], d_tab)
            tt("dve", lamrow[0:1, 0:64], lamrow[0:1, 0:64], lamrow[0:1, 64:128], ALU.mult, [b_lr], [b_lr])
            tt("dve", lamrow[0:1, 128:192], lamrow[0:1, 128:192], lamrow[0:1, 192:256], ALU.mult, [b_lr], [b_lr])
            S.add("dve", lambda e: e.reduce_sum(out=lamrow[0:1, 256:257], in_=lamrow[0:1, 0:64], axis=AX.X), reads=[b_lr], writes=[b_lr])
            S.add("dve", lambda e: e.reduce_sum(out=lamrow[0:1, 257:258], in_=lamrow[0:1, 128:192], axis=AX.X), reads=[b_lr], writes=[b_lr])
            act(lamrow[0:1, 258:260], lamrow[0:1, 256:258], AF.Exp, [b_lr], [b_lr])
            tt("dve", lamrow[0:1, 260:261], lamrow[0:1, 259:260], lamrow[0:1, 258:259], ALU.subtract, [b_lr], [b_lr])
            ts("dve", lamrow[0:1, 261:262], lamrow[0:1, 260:261], -LAM_INIT0, None, ALU.add, None, [b_lr], [b_lr])
            pb = bank()
            mm(psum[pb][:, 0:1], ones32[0:1, :], lamrow[0:1, 261:262], True, True, [b_lr, b_const], [b_ps[pb]])
            cp("dve", sm[:, 0:1], psum[pb][:, 0:1], [b_ps[pb]], [b_sm])
            ts("dve", sm[:, 1:2], sm[:, 1:2], 1.0 - LAM_INIT0, None, ALU.mult, None, [b_sm], [b_sm])

            L0S = float(os.environ.get("KL0", "9"))
            if L0S <= 1:
                return

            def inproj_fm(win, bi, ncols, evac):
                si = wload3(win[bi % NB0], KC)
                wv = wview(si, KC)
                for half in range(2):
                    pb = bank()
                    for k in range(KC):
                        mm(psum[pb][0:ncols, :], wv[:, k, 0:ncols], hT[:, k, half * 512:(half + 1) * 512],
                           k == 0, k == KC - 1, [b_w[si]] + b_hT[4 * half:4 * half + 4], [b_ps[pb]])
                    evac(half, pb)

            def rope(src, bsrc, dst, bdst, nrow):
                for half in range(2):
                    hs = slice(half * 512, (half + 1) * 512)
                    pb = bank()
                    mm(psum[pb][0:nrow, :], rm_t[0:nrow, 0:nrow], src[0:nrow, hs], True, True, [bsrc, b_tab], [b_ps[pb]])
                    tt("pool", tmpa[0:nrow, :], src[0:nrow, hs], cos_t[0:nrow, hs], ALU.mult, [bsrc, b_tab], [b_ta])
                    tt("dve", tmpb[0:nrow, :], psum[pb][0:nrow, :], sin_t[0:nrow, hs], ALU.mult, [b_ps[pb], b_tab], [b_tb])
                    tt("pool", dst[0:nrow, hs], tmpa[0:nrow, :], tmpb[0:nrow, :], ALU.add, [b_ta, b_tb], [bdst])

            def rstd_from(pbank, out_ap, n, wbuf, npart=128):
                act(out_ap, psum[pbank][0:npart, :], AF.Sqrt, [b_ps[pbank], b_const], [wbuf], bias=eps_t[0:npart, 0:1], scale=1.0 / n)
                recip(out_ap, out_ap, [wbuf], [wbuf])

            for b in range(4):
                def ev(half, pb, b=b):
                    r = sqr[0] % 2
                    sqr[0] += 1
                    act(sq16[r], psum[pb][:, :], AF.Square, [b_ps[pb]], [b_sq[r]])
                    ts("dve", cqg16[:, b, half * 512:(half + 1) * 512], psum[pb][:, :], sm[:, 2 + b:3 + b], None, ALU.mult, None,
                       [b_ps[pb], b_sm], [b_cq])
                    mm(psum[half][:, :], ones16[:, :], sq16[r], b == 0, b == 3, [b_sq[r], b_const], [b_ps[half]])
                inproj_fm(w0in, bi_of["cq%d" % b], 128, ev)
            for half in range(2):
                rstd_from(half, rstdq[:, half * 512:(half + 1) * 512], B_Q_LORA, b_rq)
            if L0S <= 1.2:
                return
            for b in range(2):
                def ev(half, pb, b=b):
                    r = sqr[0] % 2
                    sqr[0] += 1
                    act(sq16[r], psum[pb][:, :], AF.Square, [b_ps[pb]], [b_sq[r]])
                    cp("dve", stA[:, b, half * 512:(half + 1) * 512], psum[pb][:, :], [b_ps[pb]], [b_st[b]])
                    mm(psum[half][:, :], ones16[:, :], sq16[r], b == 0, b == 1, [b_sq[r], b_const], [b_ps[half]])
                inproj_fm(w0in, bi_of["ckv%d" % b], 128, ev)
            for half in range(2):
                rstd_from(half, rkv[:, half * 512:(half + 1) * 512], B_KV_LORA, b_rkv)
            for b in range(2):
                stt("dve", stA[:, b, :], stA[:, b, :], sm[:, 6 + b:7 + b], rkv[:, :], ALU.mult, ALU.mult,
                    [b_st[b], b_sm, b_rkv], [b_st[b]])
                cp("pool", ckvn16[:, b, 0:T], stA[:, b, :], [b_st[b]], [b_ckv])
            if L0S <= 1.4:
                return
            for i in range(NT):
                pb = bank()
                for b in range(2):
                    tr(psum[pb][:, b * 128:(b + 1) * 128], stA[:, b, i * 128:(i + 1) * 128], ident32[:, :],
                       [b_st[b], b_const], [b_ps[pb]])
                cp("act", ost[:, 0:256], psum[pb][:, 0:256], [b_ps[pb]], [b_ost])
                dma("sp", nckv_out[i * 128:(i + 1) * 128, :], ost[:, 0:256], [b_ost], [], d_ost)
            if L0S <= 1.6:
                return
            dma("sp", ctxs.rearrange("p (a c) -> p a c", c=256), cckv_in.rearrange("(a p) c -> p a c", p=128), [], [b_ctx], d_ctx)
            for b in range(2):
                pb = bank()
                for a in range(4):
                    tr(psum[pb][:, a * 128:(a + 1) * 128], ctxs[:, a * 256 + b * 128:a * 256 + (b + 1) * 128], ident32[:, :],
                       [b_ctx, b_const], [b_ps[pb]])
                cp("act", ckvn16[:, b, T:1536], psum[pb][:, :], [b_ps[pb]], [b_ckv])
            if L0S <= 1.8:
                return
            def ev_kpe(half, pb):
                cp("act", stA[0:64, 0, half * 512:(half + 1) * 512], psum[pb][0:64, :], [b_ps[pb]], [b_st[0]])
            inproj_fm(w0in, bi_of["kpe"], 64, ev_kpe)
            pb = bank()
            for i in range(NT):
                tr(psum[pb][:, i * 64:(i + 1) * 64], stA[0:64, 0, i * 128:(i + 1) * 128], ident32[0:64, 0:64],
                   [b_st[0], b_const], [b_ps[pb]])
            cp("act", ost[:, :], psum[pb][:, :], [b_ps[pb]], [b_ost])
            dma("sp", nkpe_out.rearrange("(i p) c -> p i c", p=128), ost.rearrange("p (i c) -> p i c", c=64), [b_ost], [], d_ost)
            rope(stA[:, 0, :], b_st[0], kpe16, b_kpe, 64)
            dma("sp", ctxs[:, 0:256].rearrange("p (a c) -> p a c", c=64), ckpe_in.rearrange("(a p) c -> p a c", p=128), [], [b_ctx], d_ctx)
            pb = bank()
            for a in range(4):
                tr(psum[pb][0:64, a * 128:(a + 1) * 128], ctxs[:, a * 64:(a + 1) * 64], ident32[:, :],
                   [b_ctx, b_const], [b_ps[pb]])
            cp("act", kpe16[0:64, T:1536], psum[pb][0:64, :], [b_ps[pb]], [b_kpe])

            if L0S <= 2:
                return

            def gate_block(u):
                def ev(half, pb):
                    act(gT16[:, half * 512:(half + 1) * 512], psum[pb][:, :], AF.Silu, [b_ps[pb]], [b_g])
                inproj_fm(w0in, bi_of["g%d" % u], 128, ev)

            def attention(u, diff):
                for qh in range(2):
                    qs = slice(qh * 512, (qh + 1) * 512)
                    ncomp = 2 if diff else 1
                    for kt in range(12):
                        ks = slice(kt * 128, (kt + 1) * 128)
                        for c in range(ncomp):
                            sb = bank()
                            if diff:
                                mm(psum[sb][:, :], K16[64 * c:64 * c + 64, ks], q16[64 * c:64 * c + 64, qs], True, False,
                                   [b_K, b_q], [b_ps[sb]], rg=c)
                                mm(psum[sb][:, :], mk16[0:8, ks], mq16[0:8, qs], False, True, [b_tab], [b_ps[sb]], rg=0)
                                sc = 64 ** -0.5
                            else:
                                mm(psum[sb][:, :], K16[:, ks], q16[:, qs], True, False, [b_K, b_q], [b_ps[sb]])
                                mm(psum[sb][:, :], kpe16[0:72, ks], qpe16[0:72, qs], False, True, [b_kpe, b_qpe], [b_ps[sb]])
                                sc = 192 ** -0.5
                            r = ptr[0] % 4
                            ptr[0] += 1
                            act(PT[r], psum[sb][:, :], AF.Exp, [b_ps[sb]], [b_PT[r]], scale=sc)
                            mm(psum[c][:, :], V16[:, kt, :], PT[r], kt == 0, kt == 11, [b_V, b_PT[r]], [b_ps[c]])
                            mm(psum[2 + c][:, :], ones16[:, :], PT[r], kt == 0, kt == 11, [b_const, b_PT[r]], [b_ps[2 + c]])
                    recip(o32[0], psum[2][:, :], [b_ps[2]], [b_o[0]])
                    tt("dve", o32[0], psum[0][:, :], o32[0], ALU.mult, [b_ps[0], b_o[0]], [b_o[0]])
                    if diff:
                        recip(o32[1], psum[3][:, :], [b_ps[3]], [b_o[1]])
                        tt("dve", o32[1], psum[1][:, :], o32[1], ALU.mult, [b_ps[1], b_o[1]], [b_o[1]])
                        stt("dve", o32[0], o32[1], sm[:, 0:1], o32[0], ALU.mult, ALU.add, [b_o[0], b_o[1], b_sm], [b_o[0]])
                        r = sqr[0] % 2
                        sqr[0] += 1
                        act(sq16[r], o32[0], AF.Square, [b_o[0]], [b_sq[r]])
                        pb = bank()
                        mm(psum[pb][:, :], ones16[:, :], sq16[r], True, True, [b_sq[r], b_const], [b_ps[pb]])
                        rstd_from(pb, o32[2], 128, b_o[2])
                        stt("dve", o32[0], o32[0], sm[:, 1:2], o32[2], ALU.mult, ALU.mult, [b_o[0], b_o[2], b_sm], [b_o[0]])
                    yc, b_yc = ycs[u % 2]
                    tt("pool", yc[:, qs], o32[0], gT16[:, qs], ALU.mult, [b_o[0], b_g], [b_yc])
                    if qh == 1:
                        store_y(u, yc, b_yc)

            for h in range(8 if L0S > 3 else 1):
                def evq(half, pb):
                    cp("act", stA[:, 0, half * 512:(half + 1) * 512], psum[pb][:, :], [b_ps[pb]], [b_st[0]])

                def evk(half, pb):
                    cp("act", stA[:, 1, half * 512:(half + 1) * 512], psum[pb][:, :], [b_ps[pb]], [b_st[1]])
                inproj_fm(w0in, bi_of["q%d" % h], 128, evq)
                inproj_fm(w0in, bi_of["k%d" % h], 128, evk)
                for g in range(2):
                    pb = bank()
                    for ii in range(4):
                        i = 4 * g + ii
                        tr(psum[pb][:, ii * 128:(ii + 1) * 128], stA[:, 1, i * 128:(i + 1) * 128], ident32[:, :],
                           [b_st[1], b_const], [b_ps[pb]])
                    cp("act", ost[:, :], psum[pb][:, :], [b_ps[pb]], [b_ost])
                    dma("sp", nk_out[g * 512:(g + 1) * 512, h * 128:(h + 1) * 128].rearrange("(i p) c -> p i c", p=128),
                        ost.rearrange("p (i c) -> p i c", c=128), [b_ost], [], d_ost)
                rope(stA[:, 0, :], b_st[0], q16, b_q, 128)
                rope(stA[:, 1, :], b_st[1], K16, b_K, 128)
                dma("sp", ctxs[:, 0:512].rearrange("p (a c) -> p a c", c=128),
                    ck_in[:, h * 128:(h + 1) * 128].rearrange("(a p) c -> p a c", p=128), [], [b_ctx], d_ctx)
                pb = bank()
                for a in range(4):
                    tr(psum[pb][:, a * 128:(a + 1) * 128], ctxs[:, a * 128:(a + 1) * 128], ident32[:, :],
                       [b_ctx, b_const], [b_ps[pb]])
                cp("act", K16[:, T:1536], psum[pb][:, :], [b_ps[pb]], [b_K])
                si = wload3(w0in[bi_of["v%d" % h] % NB0], KC)
                wv = wview(si, KC)
                for g in range(2):
                    pb = bank()
                    for ii in range(4):
                        i = 4 * g + ii
                        for k in range(KC):
                            mm(psum[pb][:, ii * 128:(ii + 1) * 128], hT[:, k, i * 128:(i + 1) * 128], wv[:, k, :],
                               k == 0, k == KC - 1, [b_w[si], b_hT[i]], [b_ps[pb]])
                    cp("act", V16[:, 4 * g:4 * g + 4, :], psum[pb][:, :].rearrange("p (a c) -> p a c", c=128), [b_ps[pb]], [b_V])
                    cp("dve", vst[:, :], psum[pb][:, :], [b_ps[pb]], [b_vst])
                    dma("sp", nv_out[g * 512:(g + 1) * 512, h * 128:(h + 1) * 128].rearrange("(i p) c -> p i c", p=128),
                        vst.rearrange("p (i c) -> p i c", c=128), [b_vst], [], d_vst)
                dma("pool", V16[:, 8:12, :], cv_in[:, h * 128:(h + 1) * 128].rearrange("(a p) c -> p a c", p=128), [], [b_V], d_V)
                gate_block(h)
                attention(h, True)

            if L0S <= 4:
                return
            for h in range(8 if L0S > 5 else 1):
                u = 8 + h
                si = wload3(wuq_in[h], 4)
                wv = wview(si, 4)
                for half in range(2):
                    hs = slice(half * 512, (half + 1) * 512)
                    pb = bank()
                    for k in range(4):
                        mm(psum[pb][:, :], wv[:, k, :], cqg16[:, k, hs], k == 0, k == 3, [b_w[si], b_cq], [b_ps[pb]])
                    tt("dve", q16[:, hs], psum[pb][:, :], rstdq[:, hs], ALU.mult, [b_ps[pb], b_rq], [b_q])
                si = wload3(wuq_in[8 + h], 4)
                wv = wview(si, 4)
                for half in range(2):
                    hs = slice(half * 512, (half + 1) * 512)
                    pb = bank()
                    for k in range(4):
                        mm(psum[pb][0:64, :], wv[:, k, 0:64], cqg16[:, k, hs], k == 0, k == 3, [b_w[si], b_cq], [b_ps[pb]])
                    tt("dve", stA[0:64, 0, hs], psum[pb][0:64, :], rstdq[0:64, hs], ALU.mult, [b_ps[pb], b_rq], [b_st[0]])
                rope(stA[:, 0, :], b_st[0], qpe16, b_qpe, 64)
                si = wload3(wukv_in[h], 2)
                wv = wview(si, 2)
                for kb in range(3):
                    pb = bank()
                    for k in range(2):
                        mm(psum[pb][:, :], wv[:, k, :], ckvn16[:, k, kb * 512:(kb + 1) * 512], k == 0, k == 1,
                           [b_w[si], b_ckv], [b_ps[pb]])
                    cp("act", K16[:, kb * 512:(kb + 1) * 512], psum[pb][:, :], [b_ps[pb]], [b_K])
                si = wload3(wukv_in[8 + h], 2)
                wv = wview(si, 2)
                for g in range(3):
                    pb = bank()
                    for ii in range(4):
                        kt = 4 * g + ii
                        for k in range(2):
                            mm(psum[pb][:, ii * 128:(ii + 1) * 128], ckvn16[:, k, kt * 128:(kt + 1) * 128], wv[:, k, :],
                               k == 0, k == 1, [b_w[si], b_ckv], [b_ps[pb]])
                    cp("act", V16[:, 4 * g:4 * g + 4, :], psum[pb][:, :].rearrange("p (a c) -> p a c", c=128), [b_ps[pb]], [b_V])
                gate_block(u)
                attention(u, False)
        def l1_phase():
            ar = Arena()
            tri_a, b_tri = ar.f32(768)
            tri = [tri_a[:, 0:384], tri_a[:, 384:768]]
            msk_a, b_msk = ar.bf16(1024)
            msk4 = [msk_a[:, 0:512], msk_a[:, 512:1024]]
            lvl_a, b_lvl = ar.bf16(4 * 256)
            lvl = lvl_a.rearrange("p (l c) -> p l c", c=256)
            mskT_a, b_mskT = ar.f32(256)
            mskT = [mskT_a[:, 0:128], mskT_a[:, 128:256]]
            blk16, b_blk = ar.bf16(128)
            small, b_small = ar.f32(1024)
            NBL1 = len(L1B)
            mu_t = small[:, 0:2 * NBL1].rearrange("p (b m) -> p b m", m=2)
            c0_t = small[:, 136:136 + NBL1]
            mufix = small[:, 204:204 + 2 * NBL1].rearrange("p (b m) -> p b m", m=2)
            a0_t = small[:, 340:372].rearrange("p (z h) -> p z h", h=16)
            hpar = small[:, 372:468].rearrange("p (w h) -> p w h", h=16)
            b_c = [b_tri, b_msk, b_mskT, b_blk, b_small, b_lvl]
            w0h, b_w0h = ar.f32(256)
            twad = [ar.bf16(T) for _ in range(4)]
            tw16 = [twad[0][0], twad[1][0]]
            ad16 = [twad[2][0], twad[3][0]]
            b_tw = [twad[0][1], twad[1][1]]
            b_ad = [twad[2][1], twad[3][1]]
            w2h_a, b_w2 = ar.bf16(4 * 128)
            w2h = w2h_a.rearrange("p (w c) -> p w c", c=128)
            r32, b_r = ar.f32(T)
            k32, b_k = ar.f32(T)
            v32, b_v = ar.f32(T)
            kk32, b_kk = ar.f32(T)
            acc32, b_acc = ar.f32(T)
            g16, b_g = ar.bf16(T)
            v16, b_v16 = ar.bf16(T)
            bon32, b_bon = ar.f32(T)
            rawy, b_rawy = ar.f32(T)
            raw, b_raw = rawy, b_rawy
            lw32, b_lw = rawy, b_rawy
            y32, b_y = rawy, b_rawy
            sqs = [ar.bf16(512), ar.bf16(512)]
            sq16 = [sqs[0][0], sqs[1][0]]
            b_sq = [sqs[0][1], sqs[1][1]]
            tmp32, b_tmp = ar.f32(512)
            grpA, b_grpA = ar.f32(3072)
            G3 = grpA.rearrange("p (i c) -> p i c", c=384)
            b_G3 = b_grpA
            YX = [grpA[:, 1024 * r_:1024 * (r_ + 1)].bitcast(BF16).rearrange("p (q c) -> p q c", c=256) for r_ in range(2)]
            b_YX = [[b_grpA[4 * r_ + ql // 2] for ql in range(8)] for r_ in range(2)]
            TMPm = grpA[:, 2048:3072].bitcast(BF16).rearrange("p (q c) -> p q c", c=256)
            b_TM = [b_grpA[8 + ql // 2] for ql in range(8)]
            grpB, b_grpB = ar.f32(2048)
            kz32, b_kz = grpB[:, 0:T], b_grpB[0:4]
            al32, b_al = grpB[:, T:2 * T], b_grpB[4:8]
            P2 = grpB[:, 0:1024].bitcast(BF16).rearrange("p (q c) -> p q c", c=256)
            b_P2 = [b_grpB[ql // 2] for ql in range(8)]
            P4 = grpB[:, 1024:2048].bitcast(BF16).rearrange("p (q c) -> p q c", c=256)
            b_P4 = [b_grpB[4 + ql // 2] for ql in range(8)]
            grpC, b_grpC = ar.f32(T)
            a32, b_a = grpC, b_grpC
            d32, b_d = grpC, b_grpC
            Ginv, b_Gi = grpC, b_grpC
            MN_a, b_MN_all = ar.bf16(16 * 256)
            MN = MN_a.rearrange("p (q c) -> p q c", c=256)
            b_MN = [b_MN_all[q // 2] for q in range(16)]
            XTf_a, b_XTf_all = ar.bf16(16 * 128)
            XTf = XTf_a.rearrange("p (q c) -> p q c", c=128)
            b_XTf = [b_XTf_all[q // 4] for q in range(16)]
            BR_a, b_BR = ar.bf16(NT * 256)
            BR = BR_a.rearrange("p (i c) -> p i c", c=256)
            KA_a, b_KA = ar.bf16(NT * 256)
            KA = KA_a.rearrange("p (i c) -> p i c", c=256)
            UG = KA_a.rearrange("p (q c) -> p q c", c=128)
            b_UG = [b_KA[q // 4] for q in range(16)]
            Kb_a, b_Kb = ar.bf16(NT * 256)
            KbAb = Kb_a.rearrange("p (i c) -> p i c", c=256)
            wl16 = Kb_a[:, 0:1024].rearrange("p (q c) -> p q c", c=64)
            b_wl = [b_Kb[q // 8] for q in range(16)]
            lkt = [Kb_a[:, 1024:1152], Kb_a[:, 1536:1664]]
            b_lkt = [b_Kb[2], b_Kb[3]]
            tok_a, b_tok_all = ar.bf16(NT * 512)
            tokT = tok_a.rearrange("p (i c) -> p i c", c=512)
            b_tok = [b_tok_all[i] for i in range(NT)]
            GM_a, b_GM_all = ar.bf16(16 * 256)
            GM = GM_a.rearrange("p (q c) -> p q c", c=256)
            b_GM = [b_GM_all[q // 2] for q in range(16)]
            Qt, b_Qt = ar.bf16(T)
            TT_a, b_TT = ar.f32(16 * 128)
            TT = TT_a.rearrange("p (c j) -> p c j", j=128)
            HL_a, b_HL = ar.f32(16 * 64)
            HL = HL_a.rearrange("p (c j) -> p c j", j=64)
            Hs_a, b_Hs = ar.f32(17 * 64)
            Hs = Hs_a.rearrange("p (n j) -> p n j", j=64)
            H16_a, b_H16 = ar.bf16(17 * 64)
            Hs16 = H16_a.rearrange("p (n j) -> p n j", j=64)
            slsto, b_slsto = ar.f32(256)
            sl32, b_sl = slsto[:, 0:128], b_slsto
            sto, b_sto = slsto[:, 128:256], b_slsto
            ycs = [ar.bf16(T)] * 2
            LAYOUT["l1"] = list(ar.log)
            LAYOUT["l1_total"] = ar.off
            d_c1 = S.dsem("dl1c")
            d_w2 = S.dsem("dw2h")
            d_sl = S.dsem("dsl")
            d_sto = S.dsem("dsto")
            bi_of = {name: i for i, (name, _, _) in enumerate(L1B)}
            sqr = [0]
            lkr = [0]
            lastcol_of = [63, 0]

            for z in range(2):
                dma("sp", tri[z], tri_in[z], [], [b_c], d_c1)
                for rep_ in range(2):
                    dma("pool", msk4[z][:, rep_ * 256:(rep_ + 1) * 256], msk_in[z], [], [b_c], d_c1)
                dma("sp", mskT[z], msk_in[1 - z, :, 0:128], [], [b_c], d_c1)
            dma("pool", blk16, blk64_in, [], [b_c], d_c1)
            for l_ in range(4):
                for rep_ in range(2):
                    dma("pool", lvl[:, l_, rep_ * 128:(rep_ + 1) * 128], lvl_in[l_], [], [b_c], d_c1)
            dma("sp", mu_t, mu_in, [], [b_c], d_c1)
            dma("sp", a0_t, a0_in, [], [b_c], d_c1)
            dma("sp", hpar[:, 0:5, :], hpar_in, [], [b_c], d_c1)
            ts("dve", hpar[:, 5, :], hpar[:, 1, :], -1.0, 1.0, ALU.mult, ALU.add, [b_c], [b_c])
            tt("dve", c0_t, mu_t[:, :, 0], mu_t[:, :, 1], ALU.add, [b_c], [b_c])
            ts("dve", c0_t, c0_t, -1.0, 1.0, ALU.mult, ALU.add, [b_c], [b_c])
            ts("dve", mufix.rearrange("p b m -> p (b m)"), mu_t.rearrange("p b m -> p (b m)"), keep_t[:, 1:2], -1.0, ALU.mult, ALU.mult,
               [b_c, b_const], [b_c])
            memset("dve", TT.rearrange("p c j -> p (c j)"), 0.0, [b_TT])
            d_ones = S.dsem("dl1ones")
            for z in range(2):
                pass

            def inproj_fm(bi, ncols, evac):
                si = wload3(w1in[bi % NB1], KC)
                wv = wview(si, KC)
                for half in range(2):
                    pb = bank(0, 8)
                    for k in range(KC):
                        mm(psum[pb][0:ncols, :], wv[:, k, 0:ncols], hT[:, k, half * 512:(half + 1) * 512],
                           k == 0, k == KC - 1, [b_w[si]] + b_hT[4 * half:4 * half + 4], [b_ps[pb]])
                    evac(half, pb)

            def shifted(bi, nrow, dst, bdst):
                def ev(half, pb):
                    cp("act", raw[0:nrow, half * 512:(half + 1) * 512], psum[pb][0:nrow, :], [b_ps[pb]], [b_raw])
                inproj_fm(bi, nrow, ev)
                act(dst[0:nrow, :], raw[0:nrow, :], AF.Identity, [b_raw, b_c], [bdst], scale=c0_t[0:nrow, bi:bi + 1])
                stt("dve", dst[0:nrow, 1:T], raw[0:nrow, 0:T - 1], mu_t[0:nrow, bi, 0:1], dst[0:nrow, 1:T], ALU.mult, ALU.add,
                    [b_raw, b_c, bdst], [bdst])
                stt("dve", dst[0:nrow, 0:T - 1], raw[0:nrow, 1:T], mu_t[0:nrow, bi, 1:2], dst[0:nrow, 0:T - 1], ALU.mult, ALU.add,
                    [b_raw, b_c, bdst], [bdst])
                for bnd in (256, 512, 768):
                    stt("dve", dst[0:nrow, bnd:bnd + 1], raw[0:nrow, bnd - 1:bnd], mufix[0:nrow, bi, 0:1], dst[0:nrow, bnd:bnd + 1],
                        ALU.mult, ALU.add, [b_raw, b_c, bdst], [bdst])
                    stt("dve", dst[0:nrow, bnd - 1:bnd], raw[0:nrow, bnd:bnd + 1], mufix[0:nrow, bi, 1:2], dst[0:nrow, bnd - 1:bnd],
                        ALU.mult, ALU.add, [b_raw, b_c, bdst], [bdst])

            for z in range(2):
                shifted(bi_of["wd%d" % z], LORA, r32, b_r)
                act(tw16[z][0:LORA, :], r32[0:LORA, :], AF.Tanh, [b_r], [b_tw[z]])
                shifted(bi_of["ad%d" % z], LORA, r32, b_r)
                cp("dve", ad16[z][0:LORA, :], r32[0:LORA, :], [b_r], [b_ad[z]])

            L1S = float(os.environ.get("KL1", "9"))
            if L1S <= 1:
                return
            for hp in range(16):
                cs = slice(hp * 128, (hp + 1) * 128)
                for z in range(2):
                    dma("pool", w2h[0:LORA, z, :], w2_in[z][:, cs], [], [b_w2], d_w2)
                    dma("pool", w2h[0:LORA, 2 + z, :], a2_in[z][:, cs], [], [b_w2], d_w2)
                    dma("sp", w0h[0:1, z * 128:(z + 1) * 128], w0row_in[z][:, cs], [], [b_w0h], d_w2)
                shifted(bi_of["r%d" % hp], 128, r32, b_r)
                shifted(bi_of["k%d" % hp], 128, k32, b_k)
                shifted(bi_of["v%d" % hp], 128, v32, b_v)
                shifted(bi_of["g%d" % hp], 128, acc32, b_acc)
                act(g16[:, :], acc32[:, :], AF.Silu, [b_acc], [b_g])
                cp("pool", v16[:, :], v32[:, :], [b_v], [b_v16])
                ts("dve", kk32[:, :], k32[:, :], hpar[:, 0, hp:hp + 1], None, ALU.mult, None, [b_k, b_c], [b_kk])

                for half in range(2):
                    hs = slice(half * 512, (half + 1) * 512)
                    r = sqr[0] % 2
                    sqr[0] += 1
                    act(sq16[r], kk32[:, hs], AF.Square, [b_kk], [b_sq[r]])
                    pb = bank(0, 8)
                    mm(psum[pb][:, :], blk16[:, :], sq16[r], True, True, [b_sq[r], b_c], [b_ps[pb]])
                    ts("dve", tmp32[:, :], psum[pb][:, :], 1e-12, None, ALU.max, None, [b_ps[pb]], [b_tmp])
                    act(tmp32[:, :], tmp32[:, :], AF.Sqrt, [b_tmp], [b_tmp])
                    recip(tmp32[:, :], tmp32[:, :], [b_tmp], [b_tmp])
                    tt("dve", kk32[:, hs], kk32[:, hs], tmp32[:, :], ALU.mult, [b_kk, b_tmp], [b_kk])

                for z in range(2):
                    for half in range(2):
                        hs = slice(half * 512, (half + 1) * 512)
                        pb = bank(0, 8)
                        mm(psum[pb][:, :], w2h[0:LORA, 2 + z, :], ad16[z][0:LORA, hs], True, True, [b_w2, b_ad[z]], [b_ps[pb]])
                        act(a32[:, hs], psum[pb][:, :], AF.Sigmoid, [b_ps[pb], b_c], [b_a], bias=a0_t[:, z, hp:hp + 1])
                    ts("dve", kz32[:, :], a32[:, :], hpar[:, 1, hp:hp + 1], hpar[:, 5, hp:hp + 1], ALU.mult, ALU.add, [b_a, b_c], [b_kz])
                    tt("pool", kz32[:, :], kz32[:, :], k32[:, :], ALU.mult, [b_kz, b_k], [b_kz])
                    stt("dve", al32[:, :], kk32[:, :], -1.0, a32[:, :], ALU.mult, ALU.mult, [b_kk, b_a], [b_al])
                    tt("pool", d32[:, :], r32[:, :], kz32[:, :], ALU.mult, [b_r, b_kz], [b_d])
                    for half in range(2):
                        hs = slice(half * 512, (half + 1) * 512)
                        r = sqr[0] % 2
                        sqr[0] += 1
                        ts("dve", sq16[r], d32[:, hs], hpar[:, 2, hp:hp + 1], None, ALU.mult, None, [b_d, b_c], [b_sq[r]])
                        pb = bank(0, 8)
                        mm(psum[pb][:, :], blk16[:, :], sq16[r], True, True, [b_sq[r], b_c], [b_ps[pb]])
                        tt("dve", bon32[:, hs], psum[pb][:, :], v32[:, hs], ALU.mult, [b_ps[pb], b_v], [b_bon])
                    for g in range(2):
                        pb = bank(0, 8)
                        for ii in range(4):
                            i = 4 * g + ii
                            mm(psum[pb][:, ii * 128:(ii + 1) * 128], tw16[z][0:LORA, i * 128:(i + 1) * 128], w2h[0:LORA, z, :],
                               True, False, [b_tw[z], b_w2], [b_ps[pb]])
                            mm(psum[pb][:, ii * 128:(ii + 1) * 128], ones32[0:1, :], w0h[0:1, z * 128:(z + 1) * 128],
                               False, True, [b_const, b_w0h], [b_ps[pb]])
                        act(lw32[:, g * 512:(g + 1) * 512], psum[pb][:, :], AF.Sigmoid, [b_ps[pb]], [b_lw])
                    ts("dve", lw32[:, :], lw32[:, :], -math.exp(-0.5), None, ALU.mult, None, [b_lw], [b_lw])
                    for i in range(NT):
                        pb = bank(0, 8)
                        mm(psum[pb][:, 0:384], lw32[:, i * 128:(i + 1) * 128], tri[z][:, :], True, True, [b_lw, b_c], [b_ps[pb]])
                        act(G3[:, i, :], psum[pb][:, 0:384], AF.Exp, [b_ps[pb]], [b_G3])
                        act(Ginv[:, i * 128:(i + 1) * 128], psum[pb][:, 0:128], AF.Exp, [b_ps[pb]], [b_Gi], scale=-1.0)

                    def v3(t32):
                        return t32.rearrange("p (i c) -> p i c", c=128)
                    tt("dve", BR[:, :, 0:128], v3(kk32), G3[:, :, 128:256], ALU.mult, [b_kk, b_G3], [b_BR])
                    tt("pool", BR[:, :, 128:256], v3(r32), G3[:, :, 0:128], ALU.mult, [b_r, b_G3], [b_BR])
                    tt("dve", KA[:, :, 0:128], v3(kz32), v3(Ginv), ALU.mult, [b_kz, b_Gi], [b_KA])
                    tt("pool", KA[:, :, 128:256], v3(al32), v3(Ginv), ALU.mult, [b_al, b_Gi], [b_KA])
                    tt("dve", KbAb[:, :, 0:128], v3(kz32), G3[:, :, 256:384], ALU.mult, [b_kz, b_G3], [b_Kb])
                    tt("pool", KbAb[:, :, 128:256], v3(al32), G3[:, :, 256:384], ALU.mult, [b_al, b_G3], [b_Kb])
                    if L1S <= 2:
                        return
                    for i in range(NT):
                        pb = bank(0, 8)
                        pv = psum[pb].bitcast(BF16)
                        tr(pv[:, 0:128], v16[:, i * 128:(i + 1) * 128], ident16[:, :], [b_v16, b_const], [b_ps[pb]])
                        tr(pv[:, 128:256], BR[:, i, 0:128], ident16[:, :], [b_BR, b_const], [b_ps[pb]])
                        tr(pv[:, 256:384], KbAb[:, i, 0:128], ident16[:, :], [b_Kb, b_const], [b_ps[pb]])
                        tr(pv[:, 384:512], KbAb[:, i, 128:256], ident16[:, :], [b_Kb, b_const], [b_ps[pb]])
                        cp("act", tokT[:, i, :], pv[:, 0:512], [b_ps[pb]], [b_tok[i]])
                    gam = small[:, 724:740]
                    cp("dve", gam.rearrange("p (i c) -> p i c", c=2), G3[:, :, lastcol_of[z]:lastcol_of[z] + 65:64], [b_G3], [b_small])
                    for i in range(NT):
                        for e in range(2):
                            q = 2 * i + e
                            es_ = slice(64 * e, 64 * e + 64)
                            pb = bank(0, 8)
                            mm(psum[pb][:, 0:256], KA[es_, i, 0:128], BR[es_, i, 0:256], True, True, [b_KA, b_BR], [b_ps[pb]], rg=e)
                            mm(psum[pb][:, 256:512], KA[es_, i, 128:256], BR[es_, i, 0:256], True, True, [b_KA, b_BR], [b_ps[pb]], rg=e)
                            r = lkr[0] % 2
                            lkr[0] += 1
                            tt("dve", lkt[r], psum[pb][:, 0:128], msk4[z][:, 0:128], ALU.mult, [b_ps[pb], b_c], [b_lkt[r]])
                            tt("dve", GM[:, q, 0:128], psum[pb][:, 128:256], msk4[z][:, 128:256], ALU.mult, [b_ps[pb], b_c], [b_GM[q]])
                            tt("dve", MN[:, q, 0:128], psum[pb][:, 256:384], msk4[z][:, 0:128], ALU.mult, [b_ps[pb], b_c], [b_MN[q]])
                            tt("dve", GM[:, q, 128:256], psum[pb][:, 384:512], msk4[z][:, 128:256], ALU.mult, [b_ps[pb], b_c], [b_GM[q]])
                            pb = bank(0, 8)
                            mm(psum[pb][:, 0:128], BR[es_, i, 0:128], KA[es_, i, 128:256], True, True, [b_KA, b_BR], [b_ps[pb]], rg=e)
                            tt("dve", MN[:, q, 128:256], psum[pb][:, 0:128], mskT[z][:, :], ALU.mult, [b_ps[pb], b_c], [b_MN[q]])
                            pb = bank(0, 8)
                            mm(psum[pb][:, 0:64], lkt[r], tokT[:, i, 64 * e:64 * e + 64], True, True, [b_lkt[r], b_tok[i]], [b_ps[pb]])
                            cp("act", wl16[:, q, :], psum[pb][:, 0:64], [b_ps[pb]], [b_wl[q]])
                    if L1S <= 3:
                        return
                    def pair_mm(lhs_a, rhs_a, lhs_b, rhs_b, reads):
                        pb_ = bank(0, 8)
                        mm(psum[pb_][:, 0:128], lhs_a, rhs_a, True, True, reads, [b_ps[pb_]])
                        mm(psum[pb_][:, 128:256], lhs_b, rhs_b, True, True, reads, [b_ps[pb_]])
                        return pb_
                    for bt in range(2):
                        qs_ = list(range(8 * bt, 8 * bt + 8))
                        for q in qs_:
                            ql = q % 8
                            tt("pool", TMPm[:, ql, :], MN[:, q, :], lvl[:, 0, :], ALU.mult, [b_MN[q], b_c], [b_TM[ql]])
                            tt("pool", YX[0][:, ql, 0:128], TMPm[:, ql, 0:128], ident16[:, :], ALU.add, [b_TM[ql], b_const], [b_YX[0][ql]])
                            tt("pool", YX[0][:, ql, 128:256], TMPm[:, ql, 128:256], ident16[:, :], ALU.add, [b_TM[ql], b_const], [b_YX[0][ql]])
                        for q in qs_:
                            ql = q % 8
                            pb_ = pair_mm(TMPm[:, ql, 128:256], TMPm[:, ql, 0:128], TMPm[:, ql, 0:128], TMPm[:, ql, 128:256], [b_TM[ql]])
                            cp("act", P2[:, ql, :], psum[pb_][:, 0:256], [b_ps[pb_]], [b_P2[ql]])
                        for q in qs_:
                            ql = q % 8
                            pb_ = pair_mm(P2[:, ql, 128:256], YX[0][:, ql, 0:128], P2[:, ql, 0:128], YX[0][:, ql, 128:256], [b_P2[ql], b_YX[0][ql]])
                            tt("dve", YX[1][:, ql, :], psum[pb_][:, 0:256], YX[0][:, ql, :], ALU.add, [b_ps[pb_], b_YX[0][ql]], [b_YX[1][ql]])
                        for q in qs_:
                            ql = q % 8
                            pb_ = pair_mm(P2[:, ql, 128:256], P2[:, ql, 0:128], P2[:, ql, 0:128], P2[:, ql, 128:256], [b_P2[ql]])
                            cp("act", P4[:, ql, :], psum[pb_][:, 0:256], [b_ps[pb_]], [b_P4[ql]])
                        for q in qs_:
                            ql = q % 8
                            pb_ = pair_mm(P4[:, ql, 128:256], YX[1][:, ql, 0:128], P4[:, ql, 0:128], YX[1][:, ql, 128:256], [b_P4[ql], b_YX[1][ql]])
                            tt("dve", YX[0][:, ql, :], psum[pb_][:, 0:256], YX[1][:, ql, :], ALU.add, [b_ps[pb_], b_YX[1][ql]], [b_YX[0][ql]])
                        cur = 0
                        for l_ in (1, 2, 3):
                            nxt = 1 - cur
                            for q in qs_:
                                ql = q % 8
                                tt("pool", TMPm[:, ql, :], MN[:, q, :], lvl[:, l_, :], ALU.mult, [b_MN[q], b_c], [b_TM[ql]])
                            for q in qs_:
                                ql = q % 8
                                pb_ = pair_mm(TMPm[:, ql, 128:256], YX[cur][:, ql, 0:128], TMPm[:, ql, 0:128], YX[cur][:, ql, 128:256],
                                              [b_TM[ql], b_YX[cur][ql]])
                                cp("act", P2[:, ql, :], psum[pb_][:, 0:256], [b_ps[pb_]], [b_P2[ql]])
                            for q in qs_:
                                ql = q % 8
                                pb_ = pair_mm(YX[cur][:, ql, 128:256], P2[:, ql, 0:128], YX[cur][:, ql, 0:128], P2[:, ql, 128:256],
                                              [b_P2[ql], b_YX[cur][ql]])
                                if l_ < 3:
                                    tt("dve", YX[nxt][:, ql, :], psum[pb_][:, 0:256], YX[cur][:, ql, :], ALU.add, [b_ps[pb_], b_YX[cur][ql]], [b_YX[nxt][ql]])
                                else:
                                    tt("dve", XTf[:, q, :], psum[pb_][:, 0:128], YX[cur][:, ql, 0:128], ALU.add, [b_ps[pb_], b_YX[cur][ql]], [b_XTf[q]])
                            cur = nxt
                    X = XTf
                    bX = b_XTf
                    if L1S <= 4:
                        return
                    for i in range(NT):
                        for e in range(2):
                            q = 2 * i + e
                            pb = bank(0, 8)
                            mm(psum[pb][:, 0:64], X[:, q, :], wl16[:, q, :], True, True, [bX[q], b_wl[q]], [b_ps[pb]])
                            mm(psum[pb][:, 64:128], X[:, q, :], tokT[:, i, 128 + 64 * e:128 + 64 * e + 64], True, True, [bX[q], b_tok[i]], [b_ps[pb]])
                            cp("act", UG[:, q, :], psum[pb][:, 0:128], [b_ps[pb]], [b_UG[q]])
                    for i in range(NT):
                        pb = bank(0, 8)
                        for e in range(2):
                            q = 2 * i + e
                            es_ = slice(64 * e, 64 * e + 64)
                            mm(psum[pb][es_, 0:128], UG[:, q, 64:128], GM[:, q, 128:256], True, True, [b_UG[q], b_GM[q]], [b_ps[pb]])
                        tt("dve", Qt[:, i * 128:(i + 1) * 128], psum[pb][:, 0:128], BR[:, i, 128:256], ALU.add, [b_ps[pb], b_BR], [b_Qt])
                        pb = bank(0, 8)
                        for e in range(2):
                            q = 2 * i + e
                            es_ = slice(64 * e, 64 * e + 64)
                            for cc in range(2):
                                ts_ = slice(64 * cc, 64 * cc + 64)
                                mm(psum[pb][es_, cc * 64:cc * 64 + 64], UG[ts_, q, 64:128], tokT[ts_, i, 384 + 64 * e:384 + 64 * e + 64],
                                   True, True, [b_UG[q], b_tok[i]], [b_ps[pb]], rg=cc)
                                mm(psum[pb][es_, 128 + cc * 64:128 + cc * 64 + 64], tokT[ts_, i, 256 + 64 * e:256 + 64 * e + 64],
                                   tokT[ts_, i, 64 * e:64 * e + 64], True, False, [b_tok[i]], [b_ps[pb]], rg=cc)
                                mm(psum[pb][es_, 128 + cc * 64:128 + cc * 64 + 64], tokT[ts_, i, 384 + 64 * e:384 + 64 * e + 64],
                                   UG[ts_, q, 0:64], False, True, [b_tok[i], b_UG[q]], [b_ps[pb]], rg=cc)
                        for e in range(2):
                            es_ = slice(64 * e, 64 * e + 64)
                            cp("act", TT[es_, 2 * i:2 * i + 2, 64 * e:64 * e + 64], psum[pb][es_, 0:128].rearrange("p (c j) -> p c j", j=64),
                               [b_ps[pb]], [b_TT])
                        cp("dve", HL[:, 2 * i:2 * i + 2, :], psum[pb][:, 128:256].rearrange("p (c j) -> p c j", j=64), [b_ps[pb]], [b_HL])
                    if L1S <= 5:
                        return
                    dma("sp", sl32.rearrange("p (e j) -> p e j", j=64)[0:64], st_in[z, 2 * hp:2 * hp + 2].rearrange("e i j -> i e j"),
                        [], [b_sl], d_sl)
                    pb = bank(0, 8)
                    tr(psum[pb][:, 0:64], sl32[0:64, :], ident32[0:64, 0:64], [b_sl, b_const], [b_ps[pb]])
                    cp("dve", Hs[:, 0, :], psum[pb][:, 0:64], [b_ps[pb]], [b_Hs])
                    cp("pool", Hs16[:, 0, :], Hs[:, 0, :], [b_Hs], [b_H16])
                    order = list(range(16)) if z == 0 else list(range(15, -1, -1))
                    for n, c in enumerate(order):
                        pb = bank(0, 8)
                        mm(psum[pb][:, 0:64], TT[:, c, :], Hs[:, n, :], True, True, [b_TT, b_Hs], [b_ps[pb]])
                        i, cc = c // 2, c % 2
                        stt("dve", Hs[:, n + 1, :], Hs[:, n, :], gam[:, c:c + 1], psum[pb][:, 0:64], ALU.mult, ALU.add, [b_Hs, b_small, b_ps[pb]], [b_Hs])
                        tt("dve", Hs[:, n + 1, :], Hs[:, n + 1, :], HL[:, c, :], ALU.add, [b_Hs, b_HL], [b_Hs])
                        if (n + 1) % 4 == 0:
                            seg = (n + 1) // 4 - 1
                            if z == 1:
                                seg = 3 - seg
                            pb2 = bank(0, 8)
                            tr(psum[pb2][0:64, 0:128], Hs[:, n + 1, :], ident32[:, :], [b_Hs, b_const], [b_ps[pb2]])
                            cp("act", sto[0:64, :], psum[pb2][0:64, 0:128], [b_ps[pb2]], [b_sto])
                            dma("sp", sst_out[z, seg, 2 * hp:2 * hp + 2].rearrange("e i j -> i e j"),
                                sto.rearrange("p (e j) -> p e j", j=64)[0:64], [b_sto], [], d_sto)
                            if n + 1 < 16:
                                ts("dve", Hs[:, n + 1, :], Hs[:, n + 1, :], keep_t[:, 0:1], None, ALU.mult, None, [b_Hs, b_const], [b_Hs])
                        if n + 1 < 16:
                            cp("pool", Hs16[:, n + 1, :], Hs[:, n + 1, :], [b_Hs], [b_H16])
                    if L1S <= 6:
                        return
                    for half in range(2):
                        pb = bank(0, 8)
                        for ii in range(4):
                            i = 4 * half + ii
                            for cc in range(2):
                                c = 2 * i + cc
                                n = order.index(c)
                                ts_ = slice(64 * cc, 64 * cc + 64)
                                col = slice(ii * 128 + cc * 64, ii * 128 + cc * 64 + 64)
                                for e in range(2):
                                    q = 2 * i + e
                                    es_ = slice(64 * e, 64 * e + 64)
                                    mm(psum[pb][es_, col], Hs16[es_, n, :], Qt[es_, i * 128 + cc * 64:i * 128 + cc * 64 + 64], True, False,
                                       [b_H16, b_Qt], [b_ps[pb]], rg=e)
                                    mm(psum[pb][es_, col], tokT[ts_, i, 64 * e:64 * e + 64], GM[ts_, q, cc * 64:cc * 64 + 64], False, False,
                                       [b_tok[i], b_GM[q]], [b_ps[pb]], rg=cc)
                                    mm(psum[pb][es_, col], UG[ts_, q, 0:64], GM[ts_, q, 128 + cc * 64:128 + cc * 64 + 64], False, True,
                                       [b_UG[q], b_GM[q]], [b_ps[pb]], rg=cc)
                        cp("act", y32[:, half * 512:(half + 1) * 512], psum[pb][:, :], [b_ps[pb]], [b_y])
                    for half in range(2):
                        hs = slice(half * 512, (half + 1) * 512)
                        r = sqr[0] % 2
                        sqr[0] += 1
                        cp("pool", sq16[r], y32[:, hs], [b_y], [b_sq[r]])
                        pb = bank(0, 8)
                        mm(psum[pb][:, :], blk16[:, :], sq16[r], True, True, [b_sq[r], b_c], [b_ps[pb]])
                        stt("dve", d32[:, hs], psum[pb][:, :], -1.0 / 64, y32[:, hs], ALU.mult, ALU.add, [b_ps[pb], b_y], [b_d])
                        r = sqr[0] % 2
                        sqr[0] += 1
                        act(sq16[r], d32[:, hs], AF.Square, [b_d], [b_sq[r]])
                        pb = bank(0, 8)
                        mm(psum[pb][:, :], blk16[:, :], sq16[r], True, True, [b_sq[r], b_c], [b_ps[pb]])
                        act(tmp32[:, :], psum[pb][:, :], AF.Sqrt, [b_ps[pb], b_const], [b_tmp], bias=eps_t[:, 1:2], scale=1.0 / 64)
                        recip(tmp32[:, :], tmp32[:, :], [b_tmp], [b_tmp])
                        tt("dve", d32[:, hs], d32[:, hs], tmp32[:, :], ALU.mult, [b_d, b_tmp], [b_d])
                        ts("dve", d32[:, hs], d32[:, hs], hpar[:, 3, hp:hp + 1], hpar[:, 4, hp:hp + 1], ALU.mult, ALU.add, [b_d, b_c], [b_d])
                        if z == 0:
                            tt("pool", acc32[:, hs], d32[:, hs], bon32[:, hs], ALU.add, [b_d, b_bon], [b_acc])
                        else:
                            tt("pool", d32[:, hs], d32[:, hs], bon32[:, hs], ALU.add, [b_d, b_bon], [b_d])
                            tt("pool", acc32[:, hs], acc32[:, hs], d32[:, hs], ALU.add, [b_d, b_acc], [b_acc])
                yc, b_yc = ycs[hp % 2]
                tt("dve", yc[:, :], acc32[:, :], g16[:, :], ALU.mult, [b_acc, b_g], [b_yc])
                store_y(hp, yc, b_yc)
        STAGE = int(os.environ.get("KSTAGE", "9"))
        b_x1 = [S.buf("x1_%d" % i) for i in range(NT)]
        modulation3()
        if STAGE >= 2:
            arg = Arena(12800)
            load_row_bcast(arg, gsc_dr[0], [b_gsc], gate_b, b_gateb, "grow0")
            norm_phase(0, x_in, None, Arena(12800 + 2048))
        if STAGE >= 3:
            l0_phase()
        if STAGE >= 4:
            outproj_phase(w0out, x_in, None, x1_dr, b_x1, False, Arena(0), "a")
        if STAGE >= 5:
            load_row_bcast(Arena(30000), gsc_dr[1], [b_gsc], gate_b, b_gateb, "grow1")
            norm_phase(1, x1_dr, b_x1, Arena(20000))
            if not SKIP_L1:
                l1_phase()
            else:
                ar0 = Arena(0)
                zc, b_zc = ar0.bf16(T)
                memset("dve", zc, 0.0, [b_zc])
                for u in range(KC):
                    store_y(u, zc, b_zc)
            if not os.environ.get("KNOFINAL"):
                outproj_phase(w1out, x1_dr, b_x1, y_out, None, True, Arena(0), "b")
        if DEBUG:
            d_d = S.dsem("ddbg")
            dma("sp", dbg["mod"], modT[:], [b_mod], [], d_d)
            dma("sp", dbg["hT"], hT[:].rearrange("p k t -> p (k t)"), b_hT, [], d_d)
        S.emit()
    return nc, S


_CACHE = {}


def _rope_tables(identity):
    cos = np.ones((64, T), np.float32)
    sin = np.zeros((64, T), np.float32)
    if not identity:
        t = np.arange(T)
        row = (t // 64).astype(np.float32)
        col = (t % 64).astype(np.float32)
        freqs = (10000.0 ** (-np.arange(16, dtype=np.float32) / 16)).astype(np.float32)
        for half, pos in ((0, row), (1, col)):
            ang = pos[None, :] * freqs[:, None]
            c, s_ = np.cos(ang), np.sin(ang)
            base = 32 * half
            cos[base:base + 16] = c
            cos[base + 16:base + 32] = c
            sin[base:base + 16] = -s_
            sin[base + 16:base + 32] = s_
    return np.concatenate([cos, cos], 0), np.concatenate([sin, sin], 0)


def _swap_matrix():
    m = np.zeros((128, 128), np.float32)
    for blk in range(4):
        b0 = 32 * blk
        for d in range(16):
            m[b0 + d + 16, b0 + d] = 1.0
            m[b0 + d, b0 + d + 16] = 1.0
    return m


def _chunk_consts():
    idx = np.arange(128)
    same = (idx[:, None] // 64) == (idx[None, :] // 64)
    tri = np.zeros((2, 128, 384), np.float32)
    msk = np.zeros((2, 128, 256), np.float32)
    for z in range(2):
        if z == 0:
            incl = idx[:, None] <= idx[None, :]
            strict = idx[:, None] < idx[None, :]
            after = idx[:, None] > idx[None, :]
        else:
            incl = idx[:, None] >= idx[None, :]
            strict = idx[:, None] > idx[None, :]
            after = idx[:, None] < idx[None, :]
        tri[z, :, 0:128] = incl & same
        tri[z, :, 128:256] = strict & same
        tri[z, :, 256:384] = after & same
        msk[z, :, 0:128] = strict & same
        msk[z, :, 128:256] = incl & same
    blk = same.astype(np.float32)
    lv = np.zeros((4, 128, 128), np.float32)
    lv[0] = (idx[:, None] // 8) == (idx[None, :] // 8)
    for li, b in enumerate((8, 16, 32)):
        lv[1 + li] = ((idx[:, None] // (2 * b)) == (idx[None, :] // (2 * b))) & ((idx[:, None] // b) != (idx[None, :] // b))
    return tri, msk, blk, lv


def _fm(vec, k):
    return np.ascontiguousarray(np.asarray(vec, np.float32).reshape(k, 128).T)


def kernel(x_prompt, x_sample, cache_l0_a_k, cache_l0_a_v, cache_l0_mla_ckv, cache_l0_mla_kpe,
           state_l1_fwd, state_l1_bwd, c, c_ctx, mod_w, mod_b, norm_g, final_norm_g,
           l0_w_in, l0_w_out, l0_diff_lambda, l0_subln_g, l0_q_norm_g, l0_w_uq, l0_kv_norm_g, l0_w_ukv,
           l1_w_in, l1_w_out, l1_mu, l1_w0, l1_w2, l1_a0, l1_a2, l1_k_k, l1_k_a, l1_r_k, l1_ln_w, l1_ln_b):
    f = lambda a: np.ascontiguousarray(np.asarray(a, dtype=np.float32))
    if "nc" not in _CACHE:
        _CACHE["nc"] = build_program()
    nc, S = _CACHE["nc"]
    L0B, L1B = l0_blocks(), l1_blocks()
    shared = {}
    mw = f(mod_w)
    shared["modw"] = np.ascontiguousarray(mw.reshape(2, KC, 128, 24, 256).transpose(0, 3, 2, 1, 4))
    shared["modb"] = f(mod_b).reshape(2, 1, 6144)
    shared["normg"] = np.ascontiguousarray(f(norm_g).reshape(2, KC, 128).transpose(2, 0, 1))
    shared["fng"] = f(final_norm_g).reshape(1, D)
    shared["w0in"] = pack_blocks(f(l0_w_in), L0B)
    shared["w0out"] = np.ascontiguousarray(f(l0_w_out).reshape(KC, 128, 8, 256).transpose(2, 1, 0, 3))
    wuq = f(l0_w_uq)
    uqb = [("n%d" % h, 192 * h, 128) for h in range(8)] + [("p%d" % h, 192 * h + 128, 64) for h in range(8)]
    shared["wuq"] = pack_blocks(wuq, uqb)
    wukv = f(l0_w_ukv)
    kvb = [("k%d" % h, 256 * h, 128) for h in range(8)] + [("v%d" % h, 256 * h + 128, 128) for h in range(8)]
    shared["wukv"] = pack_blocks(wukv, kvb)
    shared["lam"] = f(l0_diff_lambda).reshape(1, 256)
    shared["subg"] = f(l0_subln_g).reshape(128, 1)
    shared["qng"] = _fm(l0_q_norm_g, 4)
    shared["kvng"] = _fm(l0_kv_norm_g, 2)
    shared["rm"] = _swap_matrix()
    shared["ident"] = np.eye(128, dtype=np.float32)
    shared["w1in"] = pack_blocks(f(l1_w_in), L1B)
    shared["w1out"] = np.ascontiguousarray(f(l1_w_out).reshape(KC, 128, 8, 256).transpose(2, 1, 0, 3))
    mu = f(l1_mu)
    mu_t = np.zeros((128, len(L1B), 2), np.float32)
    for i, (_, c0, n) in enumerate(L1B):
        mu_t[:n, i, :] = mu[:, c0:c0 + n].T
    shared["mu"] = mu_t
    shared["w0row"] = f(l1_w0).reshape(2, 1, 2048)
    shared["w2"] = f(l1_w2)
    shared["a2"] = f(l1_a2)
    shared["a0"] = np.ascontiguousarray(f(l1_a0).reshape(2, 16, 128).transpose(2, 0, 1))
    hp = np.stack([f(l1_k_k), f(l1_k_a), f(l1_r_k), f(l1_ln_w), f(l1_ln_b)], 0)
    shared["hpar"] = np.ascontiguousarray(hp.reshape(5, 16, 128).transpose(2, 0, 1))
    tri, msk, blk, lv = _chunk_consts()
    shared["tri"], shared["msk"], shared["blk64"], shared["lvl"] = tri, msk, blk, lv
    cos_id, sin_id = _rope_tables(True)
    cos_r, sin_r = _rope_tables(False)

    xp, xs = f(x_prompt), f(x_sample)
    ck, cv = f(cache_l0_a_k), f(cache_l0_a_v)
    cckv, ckpe = f(cache_l0_mla_ckv), f(cache_l0_mla_kpe)
    sf, sb_ = f(state_l1_fwd), f(state_l1_bwd)
    cc, cctx = f(c), f(c_ctx)
    in_maps = []
    for core in range(8):
        m = dict(shared)
        prompt = core < 4
        maskk = np.zeros((8, 1536), np.float32)
        maskq = np.zeros((8, T), np.float32)
        if prompt:
            m["x"] = np.ascontiguousarray(xp[4 * core:4 * core + 4].reshape(T, D))
            m["cond"] = _fm(cctx, KC)
            m["cos"], m["sin"] = cos_id, sin_id
            for j in range(4):
                maskk[j, 256 * j:256 * (j + 1)] = 1.0
                maskq[j, :] = NEG
                maskq[j, 256 * j:256 * (j + 1)] = 0.0
            maskk[4, 1024:] = 1.0
            maskq[4, :] = NEG
            m["ck"] = np.zeros((512, 1024), np.float32)
            m["cv"] = np.zeros((512, 1024), np.float32)
            m["cckv"] = np.zeros((512, 256), np.float32)
            m["ckpe"] = np.zeros((512, 64), np.float32)
            m["st"] = np.zeros((2, 32, 64, 64), np.float32)
            m["keep"] = np.tile(np.array([[0.0, 1.0]], np.float32), (128, 1))
        else:
            b = core - 4
            m["x"] = np.ascontiguousarray(xs[b])
            m["cond"] = _fm(cc[b], KC)
            m["cos"], m["sin"] = cos_r, sin_r
            maskk[0, :] = 1.0
            m["ck"] = np.ascontiguousarray(ck[b].reshape(512, 1024))
            m["cv"] = np.ascontiguousarray(cv[b].reshape(512, 1024))
            m["cckv"] = np.ascontiguousarray(cckv[b])
            m["ckpe"] = np.ascontiguousarray(ckpe[b])
            m["st"] = np.ascontiguousarray(np.stack([sf[b], sb_[b]], 0))
            m["keep"] = np.tile(np.array([[1.0, 0.0]], np.float32), (128, 1))
        m["maskk"], m["maskq"] = maskk, maskq
        in_maps.append(m)
    if FAKE:
        for m in in_maps:
            m["modw"] = m["modw"][:, 0:1]
            m["w0in"] = m["w0in"][0:4]
            m["w1in"] = m["w1in"][0:4]
            m["w0out"] = m["w0out"][0:1]
            m["w1out"] = m["w1out"][0:1]
    res = run_bass_kernel_spmd(nc, in_maps, core_ids=list(range(8)))
    R = res.results
    _CACHE["last"] = R
    y_prompt = np.stack([R[cidx]["y"].reshape(4, 256, D) for cidx in range(4)], 0).reshape(16, 256, D)
    y_sample = np.stack([R[4 + b]["y"] for b in range(4)], 0)
    nk = np.concatenate([R[cidx]["nk"].reshape(4, 256, 8, 2, 64) for cidx in range(4)], 0)
    nv = np.concatenate([R[cidx]["nv"].reshape(4, 256, 8, 128) for cidx in range(4)], 0)
    nckv = np.concatenate([R[cidx]["nckv"].reshape(4, 256, 256) for cidx in range(4)], 0)
    nkpe = np.concatenate([R[cidx]["nkpe"].reshape(4, 256, 64) for cidx in range(4)], 0)
    sfo = np.concatenate([R[cidx]["sst"][0] for cidx in range(4)], 0)
    sbo = np.concatenate([R[cidx]["sst"][1] for cidx in range(4)], 0)
    out = (y_prompt, y_sample, nk, nv, nckv, nkpe, sfo, sbo)
    return tuple(np.ascontiguousarray(o.astype(np.float32)) for o in out)
```
